# Optimizing a Trainium2 kernel written in Bass

```python
import jax, jax.numpy as jnp
from jax import lax
import numpy as np

D_MODEL = 4096
BATCH = 2
SEQ = 8192
DEPTH = 1

CHUNK = 64
QUERY_BLOCK = 128
MIX_WIDTH = D_MODEL
FOX_WIDTH = MIX_WIDTH // 2
RWKV_WIDTH = MIX_WIDTH - FOX_WIDTH
FOX_HEAD_DIM = 128
FOX_HEADS = FOX_WIDTH // FOX_HEAD_DIM
RWKV_HEAD_DIM = 64
RWKV_HEADS = RWKV_WIDTH // RWKV_HEAD_DIM
DECAY_LORA = max(32, int(round(1.8 * RWKV_WIDTH ** 0.5 / 32)) * 32)
AAA_LORA = max(32, int(round(1.8 * RWKV_WIDTH ** 0.5 / 32)) * 32)
SHIFT_COLS = 3 * RWKV_WIDTH + DECAY_LORA + AAA_LORA
IN_SIZES = (FOX_WIDTH, FOX_WIDTH, FOX_WIDTH, FOX_HEADS,
            RWKV_WIDTH, RWKV_WIDTH, RWKV_WIDTH, DECAY_LORA, AAA_LORA,
            MIX_WIDTH)
IN_COLS = sum(IN_SIZES)
ALPHA = (2 * DEPTH) ** 0.25
BETA = (8 * DEPTH) ** -0.25
LN_EPS = 1e-5
GN_EPS = 64e-5
FOX_SCALE = FOX_HEAD_DIM ** -0.5

kernel_name = "fox_rwkv7_hybrid_deepnorm_layer"


def _split_points(sizes):
    pts, acc = [], 0
    for s in sizes[:-1]:
        acc += s
        pts.append(acc)
    return pts


def _layernorm(x, gain, bias, eps):
    xf = x.astype(jnp.float32)
    mu = jnp.mean(xf, axis=-1, keepdims=True)
    var = jnp.mean(jnp.square(xf - mu), axis=-1, keepdims=True)
    return (xf - mu) * lax.rsqrt(var + eps) * gain.astype(jnp.float32) + bias.astype(jnp.float32)


def _fox_attention(q, k, v, log_f):
    B, S, H, Dh = q.shape
    qf = jnp.transpose(q, (0, 2, 1, 3)).astype(jnp.float32) * FOX_SCALE
    kf = jnp.transpose(k, (0, 2, 1, 3)).astype(jnp.float32)
    vf = jnp.transpose(v, (0, 2, 1, 3)).astype(jnp.float32)
    c = jnp.transpose(jnp.cumsum(log_f.astype(jnp.float32), axis=1), (0, 2, 1))
    key_pos = jnp.arange(S)

    def block(i):
        start = i * QUERY_BLOCK
        qb = lax.dynamic_slice_in_dim(qf, start, QUERY_BLOCK, axis=2)
        cb = lax.dynamic_slice_in_dim(c, start, QUERY_BLOCK, axis=2)
        q_pos = start + jnp.arange(QUERY_BLOCK)
        logits = jnp.einsum('bhqd,bhkd->bhqk', qb, kf) + cb[..., :, None] - c[..., None, :]
        logits = jnp.where(key_pos[None, :] <= q_pos[:, None], logits, -jnp.inf)
        p = jax.nn.softmax(logits, axis=-1)
        return jnp.einsum('bhqk,bhkd->bhqd', p, vf)

    out = lax.map(block, jnp.arange(S // QUERY_BLOCK))
    return jnp.transpose(out, (1, 0, 3, 2, 4)).reshape(B, S, H * Dh)


def _rwkv7_scan(r, decay, k, v, a_vec, b_vec):
    B, S, H, N = r.shape
    n_chunks = S // CHUNK

    def to_chunks(t):
        return jnp.transpose(t, (1, 0, 2, 3)).reshape(n_chunks, CHUNK, B, H, N)

    xs = tuple(to_chunks(t) for t in (r, decay, k, v, a_vec, b_vec))

    def frame_step(state, inp):
        r_t, w_t, k_t, v_t, a_t, b_t = inp
        sa = jnp.einsum('bhvk,bhk->bhv', state, a_t)
        state = (state * w_t[:, :, None, :] + sa[..., None] * b_t[:, :, None, :]
                 + v_t[..., None] * k_t[:, :, None, :])
        return state, jnp.einsum('bhvk,bhk->bhv', state, r_t)

    def chunk_step(state, chunk_inp):
        return lax.scan(frame_step, state, chunk_inp)

    state0 = jnp.zeros((B, H, N, N), jnp.float32)
    _, y = lax.scan(chunk_step, state0, xs)
    return jnp.transpose(y.reshape(S, B, H, N), (1, 0, 2, 3))


def setup_inputs(seed: int = 0) -> dict:
    key = jax.random.key(seed)
    ks = jax.random.split(key, 20)
    f32 = jnp.float32
    x = jax.random.normal(ks[0], (BATCH, SEQ, D_MODEL), f32)
    col_scale = jnp.concatenate([
        jnp.ones((2 * FOX_WIDTH,), f32), jnp.full((FOX_WIDTH,), BETA, f32), jnp.ones((FOX_HEADS,), f32),
        jnp.ones((2 * RWKV_WIDTH,), f32), jnp.full((RWKV_WIDTH,), BETA, f32),
        jnp.ones((DECAY_LORA + AAA_LORA + MIX_WIDTH,), f32)])
    w_in = jax.random.normal(ks[1], (D_MODEL, IN_COLS), f32) * (D_MODEL ** -0.5) * col_scale
    f_bias = 4.0 + 0.5 * jax.random.normal(ks[2], (FOX_HEADS,), f32)
    mu_shift = jax.random.uniform(ks[3], (SHIFT_COLS,), f32)
    w0 = 0.5 * jax.random.normal(ks[4], (RWKV_WIDTH,), f32)
    w_up = 0.5 * jax.random.normal(ks[5], (DECAY_LORA, RWKV_WIDTH), f32) * DECAY_LORA ** -0.5
    a0 = 0.1 * jax.random.normal(ks[6], (RWKV_WIDTH,), f32)
    a_up = 0.5 * jax.random.normal(ks[7], (AAA_LORA, RWKV_WIDTH), f32) * AAA_LORA ** -0.5
    k_k = 0.85 + 0.1 * jax.random.normal(ks[8], (RWKV_WIDTH,), f32)
    k_a = 1.0 + 0.1 * jax.random.normal(ks[9], (RWKV_WIDTH,), f32)
    r_k = 0.1 * jax.random.normal(ks[10], (RWKV_HEADS, RWKV_HEAD_DIM), f32)
    gn_gain = 1.0 + 0.1 * jax.random.normal(ks[11], (RWKV_WIDTH,), f32)
    gn_bias = 0.01 * jax.random.normal(ks[12], (RWKV_WIDTH,), f32)
    w_out = jax.random.normal(ks[13], (MIX_WIDTH, D_MODEL), f32) * (MIX_WIDTH ** -0.5) * BETA
    ln_gain = 1.0 + 0.1 * jax.random.normal(ks[14], (D_MODEL,), f32)
    ln_bias = 0.01 * jax.random.normal(ks[15], (D_MODEL,), f32)
    return {"x": x, "w_in": w_in, "f_bias": f_bias, "mu_shift": mu_shift, "w0": w0, "w_up": w_up,
            "a0": a0, "a_up": a_up, "k_k": k_k, "k_a": k_a, "r_k": r_k, "gn_gain": gn_gain,
            "gn_bias": gn_bias, "w_out": w_out, "ln_gain": ln_gain, "ln_bias": ln_bias}


def reference(x, w_in, f_bias, mu_shift, w0, w_up, a0, a_up, k_k, k_a, r_k, gn_gain, gn_bias,
              w_out, ln_gain, ln_bias):
    B, S, _ = x.shape
    f32 = jnp.float32
    for _layer in range(DEPTH):
        proj = jnp.einsum('bsd,dc->bsc', x, w_in)
        fox_end = 3 * FOX_WIDTH + FOX_HEADS
        fox_cols = proj[..., :fox_end]
        rw_cols = proj[..., fox_end:fox_end + SHIFT_COLS]
        z = proj[..., fox_end + SHIFT_COLS:]

        fq, fk, fv, ff = jnp.split(fox_cols, _split_points(IN_SIZES[:4]), axis=-1)
        log_f = jax.nn.log_sigmoid(ff.astype(f32) + f_bias.astype(f32))
        fox_out = _fox_attention(fq.reshape(B, S, FOX_HEADS, FOX_HEAD_DIM),
                                 fk.reshape(B, S, FOX_HEADS, FOX_HEAD_DIM),
                                 fv.reshape(B, S, FOX_HEADS, FOX_HEAD_DIM), log_f)

        rw = rw_cols.astype(f32)
        rw_prev = jnp.pad(rw, ((0, 0), (1, 0), (0, 0)))[:, :-1]
        rw = rw + (rw_prev - rw) * mu_shift.astype(f32)
        r, kr, vr, wd, ad = jnp.split(rw, _split_points(IN_SIZES[4:9]), axis=-1)
        w_raw = -jax.nn.softplus(-(w0 + jnp.tanh(wd) @ w_up)) - 0.5
        decay = jnp.exp(-jnp.exp(w_raw))
        a = jax.nn.sigmoid(a0 + ad @ a_up)
        hs = (B, S, RWKV_HEADS, RWKV_HEAD_DIM)
        kk = (kr * k_k).reshape(hs)
        kk = kk / jnp.maximum(jnp.linalg.norm(kk, axis=-1, keepdims=True), 1e-12)
        kr = kr * (1.0 + (a - 1.0) * k_a)
        rh, kh, vh, ah = r.reshape(hs), kr.reshape(hs), vr.reshape(hs), a.reshape(hs)
        y = _rwkv7_scan(rh, decay.reshape(hs), kh, vh, -kk, kk * ah)
        mu = jnp.mean(y, axis=-1, keepdims=True)
        var = jnp.mean(jnp.square(y - mu), axis=-1, keepdims=True)
        y = ((y - mu) * lax.rsqrt(var + GN_EPS)).reshape(B, S, RWKV_WIDTH) * gn_gain + gn_bias
        bonus = jnp.sum(rh * kh * r_k, axis=-1, keepdims=True) * vh
        rwkv_out = y + bonus.reshape(B, S, RWKV_WIDTH)

        h = jnp.concatenate([fox_out, rwkv_out], axis=-1) * jax.nn.silu(z.astype(f32))
        out = jnp.einsum('bsc,cd->bsd', h.astype(x.dtype), w_out)
        x = _layernorm(ALPHA * x.astype(f32) + out.astype(f32), ln_gain, ln_bias, LN_EPS).astype(x.dtype)
    return x
```

```python
import contextlib
import numpy as np
import concourse.bass as bass
import concourse.mybir as mybir
from concourse.bass_utils import run_bass_kernel_spmd

F32 = mybir.dt.float32
BF16 = mybir.dt.bfloat16
AF = mybir.ActivationFunctionType
ALU = mybir.AluOpType
AX = mybir.AxisListType

ENGS = ["sp", "pe", "act", "dve", "pool"]
DECAY_C = float(np.exp(-0.5))


class Prog:
    def __init__(self, nc):
        self.nc = nc
        self.ops = {e: [] for e in ENGS}
        self.cnt = {e: 0 for e in ENGS}
        self.dma_cnt = {}
        self.last_w = {}
        self.readers = {}
        self.waited = {e: {} for e in ENGS}
        self.pending_barrier = {e: [] for e in ENGS}

    def _need(self, engine, tok, kind):
        semkey, val, src = tok
        if semkey.startswith("D:"):
            val = self.dma_cnt[semkey]
        elif src == engine:
            if engine == "pe":
                return None
            if kind != "RAW":
                return None
        if self.waited[engine].get(semkey, -1) >= val:
            return None
        self.waited[engine][semkey] = val
        return (semkey, val)

    def op(self, engine, fn, reads=(), writes=(), dma=None):
        waits = []
        for tok in self.pending_barrier[engine]:
            w = self._need(engine, tok, "RAW")
            if w:
                waits.append(w)
        self.pending_barrier[engine] = []
        for k in reads:
            t = self.last_w.get(k)
            if t is not None:
                w = self._need(engine, t, "RAW")
                if w:
                    waits.append(w)
        for k in writes:
            t = self.last_w.get(k)
            if t is not None:
                w = self._need(engine, t, "WAW")
                if w:
                    waits.append(w)
            for t in self.readers.get(k, ()):
                w = self._need(engine, t, "WAR")
                if w:
                    waits.append(w)
        if dma is not None:
            semkey = "D:" + dma
            self.dma_cnt[semkey] = self.dma_cnt.get(semkey, 0) + 16
            tok = (semkey, self.dma_cnt[semkey], None)
            inc = (semkey, 16)
        else:
            self.cnt[engine] += 1
            semkey = "E:" + engine
            tok = (semkey, self.cnt[engine], engine)
            inc = (semkey, 1)
        for k in writes:
            self.last_w[k] = tok
            self.readers[k] = []
        for k in reads:
            if k not in writes:
                self.readers.setdefault(k, []).append(tok)
        self.ops[engine].append((fn, waits, inc))
        return tok

    def barrier(self):
        toks = []
        for e in ENGS:
            if self.cnt[e] > 0:
                toks.append(("E:" + e, self.cnt[e], "__none__"))
        for semkey, v in self.dma_cnt.items():
            toks.append((semkey, v, None))
        for e in ENGS:
            self.pending_barrier[e] = list(toks)
        self.last_w = {}
        self.readers = {}

    def emit(self):
        nc = self.nc
        self.barrier()
        fin = []
        for tok in self.pending_barrier["sp"]:
            w = self._need("sp", tok, "RAW")
            if w:
                fin.append(w)
        semkeys = ["E:" + e for e in ENGS if self.cnt[e] > 0] + list(self.dma_cnt.keys())
        with contextlib.ExitStack() as st:
            sems = {}
            for i, k in enumerate(semkeys):
                sems[k] = st.enter_context(nc.semaphore("s%d" % i))
            block = st.enter_context(nc.Block())

            def run(engname, eng):
                for fn, waits, inc in self.ops[engname]:
                    for (sk, v) in waits:
                        eng.wait_ge(sems[sk], v)
                    ins = fn(eng)
                    ins.then_inc(sems[inc[0]], inc[1])
                if engname == "sp":
                    for (sk, v) in fin:
                        eng.wait_ge(sems[sk], v)

            @block.sync
            def _(e):
                run("sp", e)

            @block.tensor
            def _(e):
                run("pe", e)

            @block.scalar
            def _(e):
                run("act", e)

            @block.vector
            def _(e):
                run("dve", e)

            @block.gpsimd
            def _(e):
                run("pool", e)


class Cfg:
    def __init__(self, D=4096, S=8192, NF=16, NP=16, dbg=False, phases="ABCDE"):
        self.D, self.S, self.NF, self.NP = D, S, NF, NP
        self.FW = NF * 128
        self.RW = NP * 128
        self.W = self.FW + self.RW
        self.KC = D // 128
        self.NB = S // 512
        self.NKT = S // 128
        self.NCH = S // 64
        self.MISC = 256
        self.secs = [("q", self.FW), ("k", self.FW), ("v", self.FW), ("r", self.RW), ("kr", self.RW),
                     ("vr", self.RW), ("z", self.W), ("misc", self.MISC)]
        self.NC = sum(n for _, n in self.secs)
        self.dbg = dbg
        self.phases = phases
        self.alpha = 2.0 ** 0.25
        self.ln_eps = 1e-5
        self.gn_eps = 64e-5


def host_layout(cfg, inp):
    FW, RW, NF = cfg.FW, cfg.RW, cfg.NF
    w_in = np.asarray(inp["w_in"], np.float32)
    o = 0
    q = w_in[:, o:o + FW]; o += FW
    k = w_in[:, o:o + FW]; o += FW
    v = w_in[:, o:o + FW]; o += FW
    f = w_in[:, o:o + NF]; o += NF
    r = w_in[:, o:o + RW]; o += RW
    kr = w_in[:, o:o + RW]; o += RW
    vr = w_in[:, o:o + RW]; o += RW
    wd = w_in[:, o:o + 96]; o += 96
    ad = w_in[:, o:o + 96]; o += 96
    z = w_in[:, o:o + cfg.W]; o += cfg.W
    assert o == w_in.shape[1]
    misc = np.zeros((cfg.D, cfg.MISC), np.float32)
    misc[:, 0:96] = wd
    misc[:, 96:192] = ad
    misc[:, 192:192 + NF] = f
    w_perm = np.ascontiguousarray(np.concatenate([q, k, v, r, kr, vr, z, misc], axis=1))
    mu = np.asarray(inp["mu_shift"], np.float32)
    def pc(vec):
        return np.ascontiguousarray(np.asarray(vec, np.float32).reshape(cfg.NP, 128).T)
    ptab = np.stack([pc(mu[0:RW]), pc(mu[RW:2 * RW]), pc(mu[2 * RW:3 * RW]), pc(inp["w0"]), pc(inp["a0"]),
                     pc(inp["k_k"]), pc(inp["k_a"]), pc(np.asarray(inp["r_k"]).reshape(-1)),
                     pc(inp["gn_gain"]), pc(inp["gn_bias"])], axis=2)
    mtab = np.zeros((128, 2), np.float32)
    mtab[0:96, 0] = mu[3 * RW:3 * RW + 96]
    mtab[0:96, 1] = mu[3 * RW + 96:3 * RW + 192]
    fb = np.zeros((128, 1), np.float32)
    fb[0:NF, 0] = np.asarray(inp["f_bias"], np.float32)
    return {
        "w_in_p": w_perm,
        "w_out": np.ascontiguousarray(np.asarray(inp["w_out"], np.float32)),
        "ptab": np.ascontiguousarray(ptab),
        "mtab": mtab,
        "fb": fb,
        "w_up": np.ascontiguousarray(np.asarray(inp["w_up"], np.float32)),
        "a_up": np.ascontiguousarray(np.asarray(inp["a_up"], np.float32)),
        "ln_g": np.ascontiguousarray(np.asarray(inp["ln_gain"], np.float32).reshape(1, -1)),
        "ln_b": np.ascontiguousarray(np.asarray(inp["ln_bias"], np.float32).reshape(1, -1)),
    }


def I(method, *args, **kw):
    return lambda e: getattr(e, method)(*args, **kw)


def build(cfg):
    D, S, NF, NP, KC, NB, NKT = cfg.D, cfg.S, cfg.NF, cfg.NP, cfg.KC, cfg.NB, cfg.NKT
    FW, RW, W, NC = cfg.FW, cfg.RW, cfg.W, cfg.NC
    nc = bass.Bass("TRN2", target_bir_lowering=False)

    def din(name, shape, dt=F32):
        return nc.dram_tensor(name, shape, dt, kind="ExternalInput").ap()

    x = din("x", [S, D])
    w_in_p = din("w_in_p", [D, NC])
    w_out = din("w_out", [W, D])
    ptab_d = din("ptab", [128, NP, 10])
    mtab_d = din("mtab", [128, 2])
    fb_d = din("fb", [128, 1])
    w_up_d = din("w_up", [96, RW])
    a_up_d = din("a_up", [96, RW])
    ln_g_d = din("ln_g", [1, D])
    ln_b_d = din("ln_b", [1, D])
    y_out = nc.dram_tensor("y", [S, D], F32, kind="ExternalOutput").ap()

    def scr(name, shape, dt):
        kind = "ExternalOutput" if (cfg.dbg and name in cfg.dbg) else "Internal"
        return nc.dram_tensor(name, shape, dt, kind=kind).ap()

    w_in_bf = scr("w_in_bf", [D, NC], BF16)
    w_out_bf = scr("w_out_bf", [W, D], BF16)
    QT = scr("QT", [FW, S], BF16)
    KT = scr("KT", [FW, S], BF16)
    Vs = scr("Vs", [S, FW], BF16)
    RWs = scr("RWs", [3 * RW, S], F32)
    MISCs = scr("MISCs", [256, S], F32)
    ZT = scr("ZT", [W, S], F32)
    HT = scr("HT", [W, S], BF16)

    P = Prog(nc)
    with contextlib.ExitStack() as st:
        ARENA_N = 51200
        arena = st.enter_context(nc.sbuf_tensor("arena", [128, ARENA_N], F32))
        banks = [st.enter_context(nc.psum_tensor("bank%d" % i, [128, 512], F32)) for i in range(8)]
        apos = [0]
        atop = [ARENA_N]

        def reset():
            apos[0] = 0

        def f32t(n):
            a = apos[0]
            apos[0] += n
            assert apos[0] <= atop[0], (apos[0], atop[0])
            return arena[:, a:a + n]

        def bf16t(n):
            n2 = (n + 1) // 2
            return f32t(n2).bitcast(BF16)[:, 0:n]

        def const_f32(n):
            atop[0] -= n
            return arena[:, atop[0]:atop[0] + n]

        ident_bf = const_f32(64).bitcast(BF16)
        ident_f = const_f32(128)
        ones_bd = const_f32(128)
        ones_bf = const_f32(64).bitcast(BF16)
        m_su = const_f32(128)
        m_sl = const_f32(128)
        m_ui = const_f32(128)
        tri_bf = const_f32(64).bitcast(BF16)
        tmpf = const_f32(128)
        ptab = const_f32(NP * 10).rearrange("p (a b) -> p a b", b=10)
        omu = const_f32(NP * 3).rearrange("p (a b) -> p a b", b=3)
        omka = const_f32(NP)
        mtab = const_f32(2)
        omm = const_f32(2)
        fbc = const_f32(1)
        nfb = const_f32(1)
        eps_gn = const_f32(1)
        eps_ln = const_f32(1)

        P.op("pool", I("memset", ident_f, 0.0), writes=["ident_f"])
        P.op("pool", I("affine_select", out=ident_f, in_=ident_f, pattern=[[-1, 128]], compare_op=ALU.not_equal,
                       fill=1.0, base=0, channel_multiplier=1), reads=["ident_f"], writes=["ident_f"])
        P.op("pool", I("tensor_copy", out=ident_bf, in_=ident_f), reads=["ident_f"], writes=["ident_bf"])
        P.op("pool", I("memset", ones_bf, 1.0), writes=["ones_bf"])

        def tri_mask(t, key, chan_mult, step, cmp, bd=True):
            P.op("pool", I("memset", t, 1.0), writes=[key])
            P.op("pool", I("affine_select", out=t, in_=t, pattern=[[step, 128]], compare_op=cmp, fill=0.0,
                           base=0, channel_multiplier=chan_mult), reads=[key], writes=[key])
            if bd:
                P.op("pool", I("memset", t[0:64, 64:128], 0.0), reads=[key], writes=[key])
                P.op("pool", I("memset", t[64:128, 0:64], 0.0), reads=[key], writes=[key])
        tri_mask(m_su, "m_su", -1, 1, ALU.is_gt)
        tri_mask(m_sl, "m_sl", 1, -1, ALU.is_gt)
        tri_mask(m_ui, "m_ui", -1, 1, ALU.is_ge)
        tri_mask(tmpf, "tmpf", -1, 1, ALU.is_ge, bd=False)
        P.op("pool", I("tensor_copy", out=tri_bf, in_=tmpf), reads=["tmpf"], writes=["tri_bf"])
        P.op("pool", I("memset", ones_bd, 1.0), writes=["ones_bd"])
        P.op("pool", I("memset", ones_bd[0:64, 64:128], 0.0), reads=["ones_bd"], writes=["ones_bd"])
        P.op("pool", I("memset", ones_bd[64:128, 0:64], 0.0), reads=["ones_bd"], writes=["ones_bd"])
        P.op("sp", I("dma_start", out=ptab, in_=ptab_d), writes=["ptab"], dma="c0")
        P.op("sp", I("dma_start", out=mtab, in_=mtab_d), writes=["mtab"], dma="c0")
        P.op("sp", I("dma_start", out=fbc, in_=fb_d), writes=["fbc"], dma="c0")
        P.op("dve", I("tensor_scalar", out=omu, in0=ptab[:, :, 0:3], scalar1=-1.0, scalar2=1.0, op0=ALU.mult, op1=ALU.add),
             reads=["ptab"], writes=["omu"])
        P.op("dve", I("tensor_scalar", out=omka, in0=ptab[:, :, 6], scalar1=-1.0, scalar2=1.0, op0=ALU.mult, op1=ALU.add),
             reads=["ptab"], writes=["omka"])
        P.op("dve", I("tensor_scalar", out=omm, in0=mtab, scalar1=-1.0, scalar2=1.0, op0=ALU.mult, op1=ALU.add),
             reads=["mtab"], writes=["omm"])
        P.op("dve", I("tensor_scalar", out=nfb, in0=fbc, scalar1=-1.0, scalar2=None, op0=ALU.mult),
             reads=["fbc"], writes=["nfb"])
        P.op("dve", I("memset", eps_gn, cfg.gn_eps), writes=["eps_gn"])
        P.op("dve", I("memset", eps_ln, cfg.ln_eps), writes=["eps_ln"])
        P.barrier()

        def castload(dst_bf, src_f32, key, dma):
            n = src_f32.shape[-1]
            for c0 in range(0, n, 2048):
                c1 = min(n, c0 + 2048)
                P.op("pool", I("dma_start", out=dst_bf[:, c0:c1], in_=src_f32[:, c0:c1]), writes=[key], dma=dma)

        def ksplit(n, step=8):
            return [(k0, min(n, k0 + step)) for k0 in range(0, n, step)]

        if "A" in cfg.phases:
            reset()
            wb = [bf16t(max(NC, D)) for _ in range(2)]
            it = 0
            for (src, dst, nrows, ncols) in ((w_in_p, w_in_bf, D, NC), (w_out, w_out_bf, W, D)):
                for rc in range(nrows // 128):
                    s = it % 2
                    it += 1
                    castload(wb[s][:, 0:ncols], src[rc * 128:(rc + 1) * 128, :], "wb%d" % s, "wbld%d" % s)
                    P.op("sp", I("dma_start", out=dst[rc * 128:(rc + 1) * 128, :], in_=wb[s][:, 0:ncols]),
                         reads=["wb%d" % s], dma="wbst%d" % s)
            P.barrier()

        if "B" in cfg.phases:
            reset()
            xb = bf16t(4 * D).rearrange("p (a d) -> p a d", a=4)
            xT = bf16t(KC * 512).rearrange("p (k t) -> p k t", t=512)
            wg = [bf16t(KC * 512).rearrange("p (k c) -> p k c", c=512) for _ in range(2)]
            stf = [f32t(512) for _ in range(4)]
            stb = [bf16t(512) for _ in range(4)]
            groups = []
            o = 0
            for name, n in cfg.secs:
                lo = 0
                while lo < n:
                    gw = min(512, n - lo)
                    groups.append((name, o + lo, gw, lo))
                    lo += gw
                o += n
            w_in_v = w_in_bf.rearrange("(k p) c -> p k c", p=128)
            git = 0
            pit = 0
            sit = 0
            sec_row = {"r": 0, "kr": RW, "vr": 2 * RW}
            for tb in range(NB):
                t0 = tb * 512
                for tt in range(4):
                    castload(xb[:, tt, :], x[t0 + tt * 128:t0 + (tt + 1) * 128, :], "xb", "xbld")
                for kc in range(KC):
                    bk = 4 + (kc % 2)
                    psb = banks[bk][:].bitcast(BF16)
                    for tt in range(4):
                        P.op("pe", I("transpose", out=psb[:, tt * 128:(tt + 1) * 128], in_=xb[:, tt, kc * 128:(kc + 1) * 128],
                                     identity=ident_bf), reads=["xb", "ident_bf"], writes=["ps%d" % bk])
                    if kc % 2 == 0:
                        P.op("act", I("copy", out=xT[:, kc, :], in_=psb[:, 0:512]), reads=["ps%d" % bk], writes=["xT%d" % kc])
                    else:
                        P.op("dve", I("tensor_copy", out=xT[:, kc, :], in_=psb[:, 0:512]), reads=["ps%d" % bk], writes=["xT%d" % kc])
                xTk = ["xT%d" % kc for kc in range(KC)]
                for (name, c0, gw, lo) in groups:
                    s = git % 2
                    git += 1
                    for (k0, k1) in ksplit(KC):
                        P.op("sp", I("dma_start", out=wg[s][:, k0:k1, 0:gw], in_=w_in_v[:, k0:k1, c0:c0 + gw]),
                             writes=["wg%d" % s], dma="wgld%d" % s)
                    if name == "v":
                        for tt in range(4):
                            bk = pit % 4
                            pit += 1
                            for kc in range(KC):
                                P.op("pe", I("matmul", banks[bk][:, 0:gw], lhsT=xT[:, kc, tt * 128:(tt + 1) * 128],
                                             rhs=wg[s][:, kc, 0:gw], start=(kc == 0), stop=(kc == KC - 1)),
                                     reads=[xTk[kc], "wg%d" % s], writes=["ps%d" % bk])
                            ss = sit % 4
                            sit += 1
                            P.op("act", I("copy", out=stb[ss][:, 0:gw], in_=banks[bk][:, 0:gw]),
                                 reads=["ps%d" % bk], writes=["stb%d" % ss])
                            P.op("sp", I("dma_start", out=Vs[t0 + tt * 128:t0 + (tt + 1) * 128, lo:lo + gw], in_=stb[ss][:, 0:gw]),
                                 reads=["stb%d" % ss], dma="stbst%d" % ss)
                        continue
                    for cc in range((gw + 127) // 128):
                        cw = min(128, gw - cc * 128)
                        bk = pit % 4
                        pit += 1
                        for kc in range(KC):
                            P.op("pe", I("matmul", banks[bk][0:cw, :], lhsT=wg[s][:, kc, cc * 128:cc * 128 + cw], rhs=xT[:, kc, :],
                                         start=(kc == 0), stop=(kc == KC - 1)),
                                 reads=[xTk[kc], "wg%d" % s], writes=["ps%d" % bk])
                        ss = sit % 4
                        sit += 1
                        row = lo + cc * 128
                        if name in ("q", "k"):
                            sc = (128.0 ** -0.5) if name == "q" else 1.0
                            dst = QT if name == "q" else KT
                            P.op("act", I("activation", out=stb[ss], in_=banks[bk][:], func=AF.Copy, scale=sc),
                                 reads=["ps%d" % bk], writes=["stb%d" % ss])
                            P.op("sp", I("dma_start", out=dst[row:row + 128, t0:t0 + 512], in_=stb[ss]),
                                 reads=["stb%d" % ss], dma="stbst%d" % ss)
                        elif name == "z":
                            P.op("act", I("activation", out=stf[ss], in_=banks[bk][:], func=AF.Silu),
                                 reads=["ps%d" % bk], writes=["stf%d" % ss])
                            P.op("sp", I("dma_start", out=ZT[row:row + 128, t0:t0 + 512], in_=stf[ss]),
                                 reads=["stf%d" % ss], dma="stfst%d" % ss)
                        else:
                            if name == "misc":
                                dst, r0 = MISCs, row
                            else:
                                dst, r0 = RWs, sec_row[name] + row
                            P.op("dve", I("tensor_copy", out=stf[ss][0:cw, :], in_=banks[bk][0:cw, :]),
                                 reads=["ps%d" % bk], writes=["stf%d" % ss])
                            P.op("sp", I("dma_start", out=dst[r0:r0 + cw, t0:t0 + 512], in_=stf[ss][0:cw, :]),
                                 reads=["stf%d" % ss], dma="stfst%d" % ss)
            P.barrier()

        if "C" in cfg.phases:
            reset()
            NQB = NB
            FFt = f32t(S)
            CL = f32t(S)
            onesF = f32t(S)
            CLR = f32t(NQB)
            SEL = f32t(NF * 128).rearrange("p (h m) -> p h m", m=128)
            CT = f32t(NKT * NF).rearrange("p (k h) -> p k h", h=NF)
            CREF = f32t(NF * NQB).rearrange("p (h q) -> p h q", q=NQB)
            KTh = [bf16t(S) for _ in range(2)]
            Vh = [bf16t(S).rearrange("p (k d) -> p k d", d=128) for _ in range(2)]
            Qb = [bf16t(512) for _ in range(2)]
            Zb = [f32t(512) for _ in range(2)]
            Pt = [bf16t(512) for _ in range(3)]
            Bt = [f32t(NKT) for _ in range(2)]
            Rr = f32t(512)
            Ot = f32t(512)
            Hb = [bf16t(512) for _ in range(2)]
            P.op("sp", I("dma_start", out=FFt[0:NF, :], in_=MISCs[192:192 + NF, :]), writes=["FFt"], dma="c1")
            P.op("pool", I("memset", onesF[0:NF, :], 1.0), writes=["onesF"])
            P.op("act", I("activation", out=FFt[0:NF, :], in_=FFt[0:NF, :], func=AF.Exp, bias=nfb[0:NF, :], scale=-1.0),
                 reads=["FFt", "nfb"], writes=["FFt"])
            P.op("act", I("activation", out=FFt[0:NF, :], in_=FFt[0:NF, :], func=AF.Ln, bias=1.0, scale=1.0),
                 reads=["FFt"], writes=["FFt"])
            P.op("dve", I("tensor_tensor_scan", out=CL[0:NF, :], data0=onesF[0:NF, :], data1=FFt[0:NF, :], initial=0.0,
                          op0=ALU.mult, op1=ALU.add), reads=["FFt", "onesF"], writes=["CL"])
            CLv = CL.rearrange("p (q t) -> p q t", t=512)
            P.op("dve", I("tensor_copy", out=CLR[0:NF, :], in_=CLv[0:NF, :, 256]), reads=["CL"], writes=["CLR"])
            P.op("pool", I("memset", SEL[0:NF], 0.0), writes=["SEL"])
            P.op("pool", I("affine_select", out=SEL[0:NF], in_=SEL[0:NF], pattern=[[-1, NF], [0, 128]],
                           compare_op=ALU.not_equal, fill=1.0, base=0, channel_multiplier=1), reads=["SEL"], writes=["SEL"])
            per = 512 // NF
            for k0 in range(0, NKT, per):
                k1 = min(NKT, k0 + per)
                bk = (k0 // per) % 2
                for kt in range(k0, k1):
                    P.op("pe", I("matmul", banks[bk][:, (kt - k0) * NF:(kt - k0 + 1) * NF], lhsT=CL[0:NF, kt * 128:(kt + 1) * 128],
                                 rhs=ident_f[0:NF, 0:NF], start=True, stop=True),
                         reads=["CL", "ident_f"], writes=["ps%d" % bk])
                P.op("dve", I("tensor_copy", out=CT[:, k0:k1, :],
                              in_=banks[bk][:, 0:(k1 - k0) * NF].rearrange("p (k h) -> p k h", h=NF)),
                     reads=["ps%d" % bk], writes=["CT"])
            for h in range(NF):
                bk = 2 + h % 2
                P.op("pe", I("matmul", banks[bk][:, 0:NQB], lhsT=SEL[0:NF, h, :], rhs=CLR[0:NF, :], start=True, stop=True),
                     reads=["SEL", "CLR"], writes=["ps%d" % bk])
                P.op("dve", I("tensor_copy", out=CREF[:, h, :], in_=banks[bk][:, 0:NQB]), reads=["ps%d" % bk], writes=["CREF"])
            Vs_v = Vs.rearrange("(k p) c -> p k c", p=128)
            sidx = 0
            for h in range(NF):
                hs = h % 2
                P.op("sp", I("dma_start", out=KTh[hs], in_=KT[h * 128:(h + 1) * 128, :]), writes=["KTh%d" % hs], dma="kth%d" % hs)
                for (k0, k1) in ksplit(NKT):
                    P.op("sp", I("dma_start", out=Vh[hs][:, k0:k1, :], in_=Vs_v[:, k0:k1, h * 128:(h + 1) * 128]),
                         writes=["Vh%d" % hs], dma="vh%d" % hs)
                for qb in range(NQB):
                    qs = qb % 2
                    q0 = qb * 512
                    P.op("sp", I("dma_start", out=Qb[qs], in_=QT[h * 128:(h + 1) * 128, q0:q0 + 512]), writes=["Qb%d" % qs], dma="qb%d" % qs)
                    P.op("sp", I("dma_start", out=Zb[qs], in_=ZT[h * 128:(h + 1) * 128, q0:q0 + 512]), writes=["Zb%d" % qs], dma="zb%d" % qs)
                    nkt = 4 * qb + 4
                    P.op("dve", I("tensor_scalar", out=Bt[qs][:, 0:nkt], in0=CT[:, 0:nkt, h], scalar1=CREF[:, h, qb:qb + 1],
                                  scalar2=None, op0=ALU.subtract), reads=["CT", "CREF"], writes=["Bt%d" % qs])
                    pso = 2 + qs
                    psr = 4 + qs
                    for kt in range(nkt):
                        j = kt - 4 * qb
                        cs = 0 if j < 0 else j * 128
                        bs = sidx % 2
                        ps_ = sidx % 3
                        sidx += 1
                        P.op("pe", I("matmul", banks[bs][:, cs:512], lhsT=KTh[hs][:, kt * 128:(kt + 1) * 128], rhs=Qb[qs][:, cs:512],
                                     start=True, stop=True), reads=["KTh%d" % hs, "Qb%d" % qs], writes=["ps%d" % bs])
                        P.op("act", I("activation", out=Pt[ps_][:, cs:512], in_=banks[bs][:, cs:512], func=AF.Exp,
                                      bias=Bt[qs][:, kt:kt + 1], scale=1.0),
                             reads=["ps%d" % bs, "Bt%d" % qs], writes=["Pt%d" % ps_])
                        if j >= 0:
                            P.op("pool", I("tensor_tensor", out=Pt[ps_][:, cs:cs + 128], in0=Pt[ps_][:, cs:cs + 128], in1=tri_bf,
                                           op=ALU.mult), reads=["Pt%d" % ps_, "tri_bf"], writes=["Pt%d" % ps_])
                        P.op("pe", I("matmul", banks[pso][:, cs:512], lhsT=Vh[hs][:, kt, :], rhs=Pt[ps_][:, cs:512],
                                     start=(kt == 0), stop=(kt == nkt - 1)),
                             reads=["Vh%d" % hs, "Pt%d" % ps_], writes=["ps%d" % pso])
                        P.op("pe", I("matmul", banks[psr][:, cs:512], lhsT=ones_bf, rhs=Pt[ps_][:, cs:512],
                                     start=(kt == 0), stop=(kt == nkt - 1)),
                             reads=["ones_bf", "Pt%d" % ps_], writes=["ps%d" % psr])
                    P.op("dve", I("reciprocal", out=Rr, in_=banks[psr][:]), reads=["ps%d" % psr], writes=["Rr"])
                    P.op("dve", I("tensor_tensor", out=Ot, in0=banks[pso][:], in1=Rr, op=ALU.mult),
                         reads=["ps%d" % pso, "Rr"], writes=["Ot"])
                    P.op("pool", I("tensor_tensor", out=Hb[qs], in0=Ot, in1=Zb[qs], op=ALU.mult),
                         reads=["Ot", "Zb%d" % qs], writes=["Hb%d" % qs])
                    P.op("sp", I("dma_start", out=HT[h * 128:(h + 1) * 128, q0:q0 + 512], in_=Hb[qs]),
                         reads=["Hb%d" % qs], dma="hb%d" % qs)
            P.barrier()

        if "D" in cfg.phases:
            reset()
            raw = {n: f32t(513) for n in ("r", "k", "v", "wd", "ad")}
            X = {n: f32t(512) for n in ("r", "k", "v", "wd", "ad")}
            tmp = {n: f32t(512) for n in ("r", "k", "v", "wd", "ad")}
            wup = f32t(128)
            aup = f32t(128)
            sg, av, kk, kk2, rn, kmod, t1, bvec, Ls, Ep, Em, Eex, Eend, bon, Zr, smask, yT, t2, t3 = [f32t(512) for _ in range(19)]
            WC = f32t(8)
            hTb = bf16t(512)
            bdn = ("a", "r", "b", "k", "Kh", "Bh", "v")
            bd = [{n: bf16t(8 * 128).rearrange("p (c t) -> p c t", t=128) for n in bdn} for _ in range(2)]
            S0 = f32t(64)
            S0b = bf16t(64)
            Pk = [bf16t(128) for _ in range(2)]
            PkT = [bf16t(128) for _ in range(2)]
            AccT = [f32t(128) for _ in range(2)]
            AccTb = [bf16t(128) for _ in range(2)]
            LakT, MrbT, MrkT, Kh_t, Bh_t = [bf16t(128) for _ in range(5)]
            V_t, Xc, Uc = [bf16t(64) for _ in range(3)]
            Yc = f32t(64)
            ysq = f32t(64)
            stat = f32t(8)
            Ybd = f32t(8 * 128).rearrange("p (c t) -> p c t", t=128)
            P.op("pool", I("memset", smask, 1.0), writes=["smask"])
            P.op("pool", I("memset", smask.rearrange("p (c t) -> p c t", t=64)[:, :, 0:1], 0.0), reads=["smask"], writes=["smask"])
            for b_ in range(2):
                for n in bdn:
                    P.op("pool", I("memset", bd[b_][n], 0.0), writes=["bd%d%s" % (b_, n)])
            P.op("pool", I("memset", Ybd, 0.0), writes=["Ybd"])

            def v3(t, p0=0, p1=128):
                return t[p0:p1, :].rearrange("p (c t) -> p c t", t=64)

            bidx = 0
            for pr in range(NP):
                f0 = pr * 128
                PC = [ptab[:, pr, j:j + 1] for j in range(10)]
                P.op("sp", I("dma_start", out=wup[0:96, :], in_=w_up_d[:, f0:f0 + 128]), writes=["wup"], dma="wup")
                P.op("sp", I("dma_start", out=aup[0:96, :], in_=a_up_d[:, f0:f0 + 128]), writes=["aup"], dma="wup")
                P.op("pool", I("memset", S0, 0.0), writes=["S0"])
                P.op("pool", I("memset", S0b, 0.0), writes=["S0b"])
                for tb in range(NB):
                    t0 = tb * 512
                    B = bd[bidx % 2]
                    bk_ = "bd%d" % (bidx % 2)
                    bidx += 1
                    srcs = [("r", RWs, f0, 128), ("k", RWs, RW + f0, 128), ("v", RWs, 2 * RW + f0, 128),
                            ("wd", MISCs, 0, 96), ("ad", MISCs, 96, 96)]
                    for n, src, row, nr in srcs:
                        if tb == 0:
                            P.op("pool", I("memset", raw[n][0:nr, 0:1], 0.0), writes=["raw" + n])
                            P.op("sp", I("dma_start", out=raw[n][0:nr, 1:513], in_=src[row:row + nr, 0:512]),
                                 writes=["raw" + n], dma="raw" + n)
                        else:
                            P.op("sp", I("dma_start", out=raw[n][0:nr, :], in_=src[row:row + nr, t0 - 1:t0 + 512]),
                                 writes=["raw" + n], dma="raw" + n)
                    P.op("sp", I("dma_start", out=Zr, in_=ZT[FW + f0:FW + f0 + 128, t0:t0 + 512]), writes=["Zr"], dma="zr")
                    lerp = [("r", "dve", PC[0], omu[:, pr, 0:1], 128), ("k", "dve", PC[1], omu[:, pr, 1:2], 128),
                            ("v", "dve", PC[2], omu[:, pr, 2:3], 128), ("wd", "dve", mtab[:, 0:1], omm[:, 0:1], 96),
                            ("ad", "dve", mtab[:, 1:2], omm[:, 1:2], 96)]
                    for n, eng, mu_c, omu_c, nr in lerp:
                        P.op(eng, I("tensor_scalar", out=tmp[n][0:nr, :], in0=raw[n][0:nr, 0:512], scalar1=mu_c[0:nr, :], scalar2=None,
                                    op0=ALU.mult), reads=["raw" + n, "ptab", "mtab"], writes=["tmp" + n])
                        P.op(eng, I("scalar_tensor_tensor", out=X[n][0:nr, :], in0=raw[n][0:nr, 1:513], scalar=omu_c[0:nr, :],
                                    in1=tmp[n][0:nr, :], op0=ALU.mult, op1=ALU.add),
                             reads=["raw" + n, "omu", "omm", "tmp" + n], writes=["X" + n])
                    P.op("act", I("activation", out=X["wd"][0:96, :], in_=X["wd"][0:96, :], func=AF.Tanh), reads=["Xwd"], writes=["Xwd"])
                    P.op("pe", I("matmul", banks[0][:], lhsT=wup[0:96, :], rhs=X["wd"][0:96, :], start=True, stop=True),
                         reads=["wup", "Xwd"], writes=["ps0"])
                    P.op("act", I("activation", out=sg, in_=banks[0][:], func=AF.Sigmoid, bias=PC[3], scale=1.0),
                         reads=["ps0", "ptab"], writes=["sg"])
                    P.op("pe", I("matmul", banks[1][:], lhsT=aup[0:96, :], rhs=X["ad"][0:96, :], start=True, stop=True),
                         reads=["aup", "Xad"], writes=["ps1"])
                    P.op("act", I("activation", out=av, in_=banks[1][:], func=AF.Sigmoid, bias=PC[4], scale=1.0),
                         reads=["ps1", "ptab"], writes=["av"])
                    P.op("pool", I("tensor_scalar", out=kk, in0=X["k"], scalar1=PC[5], scalar2=None, op0=ALU.mult),
                         reads=["Xk", "ptab"], writes=["kk"])
                    P.op("pool", I("tensor_tensor", out=kk2, in0=kk, in1=kk, op=ALU.mult), reads=["kk"], writes=["kk2"])
                    P.op("pe", I("matmul", banks[2][:], lhsT=ones_bd, rhs=kk2, start=True, stop=True),
                         reads=["ones_bd", "kk2"], writes=["ps2"])
                    P.op("dve", I("tensor_scalar", out=rn, in0=banks[2][:], scalar1=1e-24, scalar2=None, op0=ALU.max),
                         reads=["ps2"], writes=["rn"])
                    P.op("act", I("activation", out=rn, in_=rn, func=AF.Sqrt), reads=["rn"], writes=["rn"])
                    P.op("dve", I("reciprocal", out=rn, in_=rn), reads=["rn"], writes=["rn"])
                    P.op("dve", I("tensor_tensor", out=kk, in0=kk, in1=rn, op=ALU.mult), reads=["kk", "rn"], writes=["kk"])
                    P.op("dve", I("tensor_scalar", out=t1, in0=av, scalar1=PC[6], scalar2=omka[:, pr:pr + 1], op0=ALU.mult, op1=ALU.add),
                         reads=["av", "ptab", "omka"], writes=["t1"])
                    P.op("dve", I("tensor_tensor", out=kmod, in0=X["k"], in1=t1, op=ALU.mult), reads=["Xk", "t1"], writes=["kmod"])
                    P.op("pool", I("tensor_tensor", out=bvec, in0=kk, in1=av, op=ALU.mult), reads=["kk", "av"], writes=["bvec"])
                    P.op("dve", I("scalar_tensor_tensor", out=t2, in0=X["r"], scalar=PC[7], in1=kmod, op0=ALU.mult, op1=ALU.mult),
                         reads=["Xr", "ptab", "kmod"], writes=["t2"])
                    P.op("pe", I("matmul", banks[3][:], lhsT=ones_bd, rhs=t2, start=True, stop=True),
                         reads=["ones_bd", "t2"], writes=["ps3"])
                    P.op("dve", I("tensor_tensor", out=bon, in0=banks[3][:], in1=X["v"], op=ALU.mult), reads=["ps3", "Xv"], writes=["bon"])
                    P.op("dve", I("tensor_tensor_scan", out=Ls, data0=smask, data1=sg, initial=0.0, op0=ALU.mult, op1=ALU.add),
                         reads=["smask", "sg"], writes=["Ls"])
                    P.op("act", I("activation", out=Ep, in_=Ls, func=AF.Exp, scale=-DECAY_C), reads=["Ls"], writes=["Ep"])
                    P.op("act", I("activation", out=Em, in_=Ls, func=AF.Exp, scale=DECAY_C), reads=["Ls"], writes=["Em"])
                    P.op("pool", I("tensor_tensor", out=t3, in0=Ls, in1=sg, op=ALU.subtract), reads=["Ls", "sg"], writes=["t3"])
                    P.op("act", I("activation", out=Eex, in_=t3, func=AF.Exp, scale=-DECAY_C), reads=["t3"], writes=["Eex"])
                    Lsv = v3(Ls)
                    P.op("pool", I("tensor_tensor", out=v3(t3), in0=Lsv[:, :, 63:64].to_broadcast([128, 8, 64]), in1=Lsv, op=ALU.subtract),
                         reads=["Ls", "Eex"], writes=["t3"])
                    P.op("act", I("activation", out=Eend, in_=t3, func=AF.Exp, scale=-DECAY_C), reads=["t3"], writes=["Eend"])
                    P.op("act", I("activation", out=WC, in_=Lsv[:, :, 63], func=AF.Exp, scale=-DECAY_C), reads=["Ls"], writes=["WC"])
                    def bdw(eng, name, in0, in1, rk, neg=False):
                        for hh in range(2):
                            p0, p1 = hh * 64, hh * 64 + 64
                            o_ = B[name][p0:p1, :, p0:p1]
                            if in1 is None:
                                fn = I("tensor_copy", out=o_, in_=v3(in0, p0, p1))
                            elif neg:
                                fn = I("scalar_tensor_tensor", out=o_, in0=v3(in0, p0, p1), scalar=-1.0, in1=v3(in1, p0, p1),
                                       op0=ALU.mult, op1=ALU.mult)
                            else:
                                fn = I("tensor_tensor", out=o_, in0=v3(in0, p0, p1), in1=v3(in1, p0, p1), op=ALU.mult)
                            P.op(eng, fn, reads=rk, writes=[bk_ + name])
                    bdw("dve", "a", kk, Eex, ["kk", "Eex"], neg=True)
                    bdw("pool", "r", X["r"], Ep, ["Xr", "Ep"])
                    bdw("dve", "b", bvec, Em, ["bvec", "Em"])
                    bdw("pool", "k", kmod, Em, ["kmod", "Em"])
                    bdw("dve", "Kh", kmod, Eend, ["kmod", "Eend"])
                    bdw("pool", "Bh", bvec, Eend, ["bvec", "Eend"])
                    bdw("dve", "v", X["v"], None, ["Xv"])
                    ka, kr_, kb, kk_, kKh, kBh, kv = [bk_ + n for n in bdn]
                    for c in range(8):
                        aT, rT, bT, kT = B["a"][:, c, :], B["r"][:, c, :], B["b"][:, c, :], B["k"][:, c, :]
                        p5 = banks[5][:].bitcast(BF16)
                        P.op("pe", I("transpose", out=p5[:, 0:128], in_=B["Kh"][:, c, :], identity=ident_bf),
                             reads=[kKh, "ident_bf"], writes=["ps5"])
                        P.op("pe", I("transpose", out=p5[:, 128:256], in_=B["Bh"][:, c, :], identity=ident_bf),
                             reads=[kBh, "ident_bf"], writes=["ps5"])
                        P.op("pe", I("transpose", out=p5[:, 256:384], in_=B["v"][:, c, :], identity=ident_bf),
                             reads=[kv, "ident_bf"], writes=["ps5"])
                        P.op("act", I("copy", out=Kh_t, in_=p5[:, 0:128]), reads=["ps5"], writes=["Kh_t"])
                        P.op("act", I("copy", out=Bh_t, in_=p5[:, 128:256]), reads=["ps5"], writes=["Bh_t"])
                        P.op("act", I("copy", out=V_t[0:64, :], in_=p5[0:64, 256:320]), reads=["ps5"], writes=["V_t"])
                        P.op("act", I("copy", out=V_t[64:128, :], in_=p5[64:128, 320:384]), reads=["ps5"], writes=["V_t"])
                        P.op("pe", I("matmul", banks[0][:, 0:128], lhsT=aT, rhs=bT, start=True, stop=True), reads=[ka, kb], writes=["ps0"])
                        P.op("pe", I("matmul", banks[0][:, 128:256], lhsT=bT, rhs=aT, start=True, stop=True), reads=[ka, kb], writes=["ps0"])
                        P.op("pe", I("matmul", banks[1][:, 0:128], lhsT=kT, rhs=aT, start=True, stop=True), reads=[ka, kk_], writes=["ps1"])
                        P.op("pe", I("matmul", banks[1][:, 128:256], lhsT=bT, rhs=rT, start=True, stop=True), reads=[kr_, kb], writes=["ps1"])
                        P.op("pe", I("matmul", banks[1][:, 256:384], lhsT=kT, rhs=rT, start=True, stop=True), reads=[kr_, kk_], writes=["ps1"])
                        P.op("dve", I("tensor_tensor", out=Pk[0], in0=banks[0][:, 0:128], in1=m_sl, op=ALU.mult), reads=["ps0", "m_sl"], writes=["Pk0"])
                        P.op("dve", I("tensor_tensor", out=PkT[0], in0=banks[0][:, 128:256], in1=m_su, op=ALU.mult), reads=["ps0", "m_su"], writes=["PkT0"])
                        P.op("dve", I("tensor_tensor", out=LakT, in0=banks[1][:, 0:128], in1=m_su, op=ALU.mult), reads=["ps1", "m_su"], writes=["LakT"])
                        P.op("dve", I("tensor_tensor", out=MrbT, in0=banks[1][:, 128:256], in1=m_ui, op=ALU.mult), reads=["ps1", "m_ui"], writes=["MrbT"])
                        P.op("dve", I("tensor_tensor", out=MrkT, in0=banks[1][:, 256:384], in1=m_ui, op=ALU.mult), reads=["ps1", "m_ui"], writes=["MrkT"])
                        P.op("pool", I("tensor_tensor", out=AccT[0], in0=PkT[0], in1=ident_f, op=ALU.add), reads=["PkT0", "ident_f"], writes=["AccT0"])
                        P.op("pool", I("tensor_copy", out=AccTb[0], in_=AccT[0]), reads=["AccT0"], writes=["AccTb0"])
                        for l in range(5):
                            s_, d_ = l % 2, (l + 1) % 2
                            P.op("pe", I("matmul", banks[2][:, 0:128], lhsT=PkT[s_], rhs=Pk[s_], start=True, stop=True),
                                 reads=["Pk%d" % s_, "PkT%d" % s_], writes=["ps2"])
                            if l < 4:
                                P.op("pe", I("matmul", banks[2][:, 128:256], lhsT=Pk[s_], rhs=PkT[s_], start=True, stop=True),
                                     reads=["Pk%d" % s_, "PkT%d" % s_], writes=["ps2"])
                            P.op("act", I("copy", out=Pk[d_], in_=banks[2][:, 0:128]), reads=["ps2"], writes=["Pk%d" % d_])
                            if l < 4:
                                P.op("act", I("copy", out=PkT[d_], in_=banks[2][:, 128:256]), reads=["ps2"], writes=["PkT%d" % d_])
                            P.op("pe", I("matmul", banks[3][:, 0:128], lhsT=Pk[d_], rhs=AccTb[s_], start=True, stop=True),
                                 reads=["Pk%d" % d_, "AccTb%d" % s_], writes=["ps3"])
                            P.op("dve", I("tensor_tensor", out=AccT[d_], in0=banks[3][:, 0:128], in1=AccT[s_], op=ALU.add),
                                 reads=["ps3", "AccT%d" % s_], writes=["AccT%d" % d_])
                            P.op("pool", I("tensor_copy", out=AccTb[d_], in_=AccT[d_]), reads=["AccT%d" % d_], writes=["AccTb%d" % d_])
                        TT = AccTb[1]
                        P.op("pe", I("matmul", banks[6][:, 0:64], lhsT=aT, rhs=S0b, start=True, stop=False), reads=[ka, "S0b"], writes=["ps6"])
                        P.op("pe", I("matmul", banks[6][:, 0:64], lhsT=LakT, rhs=V_t, start=False, stop=True), reads=["LakT", "V_t"], writes=["ps6"])
                        P.op("act", I("copy", out=Xc, in_=banks[6][:, 0:64]), reads=["ps6"], writes=["Xc"])
                        P.op("pe", I("matmul", banks[6][:, 64:128], lhsT=TT, rhs=Xc, start=True, stop=True), reads=["AccTb1", "Xc"], writes=["ps6"])
                        P.op("act", I("copy", out=Uc, in_=banks[6][:, 64:128]), reads=["ps6"], writes=["Uc"])
                        P.op("pe", I("matmul", banks[7][:, 0:64], lhsT=rT, rhs=S0b, start=True, stop=False), reads=[kr_, "S0b"], writes=["ps7"])
                        P.op("pe", I("matmul", banks[7][:, 0:64], lhsT=MrbT, rhs=Uc, start=False, stop=False), reads=["MrbT", "Uc"], writes=["ps7"])
                        P.op("pe", I("matmul", banks[7][:, 0:64], lhsT=MrkT, rhs=V_t, start=False, stop=True), reads=["MrkT", "V_t"], writes=["ps7"])
                        P.op("pe", I("matmul", banks[4][:, 0:64], lhsT=Bh_t, rhs=Uc, start=True, stop=False), reads=["Bh_t", "Uc"], writes=["ps4"])
                        P.op("pe", I("matmul", banks[4][:, 0:64], lhsT=Kh_t, rhs=V_t, start=False, stop=True), reads=["Kh_t", "V_t"], writes=["ps4"])
                        P.op("dve", I("scalar_tensor_tensor", out=S0, in0=S0, scalar=WC[:, c:c + 1], in1=banks[4][:, 0:64],
                                      op0=ALU.mult, op1=ALU.add), reads=["S0", "WC", "ps4"], writes=["S0"])
                        P.op("pool", I("tensor_copy", out=S0b, in_=S0), reads=["S0"], writes=["S0b"])
                        P.op("act", I("activation", out=Yc, in_=banks[7][:, 0:64], func=AF.Copy, accum_out=stat[:, 0:1]),
                             reads=["ps7"], writes=["Yc", "stat0"])
                        P.op("dve", I("tensor_scalar", out=stat[:, 1:2], in0=stat[:, 0:1], scalar1=-1.0 / 64.0, scalar2=None, op0=ALU.mult),
                             reads=["stat0"], writes=["stat1"])
                        P.op("act", I("activation", out=ysq, in_=Yc, func=AF.Square, bias=stat[:, 1:2], scale=1.0, accum_out=stat[:, 2:3]),
                             reads=["Yc", "stat1"], writes=["ysq", "stat2"])
                        P.op("act", I("activation", out=stat[:, 3:4], in_=stat[:, 2:3], func=AF.Sqrt, bias=eps_gn[:, 0:1], scale=1.0 / 64.0),
                             reads=["stat2", "eps_gn"], writes=["stat3"])
                        P.op("dve", I("reciprocal", out=stat[:, 4:5], in_=stat[:, 3:4]), reads=["stat3"], writes=["stat4"])
                        for hh in range(2):
                            p0, p1 = hh * 64, hh * 64 + 64
                            P.op("dve", I("tensor_scalar", out=Ybd[p0:p1, c, p0:p1], in0=Yc[p0:p1, :], scalar1=stat[p0:p1, 1:2],
                                          scalar2=stat[p0:p1, 4:5], op0=ALU.add, op1=ALU.mult),
                                 reads=["Yc", "stat1", "stat4"], writes=["Ybd%d" % c])
                    for c in range(8):
                        bkk = 5 + (c % 2)
                        P.op("pe", I("transpose", out=banks[bkk][:, 0:128], in_=Ybd[:, c, :], identity=ident_f),
                             reads=["Ybd%d" % c, "ident_f"], writes=["ps%d" % bkk])
                        for hh in range(2):
                            p0, p1 = hh * 64, hh * 64 + 64
                            P.op("act", I("copy", out=yT[p0:p1, c * 64:(c + 1) * 64], in_=banks[bkk][p0:p1, p0:p1]),
                                 reads=["ps%d" % bkk], writes=["yT"])
                    P.op("dve", I("tensor_scalar", out=yT, in0=yT, scalar1=PC[8], scalar2=PC[9], op0=ALU.mult, op1=ALU.add),
                         reads=["yT", "ptab"], writes=["yT"])
                    P.op("dve", I("tensor_tensor", out=yT, in0=yT, in1=bon, op=ALU.add), reads=["yT", "bon"], writes=["yT"])
                    P.op("pool", I("tensor_tensor", out=hTb, in0=yT, in1=Zr, op=ALU.mult), reads=["yT", "Zr"], writes=["hTb"])
                    P.op("sp", I("dma_start", out=HT[FW + f0:FW + f0 + 128, t0:t0 + 512], in_=hTb), reads=["hTb"], dma="hTb")
            P.barrier()

        if "E" in cfg.phases:
            reset()
            TB = 256
            WKC = W // 128
            hT = bf16t(WKC * TB).rearrange("p (k t) -> p k t", t=TB)
            wo = [bf16t(WKC * 512).rearrange("p (k c) -> p k c", c=512) for _ in range(2)]
            acc = f32t(2 * D).rearrange("p (a d) -> p a d", a=2)
            xt = f32t(D)
            gain = f32t(D)
            beta = f32t(D)
            sq = f32t(D)
            st2 = f32t(8)
            P.op("sp", I("dma_start", out=gain, in_=ln_g_d.partition_broadcast(128)), writes=["gain"], dma="c2")
            P.op("sp", I("dma_start", out=beta, in_=ln_b_d.partition_broadcast(128)), writes=["beta"], dma="c2")
            HT_v = HT.rearrange("(k p) t -> p k t", p=128)
            wo_v = w_out_bf.rearrange("(k p) c -> p k c", p=128)
            NG = (D + 511) // 512
            wit = 0
            pit = 0
            for tb in range(S // TB):
                t0 = tb * TB
                for (k0, k1) in ksplit(WKC):
                    P.op("sp", I("dma_start", out=hT[:, k0:k1, :], in_=HT_v[:, k0:k1, t0:t0 + TB]), writes=["hT"], dma="hTld")
                for ng in range(NG):
                    n0 = ng * 512
                    nw = min(512, D - n0)
                    s = wit % 2
                    wit += 1
                    for (k0, k1) in ksplit(WKC):
                        P.op("sp", I("dma_start", out=wo[s][:, k0:k1, 0:nw], in_=wo_v[:, k0:k1, n0:n0 + nw]),
                             writes=["wo%d" % s], dma="wold%d" % s)
                    for tt in range(TB // 128):
                        bk = pit % 4
                        pit += 1
                        for kc in range(WKC):
                            P.op("pe", I("matmul", banks[bk][:, 0:nw], lhsT=hT[:, kc, tt * 128:(tt + 1) * 128], rhs=wo[s][:, kc, 0:nw],
                                         start=(kc == 0), stop=(kc == WKC - 1)),
                                 reads=["hT", "wo%d" % s], writes=["ps%d" % bk])
                        P.op("dve", I("tensor_copy", out=acc[:, tt, n0:n0 + nw], in_=banks[bk][:, 0:nw]),
                             reads=["ps%d" % bk], writes=["acc%d" % tt])
                for tt in range(TB // 128):
                    r0 = t0 + tt * 128
                    P.op("sp", I("dma_start", out=xt, in_=x[r0:r0 + 128, :]), writes=["xt"], dma="xtld")
                    P.op("dve", I("scalar_tensor_tensor", out=acc[:, tt, :], in0=xt, scalar=cfg.alpha, in1=acc[:, tt, :],
                                  op0=ALU.mult, op1=ALU.add), reads=["xt", "acc%d" % tt], writes=["acc%d" % tt])
                    P.op("act", I("activation", out=sq, in_=acc[:, tt, :], func=AF.Copy, accum_out=st2[:, 0:1]),
                         reads=["acc%d" % tt], writes=["sq", "st20"])
                    P.op("dve", I("tensor_scalar", out=st2[:, 1:2], in0=st2[:, 0:1], scalar1=-1.0 / D, scalar2=None, op0=ALU.mult),
                         reads=["st20"], writes=["st21"])
                    P.op("act", I("activation", out=sq, in_=acc[:, tt, :], func=AF.Square, bias=st2[:, 1:2], scale=1.0,
                                  accum_out=st2[:, 2:3]), reads=["acc%d" % tt, "st21"], writes=["sq", "st22"])
                    P.op("act", I("activation", out=st2[:, 3:4], in_=st2[:, 2:3], func=AF.Sqrt, bias=eps_ln[:, 0:1], scale=1.0 / D),
                         reads=["st22"], writes=["st23"])
                    P.op("dve", I("reciprocal", out=st2[:, 4:5], in_=st2[:, 3:4]), reads=["st23"], writes=["st24"])
                    P.op("dve", I("tensor_scalar", out=sq, in0=acc[:, tt, :], scalar1=st2[:, 1:2], scalar2=st2[:, 4:5],
                                  op0=ALU.add, op1=ALU.mult), reads=["acc%d" % tt, "st21", "st24"], writes=["sq"])
                    P.op("pool", I("tensor_tensor", out=sq, in0=sq, in1=gain, op=ALU.mult), reads=["sq", "gain"], writes=["sq"])
                    P.op("pool", I("tensor_tensor", out=sq, in0=sq, in1=beta, op=ALU.add), reads=["sq", "beta"], writes=["sq"])
                    P.op("sp", I("dma_start", out=y_out[r0:r0 + 128, :], in_=sq), reads=["sq"], dma="yst")
        P.emit()
    return nc


_CACHE = {}


def kernel(**inputs):
    cfg = Cfg()
    lay = host_layout(cfg, inputs)
    xfull = np.asarray(inputs["x"], np.float32)
    if "nc" not in _CACHE:
        _CACHE["nc"] = build(cfg)
    nc = _CACHE["nc"]
    in_maps = []
    for c in range(8):
        m = dict(lay)
        m["x"] = np.ascontiguousarray(xfull[c // 4])
        in_maps.append(m)
    res = run_bass_kernel_spmd(nc, in_maps, core_ids=list(range(8)))
    out = np.empty((2, cfg.S, cfg.D), np.float32)
    q = cfg.S // 4
    for c in range(8):
        b, g = c // 4, c % 4
        out[b, g * q:(g + 1) * q] = res.results[c]["y"][g * q:(g + 1) * q]
    return out
```

```python
import contextlib
import numpy as np
import concourse.bass as bass
import concourse.mybir as mybir
from concourse.bass_utils import run_bass_kernel_spmd

F32 = mybir.dt.float32
BF16 = mybir.dt.bfloat16
AF = mybir.ActivationFunctionType
ALU = mybir.AluOpType
AX = mybir.AxisListType

ENGS = ["sp", "pe", "act", "dve", "pool"]
DECAY_C = float(np.exp(-0.5))


class Prog:
    def __init__(self, nc):
        self.nc = nc
        self.ops = {e: [] for e in ENGS}
        self.cnt = {e: 0 for e in ENGS}
        self.dma_cnt = {}
        self.last_w = {}
        self.readers = {}
        self.waited = {e: {} for e in ENGS}
        self.pending_barrier = {e: [] for e in ENGS}

    def _need(self, engine, tok, kind):
        semkey, val, src = tok
        if semkey.startswith("D:"):
            val = self.dma_cnt[semkey]
        elif src == engine:
            if engine == "pe":
                return None
            if kind != "RAW":
                return None
        if self.waited[engine].get(semkey, -1) >= val:
            return None
        self.waited[engine][semkey] = val
        return (semkey, val)

    def op(self, engine, fn, reads=(), writes=(), dma=None):
        cand = {}

        def add(tok, kind):
            semkey, val, src = tok
            if semkey.startswith("D:"):
                val = self.dma_cnt[semkey]
            elif src == engine:
                if engine == "pe" or kind != "RAW":
                    return
            if val > cand.get(semkey, -1):
                cand[semkey] = val

        for tok in self.pending_barrier[engine]:
            add(tok, "RAW")
        self.pending_barrier[engine] = []
        for k in reads:
            t = self.last_w.get(k)
            if t is not None:
                add(t, "RAW")
        for k in writes:
            t = self.last_w.get(k)
            if t is not None:
                add(t, "WAW")
            for t in self.readers.get(k, ()):
                add(t, "WAR")
        waits = []
        for semkey, val in cand.items():
            if self.waited[engine].get(semkey, -1) >= val:
                continue
            self.waited[engine][semkey] = val
            waits.append((semkey, val))
        if dma is not None:
            semkey = "D:" + dma
            self.dma_cnt[semkey] = self.dma_cnt.get(semkey, 0) + 16
            tok = (semkey, self.dma_cnt[semkey], None)
            inc = (semkey, 16)
        else:
            self.cnt[engine] += 1
            semkey = "E:" + engine
            tok = (semkey, self.cnt[engine], engine)
            inc = (semkey, 1)
        for k in writes:
            self.last_w[k] = tok
            self.readers[k] = []
        for k in reads:
            if k not in writes:
                self.readers.setdefault(k, []).append(tok)
        self.ops[engine].append((fn, waits, inc))
        return tok

    def barrier(self):
        toks = []
        for e in ENGS:
            if self.cnt[e] > 0:
                toks.append(("E:" + e, self.cnt[e], "__none__"))
        for semkey, v in self.dma_cnt.items():
            toks.append((semkey, v, None))
        for e in ENGS:
            self.pending_barrier[e] = list(toks)
        self.last_w = {}
        self.readers = {}

    def emit(self):
        nc = self.nc
        self.barrier()
        fin = []
        for tok in self.pending_barrier["sp"]:
            w = self._need("sp", tok, "RAW")
            if w:
                fin.append(w)
        semkeys = ["E:" + e for e in ENGS if self.cnt[e] > 0] + list(self.dma_cnt.keys())
        with contextlib.ExitStack() as st:
            sems = {}
            for i, k in enumerate(semkeys):
                sems[k] = st.enter_context(nc.semaphore("s%d" % i))
            block = st.enter_context(nc.Block())

            def run(engname, eng):
                for fn, waits, inc in self.ops[engname]:
                    fuse = bool(waits) and engname in ("act", "dve", "pool") and getattr(fn, "fusable", False)
                    for (sk, v) in (waits[1:] if fuse else waits):
                        eng.wait_ge(sems[sk], v)
                    ins = fn(eng)
                    if fuse:
                        ins._wait_ge(sems[waits[0][0]], waits[0][1])
                    ins.then_inc(sems[inc[0]], inc[1])
                if engname == "sp":
                    for (sk, v) in fin:
                        eng.wait_ge(sems[sk], v)

            @block.sync
            def _(e):
                run("sp", e)

            @block.tensor
            def _(e):
                run("pe", e)

            @block.scalar
            def _(e):
                run("act", e)

            @block.vector
            def _(e):
                run("dve", e)

            @block.gpsimd
            def _(e):
                run("pool", e)


class Cfg:
    def __init__(self, D=4096, S=8192, NF=16, NP=16, dbg=False, phases="ABCDE"):
        self.D, self.S, self.NF, self.NP = D, S, NF, NP
        self.FW = NF * 128
        self.RW = NP * 128
        self.W = self.FW + self.RW
        self.KC = D // 128
        self.NB = S // 512
        self.NKT = S // 128
        self.NCH = S // 64
        self.MISC = 256
        self.secs = [("q", self.FW), ("k", self.FW), ("v", self.FW), ("r", self.RW), ("kr", self.RW),
                     ("vr", self.RW), ("z", self.W), ("misc", self.MISC)]
        self.NC = sum(n for _, n in self.secs)
        self.dbg = dbg
        self.phases = phases
        self.alpha = 2.0 ** 0.25
        self.ln_eps = 1e-5
        self.gn_eps = 64e-5


def host_layout(cfg, inp):
    FW, RW, NF = cfg.FW, cfg.RW, cfg.NF
    w_in = np.asarray(inp["w_in"], np.float32)
    o = 0
    q = w_in[:, o:o + FW]; o += FW
    k = w_in[:, o:o + FW]; o += FW
    v = w_in[:, o:o + FW]; o += FW
    f = w_in[:, o:o + NF]; o += NF
    r = w_in[:, o:o + RW]; o += RW
    kr = w_in[:, o:o + RW]; o += RW
    vr = w_in[:, o:o + RW]; o += RW
    wd = w_in[:, o:o + 96]; o += 96
    ad = w_in[:, o:o + 96]; o += 96
    z = w_in[:, o:o + cfg.W]; o += cfg.W
    assert o == w_in.shape[1]
    misc = np.zeros((cfg.D, cfg.MISC), np.float32)
    misc[:, 0:96] = wd
    misc[:, 96:192] = ad
    misc[:, 192:192 + NF] = f
    w_perm = np.ascontiguousarray(np.concatenate([q, k, v, r, kr, vr, z, misc], axis=1))
    mu = np.asarray(inp["mu_shift"], np.float32)
    def pc(vec):
        return np.ascontiguousarray(np.asarray(vec, np.float32).reshape(cfg.NP, 128).T)
    ptab = np.stack([pc(mu[0:RW]), pc(mu[RW:2 * RW]), pc(mu[2 * RW:3 * RW]), pc(inp["w0"]), pc(inp["a0"]),
                     pc(inp["k_k"]), pc(inp["k_a"]), pc(np.asarray(inp["r_k"]).reshape(-1)),
                     pc(inp["gn_gain"]), pc(inp["gn_bias"])], axis=2)
    mtab = np.zeros((128, 2), np.float32)
    mtab[0:96, 0] = mu[3 * RW:3 * RW + 96]
    mtab[0:96, 1] = mu[3 * RW + 96:3 * RW + 192]
    fb = np.zeros((128, 1), np.float32)
    fb[0:NF, 0] = np.asarray(inp["f_bias"], np.float32)
    return {
        "w_in_p": w_perm,
        "w_out": np.ascontiguousarray(np.asarray(inp["w_out"], np.float32)),
        "ptab": np.ascontiguousarray(ptab),
        "mtab": mtab,
        "fb": fb,
        "w_up": np.ascontiguousarray(np.asarray(inp["w_up"], np.float32)),
        "a_up": np.ascontiguousarray(np.asarray(inp["a_up"], np.float32)),
        "ln_g": np.ascontiguousarray(np.asarray(inp["ln_gain"], np.float32).reshape(1, -1)),
        "ln_b": np.ascontiguousarray(np.asarray(inp["ln_bias"], np.float32).reshape(1, -1)),
    }


_FUSABLE = ("tensor_tensor", "tensor_copy", "tensor_scalar", "scalar_tensor_tensor", "copy", "reciprocal",
            "tensor_tensor_scan", "activation")


def I(method, *args, **kw):
    f = lambda e: getattr(e, method)(*args, **kw)
    f.fusable = method in _FUSABLE and "accum_out" not in kw
    return f


def build(cfg):
    D, S, NF, NP, KC, NB, NKT = cfg.D, cfg.S, cfg.NF, cfg.NP, cfg.KC, cfg.NB, cfg.NKT
    FW, RW, W, NC = cfg.FW, cfg.RW, cfg.W, cfg.NC
    nc = bass.Bass("TRN2", target_bir_lowering=False)

    def din(name, shape, dt=F32):
        return nc.dram_tensor(name, shape, dt, kind="ExternalInput").ap()

    x = din("x", [S, D])
    w_in_p = din("w_in_p", [D, NC])
    w_out = din("w_out", [W, D])
    ptab_d = din("ptab", [128, NP, 10])
    mtab_d = din("mtab", [128, 2])
    fb_d = din("fb", [128, 1])
    w_up_d = din("w_up", [96, RW])
    a_up_d = din("a_up", [96, RW])
    ln_g_d = din("ln_g", [1, D])
    ln_b_d = din("ln_b", [1, D])
    y_out = nc.dram_tensor("y", [S, D], F32, kind="ExternalOutput").ap()

    def scr(name, shape, dt):
        kind = "ExternalOutput" if (cfg.dbg and name in cfg.dbg) else "Internal"
        return nc.dram_tensor(name, shape, dt, kind=kind).ap()

    w_in_bf = scr("w_in_bf", [D, NC], BF16)
    w_out_bf = scr("w_out_bf", [W, D], BF16)
    QT = scr("QT", [FW, S], BF16)
    KT = scr("KT", [FW, S], BF16)
    Vs = scr("Vs", [S, FW], BF16)
    RWs = scr("RWs", [3 * RW, S], F32)
    MISCs = scr("MISCs", [256, S], F32)
    ZT = scr("ZT", [W, S], F32)
    HT = scr("HT", [W, S], BF16)

    P = Prog(nc)
    with contextlib.ExitStack() as st:
        ARENA_N = 52600
        arena = st.enter_context(nc.sbuf_tensor("arena", [128, ARENA_N], F32))
        banks = [st.enter_context(nc.psum_tensor("bank%d" % i, [128, 512], F32)) for i in range(8)]
        apos = [0]
        atop = [ARENA_N]

        def reset():
            apos[0] = 0

        def f32t(n):
            a = apos[0]
            apos[0] += n
            assert apos[0] <= atop[0], (apos[0], atop[0])
            return arena[:, a:a + n]

        def bf16t(n):
            n2 = (n + 1) // 2
            return f32t(n2).bitcast(BF16)[:, 0:n]

        def const_f32(n):
            atop[0] -= n
            return arena[:, atop[0]:atop[0] + n]

        ident_bf = const_f32(64).bitcast(BF16)
        ident_f = const_f32(128)
        ones_bd = const_f32(128)
        ones_bf = const_f32(64).bitcast(BF16)
        m_su = const_f32(128)
        m_sl = const_f32(128)
        m_ui = const_f32(128)
        tri_bf = const_f32(64).bitcast(BF16)
        tmpf = const_f32(128)
        m_su4 = const_f32(512)
        m_sl4 = const_f32(512)
        m_ui4 = const_f32(512)
        ident4 = const_f32(512)
        ptab = const_f32(NP * 10).rearrange("p (a b) -> p a b", b=10)
        omu = const_f32(NP * 3).rearrange("p (a b) -> p a b", b=3)
        omka = const_f32(NP)
        mtab = const_f32(2)
        omm = const_f32(2)
        fbc = const_f32(1)
        nfb = const_f32(1)
        eps_gn = const_f32(1)
        eps_ln = const_f32(1)

        P.op("pool", I("memset", ident_f, 0.0), writes=["ident_f"])
        P.op("pool", I("affine_select", out=ident_f, in_=ident_f, pattern=[[-1, 128]], compare_op=ALU.not_equal,
                       fill=1.0, base=0, channel_multiplier=1), reads=["ident_f"], writes=["ident_f"])
        P.op("pool", I("tensor_copy", out=ident_bf, in_=ident_f), reads=["ident_f"], writes=["ident_bf"])
        P.op("pool", I("memset", ones_bf, 1.0), writes=["ones_bf"])

        def tri_mask(t, key, chan_mult, step, cmp, bd=True):
            P.op("pool", I("memset", t, 1.0), writes=[key])
            P.op("pool", I("affine_select", out=t, in_=t, pattern=[[step, 128]], compare_op=cmp, fill=0.0,
                           base=0, channel_multiplier=chan_mult), reads=[key], writes=[key])
            if bd:
                P.op("pool", I("memset", t[0:64, 64:128], 0.0), reads=[key], writes=[key])
                P.op("pool", I("memset", t[64:128, 0:64], 0.0), reads=[key], writes=[key])
        tri_mask(m_su, "m_su", -1, 1, ALU.is_gt)
        tri_mask(m_sl, "m_sl", 1, -1, ALU.is_gt)
        tri_mask(m_ui, "m_ui", -1, 1, ALU.is_ge)
        tri_mask(tmpf, "tmpf", -1, 1, ALU.is_ge, bd=False)
        P.op("pool", I("tensor_copy", out=tri_bf, in_=tmpf), reads=["tmpf"], writes=["tri_bf"])
        for (m1, m4, k1, k4) in ((m_su, m_su4, "m_su", "m_su4"), (m_sl, m_sl4, "m_sl", "m_sl4"), (m_ui, m_ui4, "m_ui", "m_ui4"),
                                 (ident_f, ident4, "ident_f", "ident4")):
            for i in range(4):
                P.op("pool", I("tensor_copy", out=m4[:, i * 128:(i + 1) * 128], in_=m1), reads=[k1], writes=[k4])
        P.op("pool", I("memset", ones_bd, 1.0), writes=["ones_bd"])
        P.op("pool", I("memset", ones_bd[0:64, 64:128], 0.0), reads=["ones_bd"], writes=["ones_bd"])
        P.op("pool", I("memset", ones_bd[64:128, 0:64], 0.0), reads=["ones_bd"], writes=["ones_bd"])
        P.op("sp", I("dma_start", out=ptab, in_=ptab_d), writes=["ptab"], dma="c0")
        P.op("sp", I("dma_start", out=mtab, in_=mtab_d), writes=["mtab"], dma="c0")
        P.op("sp", I("dma_start", out=fbc, in_=fb_d), writes=["fbc"], dma="c0")
        P.op("dve", I("tensor_scalar", out=omu, in0=ptab[:, :, 0:3], scalar1=-1.0, scalar2=1.0, op0=ALU.mult, op1=ALU.add),
             reads=["ptab"], writes=["omu"])
        P.op("dve", I("tensor_scalar", out=omka, in0=ptab[:, :, 6], scalar1=-1.0, scalar2=1.0, op0=ALU.mult, op1=ALU.add),
             reads=["ptab"], writes=["omka"])
        P.op("dve", I("tensor_scalar", out=omm, in0=mtab, scalar1=-1.0, scalar2=1.0, op0=ALU.mult, op1=ALU.add),
             reads=["mtab"], writes=["omm"])
        P.op("dve", I("tensor_scalar", out=nfb, in0=fbc, scalar1=-1.0, scalar2=None, op0=ALU.mult),
             reads=["fbc"], writes=["nfb"])
        P.op("dve", I("memset", eps_gn, cfg.gn_eps), writes=["eps_gn"])
        P.op("dve", I("memset", eps_ln, cfg.ln_eps), writes=["eps_ln"])
        P.barrier()

        def castload(dst_bf, src_f32, key, dma):
            n = src_f32.shape[-1]
            for c0 in range(0, n, 2048):
                c1 = min(n, c0 + 2048)
                P.op("pool", I("dma_start", out=dst_bf[:, c0:c1], in_=src_f32[:, c0:c1]), writes=[key], dma=dma)

        def ksplit(n, step=8):
            return [(k0, min(n, k0 + step)) for k0 in range(0, n, step)]

        if "A" in cfg.phases:
            reset()
            wb = [bf16t(max(NC, D)) for _ in range(2)]
            it = 0
            for (src, dst, nrows, ncols) in ((w_in_p, w_in_bf, D, NC), (w_out, w_out_bf, W, D)):
                for rc in range(nrows // 128):
                    s = it % 2
                    it += 1
                    castload(wb[s][:, 0:ncols], src[rc * 128:(rc + 1) * 128, :], "wb%d" % s, "wbld%d" % s)
                    P.op("sp", I("dma_start", out=dst[rc * 128:(rc + 1) * 128, :], in_=wb[s][:, 0:ncols]),
                         reads=["wb%d" % s], dma="wbst%d" % s)
            P.barrier()

        if "B" in cfg.phases:
            reset()
            xb = bf16t(4 * D).rearrange("p (a d) -> p a d", a=4)
            xT = bf16t(KC * 512).rearrange("p (k t) -> p k t", t=512)
            wg = [bf16t(KC * 512).rearrange("p (k c) -> p k c", c=512) for _ in range(2)]
            stf = [f32t(512) for _ in range(4)]
            stb = [bf16t(512) for _ in range(4)]
            groups = []
            o = 0
            for name, n in cfg.secs:
                lo = 0
                while lo < n:
                    gw = min(512, n - lo)
                    groups.append((name, o + lo, gw, lo))
                    lo += gw
                o += n
            w_in_v = w_in_bf.rearrange("(k p) c -> p k c", p=128)
            git = 0
            pit = 0
            sit = 0
            sec_row = {"r": 0, "kr": RW, "vr": 2 * RW}
            for tb in range(NB):
                t0 = tb * 512
                for tt in range(4):
                    castload(xb[:, tt, :], x[t0 + tt * 128:t0 + (tt + 1) * 128, :], "xb", "xbld")
                for kc in range(KC):
                    bk = 4 + (kc % 2)
                    psb = banks[bk][:].bitcast(BF16)
                    for tt in range(4):
                        P.op("pe", I("transpose", out=psb[:, tt * 128:(tt + 1) * 128], in_=xb[:, tt, kc * 128:(kc + 1) * 128],
                                     identity=ident_bf), reads=["xb", "ident_bf"], writes=["ps%d" % bk])
                    if kc % 2 == 0:
                        P.op("act", I("copy", out=xT[:, kc, :], in_=psb[:, 0:512]), reads=["ps%d" % bk], writes=["xT%d" % kc])
                    else:
                        P.op("dve", I("tensor_copy", out=xT[:, kc, :], in_=psb[:, 0:512]), reads=["ps%d" % bk], writes=["xT%d" % kc])
                xTk = ["xT%d" % kc for kc in range(KC)]
                for (name, c0, gw, lo) in groups:
                    s = git % 2
                    git += 1
                    for (k0, k1) in ksplit(KC):
                        P.op("sp", I("dma_start", out=wg[s][:, k0:k1, 0:gw], in_=w_in_v[:, k0:k1, c0:c0 + gw]),
                             writes=["wg%d" % s], dma="wgld%d" % s)
                    if name == "v":
                        for tt in range(4):
                            bk = pit % 4
                            pit += 1
                            for kc in range(KC):
                                P.op("pe", I("matmul", banks[bk][:, 0:gw], lhsT=xT[:, kc, tt * 128:(tt + 1) * 128],
                                             rhs=wg[s][:, kc, 0:gw], start=(kc == 0), stop=(kc == KC - 1)),
                                     reads=[xTk[kc], "wg%d" % s], writes=["ps%d" % bk])
                            ss = sit % 4
                            sit += 1
                            P.op("act", I("copy", out=stb[ss][:, 0:gw], in_=banks[bk][:, 0:gw]),
                                 reads=["ps%d" % bk], writes=["stb%d" % ss])
                            P.op("sp", I("dma_start", out=Vs[t0 + tt * 128:t0 + (tt + 1) * 128, lo:lo + gw], in_=stb[ss][:, 0:gw]),
                                 reads=["stb%d" % ss], dma="stbst%d" % ss)
                        continue
                    for cc in range((gw + 127) // 128):
                        cw = min(128, gw - cc * 128)
                        bk = pit % 4
                        pit += 1
                        for kc in range(KC):
                            P.op("pe", I("matmul", banks[bk][0:cw, :], lhsT=wg[s][:, kc, cc * 128:cc * 128 + cw], rhs=xT[:, kc, :],
                                         start=(kc == 0), stop=(kc == KC - 1)),
                                 reads=[xTk[kc], "wg%d" % s], writes=["ps%d" % bk])
                        ss = sit % 4
                        sit += 1
                        row = lo + cc * 128
                        if name in ("q", "k"):
                            sc = (128.0 ** -0.5) if name == "q" else 1.0
                            dst = QT if name == "q" else KT
                            P.op("act", I("activation", out=stb[ss], in_=banks[bk][:], func=AF.Copy, scale=sc),
                                 reads=["ps%d" % bk], writes=["stb%d" % ss])
                            P.op("sp", I("dma_start", out=dst[row:row + 128, t0:t0 + 512], in_=stb[ss]),
                                 reads=["stb%d" % ss], dma="stbst%d" % ss)
                        elif name == "z":
                            P.op("act", I("activation", out=stf[ss], in_=banks[bk][:], func=AF.Silu),
                                 reads=["ps%d" % bk], writes=["stf%d" % ss])
                            P.op("sp", I("dma_start", out=ZT[row:row + 128, t0:t0 + 512], in_=stf[ss]),
                                 reads=["stf%d" % ss], dma="stfst%d" % ss)
                        else:
                            if name == "misc":
                                dst, r0 = MISCs, row
                            else:
                                dst, r0 = RWs, sec_row[name] + row
                            P.op("dve", I("tensor_copy", out=stf[ss][0:cw, :], in_=banks[bk][0:cw, :]),
                                 reads=["ps%d" % bk], writes=["stf%d" % ss])
                            P.op("sp", I("dma_start", out=dst[r0:r0 + cw, t0:t0 + 512], in_=stf[ss][0:cw, :]),
                                 reads=["stf%d" % ss], dma="stfst%d" % ss)
            P.barrier()

        if "C" in cfg.phases:
            reset()
            NQB = NB
            FFt = f32t(S)
            CL = f32t(S)
            onesF = f32t(S)
            CLR = f32t(NQB)
            SEL = f32t(NF * 128).rearrange("p (h m) -> p h m", m=128)
            CT = f32t(NKT * NF).rearrange("p (k h) -> p k h", h=NF)
            CREF = f32t(NF * NQB).rearrange("p (h q) -> p h q", q=NQB)
            KTh = [bf16t(S) for _ in range(2)]
            Vh = [bf16t(S).rearrange("p (k d) -> p k d", d=128) for _ in range(2)]
            Qb = [bf16t(512) for _ in range(2)]
            Zb = [f32t(512) for _ in range(2)]
            Pt = [bf16t(512) for _ in range(3)]
            Bt = [f32t(NKT) for _ in range(2)]
            Rr = f32t(512)
            Ot = f32t(512)
            Hb = [bf16t(512) for _ in range(2)]
            P.op("sp", I("dma_start", out=FFt[0:NF, :], in_=MISCs[192:192 + NF, :]), writes=["FFt"], dma="c1")
            P.op("pool", I("memset", onesF[0:NF, :], 1.0), writes=["onesF"])
            P.op("act", I("activation", out=FFt[0:NF, :], in_=FFt[0:NF, :], func=AF.Exp, bias=nfb[0:NF, :], scale=-1.0),
                 reads=["FFt", "nfb"], writes=["FFt"])
            P.op("act", I("activation", out=FFt[0:NF, :], in_=FFt[0:NF, :], func=AF.Ln, bias=1.0, scale=1.0),
                 reads=["FFt"], writes=["FFt"])
            P.op("dve", I("tensor_tensor_scan", out=CL[0:NF, :], data0=onesF[0:NF, :], data1=FFt[0:NF, :], initial=0.0,
                          op0=ALU.mult, op1=ALU.add), reads=["FFt", "onesF"], writes=["CL"])
            CLv = CL.rearrange("p (q t) -> p q t", t=512)
            P.op("dve", I("tensor_copy", out=CLR[0:NF, :], in_=CLv[0:NF, :, 256]), reads=["CL"], writes=["CLR"])
            P.op("pool", I("memset", SEL[0:NF], 0.0), writes=["SEL"])
            P.op("pool", I("affine_select", out=SEL[0:NF], in_=SEL[0:NF], pattern=[[-1, NF], [0, 128]],
                           compare_op=ALU.not_equal, fill=1.0, base=0, channel_multiplier=1), reads=["SEL"], writes=["SEL"])
            per = 512 // NF
            for k0 in range(0, NKT, per):
                k1 = min(NKT, k0 + per)
                bk = (k0 // per) % 2
                for kt in range(k0, k1):
                    P.op("pe", I("matmul", banks[bk][:, (kt - k0) * NF:(kt - k0 + 1) * NF], lhsT=CL[0:NF, kt * 128:(kt + 1) * 128],
                                 rhs=ident_f[0:NF, 0:NF], start=True, stop=True),
                         reads=["CL", "ident_f"], writes=["ps%d" % bk])
                P.op("dve", I("tensor_copy", out=CT[:, k0:k1, :],
                              in_=banks[bk][:, 0:(k1 - k0) * NF].rearrange("p (k h) -> p k h", h=NF)),
                     reads=["ps%d" % bk], writes=["CT"])
            for h in range(NF):
                bk = 2 + h % 2
                P.op("pe", I("matmul", banks[bk][:, 0:NQB], lhsT=SEL[0:NF, h, :], rhs=CLR[0:NF, :], start=True, stop=True),
                     reads=["SEL", "CLR"], writes=["ps%d" % bk])
                P.op("dve", I("tensor_copy", out=CREF[:, h, :], in_=banks[bk][:, 0:NQB]), reads=["ps%d" % bk], writes=["CREF"])
            Vs_v = Vs.rearrange("(k p) c -> p k c", p=128)
            sidx = 0
            for h in range(NF):
                hs = h % 2
                P.op("sp", I("dma_start", out=KTh[hs], in_=KT[h * 128:(h + 1) * 128, :]), writes=["KTh%d" % hs], dma="kth%d" % hs)
                for (k0, k1) in ksplit(NKT):
                    P.op("sp", I("dma_start", out=Vh[hs][:, k0:k1, :], in_=Vs_v[:, k0:k1, h * 128:(h + 1) * 128]),
                         writes=["Vh%d" % hs], dma="vh%d" % hs)
                for qb in range(NQB):
                    qs = qb % 2
                    q0 = qb * 512
                    P.op("sp", I("dma_start", out=Qb[qs], in_=QT[h * 128:(h + 1) * 128, q0:q0 + 512]), writes=["Qb%d" % qs], dma="qb%d" % qs)
                    P.op("sp", I("dma_start", out=Zb[qs], in_=ZT[h * 128:(h + 1) * 128, q0:q0 + 512]), writes=["Zb%d" % qs], dma="zb%d" % qs)
                    nkt = 4 * qb + 4
                    P.op("dve", I("tensor_scalar", out=Bt[qs][:, 0:nkt], in0=CT[:, 0:nkt, h], scalar1=CREF[:, h, qb:qb + 1],
                                  scalar2=None, op0=ALU.subtract), reads=["CT", "CREF"], writes=["Bt%d" % qs])
                    pso = 2 + qs
                    psr = 4 + qs
                    for kt in range(nkt):
                        j = kt - 4 * qb
                        cs = 0 if j < 0 else j * 128
                        bs = sidx % 2
                        ps_ = sidx % 3
                        sidx += 1
                        P.op("pe", I("matmul", banks[bs][:, cs:512], lhsT=KTh[hs][:, kt * 128:(kt + 1) * 128], rhs=Qb[qs][:, cs:512],
                                     start=True, stop=True), reads=["KTh%d" % hs, "Qb%d" % qs], writes=["ps%d" % bs])
                        P.op("act", I("activation", out=Pt[ps_][:, cs:512], in_=banks[bs][:, cs:512], func=AF.Exp,
                                      bias=Bt[qs][:, kt:kt + 1], scale=1.0),
                             reads=["ps%d" % bs, "Bt%d" % qs], writes=["Pt%d" % ps_])
                        if j >= 0:
                            P.op("pool", I("tensor_tensor", out=Pt[ps_][:, cs:cs + 128], in0=Pt[ps_][:, cs:cs + 128], in1=tri_bf,
                                           op=ALU.mult), reads=["Pt%d" % ps_, "tri_bf"], writes=["Pt%d" % ps_])
                        P.op("pe", I("matmul", banks[pso][:, cs:512], lhsT=Vh[hs][:, kt, :], rhs=Pt[ps_][:, cs:512],
                                     start=(kt == 0), stop=(kt == nkt - 1)),
                             reads=["Vh%d" % hs, "Pt%d" % ps_], writes=["ps%d" % pso])
                        P.op("pe", I("matmul", banks[psr][:, cs:512], lhsT=ones_bf, rhs=Pt[ps_][:, cs:512],
                                     start=(kt == 0), stop=(kt == nkt - 1)),
                             reads=["ones_bf", "Pt%d" % ps_], writes=["ps%d" % psr])
                    P.op("dve", I("reciprocal", out=Rr, in_=banks[psr][:]), reads=["ps%d" % psr], writes=["Rr"])
                    P.op("dve", I("tensor_tensor", out=Ot, in0=banks[pso][:], in1=Rr, op=ALU.mult),
                         reads=["ps%d" % pso, "Rr"], writes=["Ot"])
                    P.op("pool", I("tensor_tensor", out=Hb[qs], in0=Ot, in1=Zb[qs], op=ALU.mult),
                         reads=["Ot", "Zb%d" % qs], writes=["Hb%d" % qs])
                    P.op("sp", I("dma_start", out=HT[h * 128:(h + 1) * 128, q0:q0 + 512], in_=Hb[qs]),
                         reads=["Hb%d" % qs], dma="hb%d" % qs)
            P.barrier()

        if "D" in cfg.phases:
            reset()
            raw = {n: f32t(513) for n in ("r", "k", "v", "wd", "ad")}
            X = {n: f32t(512) for n in ("r", "k", "v", "wd", "ad")}
            tmp = {n: f32t(512) for n in ("r", "k", "v", "wd", "ad")}
            wup = f32t(128)
            aup = f32t(128)
            sg, av, kk, kk2, rn, kmod, t1, bvec, Ls, Ep, Em, Eex, Eend, bon, Zr, smask, yT, t2, t3 = [f32t(512) for _ in range(19)]
            WC = f32t(8)
            hTb = bf16t(512)
            bdn = ("a", "r", "b", "k", "Kh", "Bh", "v")
            bd = [{n: bf16t(8 * 128).rearrange("p (c t) -> p c t", t=128) for n in bdn} for _ in range(2)]
            S0 = f32t(64)
            S0b = bf16t(64)
            Pk = [bf16t(512) for _ in range(2)]
            PkT = [bf16t(512) for _ in range(2)]
            AccT = [f32t(512) for _ in range(2)]
            AccTb = [bf16t(512) for _ in range(2)]
            LakT, MrbT, MrkT, Kh_t4, Bh_t4 = [bf16t(512) for _ in range(5)]
            V_t4 = bf16t(256)
            Xc, Uc = [bf16t(64) for _ in range(2)]
            Yc4 = f32t(256)
            ysq4 = f32t(256)
            stat = f32t(20)
            Ybd = f32t(8 * 128).rearrange("p (c t) -> p c t", t=128)
            P.op("pool", I("memset", smask, 1.0), writes=["smask"])
            P.op("pool", I("memset", smask.rearrange("p (c t) -> p c t", t=64)[:, :, 0:1], 0.0), reads=["smask"], writes=["smask"])
            for b_ in range(2):
                for n in bdn:
                    P.op("pool", I("memset", bd[b_][n], 0.0), writes=["bd%d%s" % (b_, n)])
            P.op("pool", I("memset", Ybd, 0.0), writes=["Ybd0", "Ybd1"])

            def v3(t, p0=0, p1=128):
                return t[p0:p1, :].rearrange("p (c t) -> p c t", t=64)

            bidx = 0
            for pr in range(NP):
                f0 = pr * 128
                PC = [ptab[:, pr, j:j + 1] for j in range(10)]
                P.op("sp", I("dma_start", out=wup[0:96, :], in_=w_up_d[:, f0:f0 + 128]), writes=["wup"], dma="wup")
                P.op("sp", I("dma_start", out=aup[0:96, :], in_=a_up_d[:, f0:f0 + 128]), writes=["aup"], dma="wup")
                P.op("pool", I("memset", S0, 0.0), writes=["S0"])
                P.op("pool", I("memset", S0b, 0.0), writes=["S0b"])
                for tb in range(NB):
                    t0 = tb * 512
                    B = bd[bidx % 2]
                    bk_ = "bd%d" % (bidx % 2)
                    bidx += 1
                    srcs = [("r", RWs, f0, 128), ("k", RWs, RW + f0, 128), ("v", RWs, 2 * RW + f0, 128),
                            ("wd", MISCs, 0, 96), ("ad", MISCs, 96, 96)]
                    for n, src, row, nr in srcs:
                        if tb == 0:
                            P.op("pool", I("memset", raw[n][0:nr, 0:1], 0.0), writes=["raw" + n])
                            P.op("sp", I("dma_start", out=raw[n][0:nr, 1:513], in_=src[row:row + nr, 0:512]),
                                 writes=["raw" + n], dma="raw" + n)
                        else:
                            P.op("sp", I("dma_start", out=raw[n][0:nr, :], in_=src[row:row + nr, t0 - 1:t0 + 512]),
                                 writes=["raw" + n], dma="raw" + n)
                    P.op("sp", I("dma_start", out=Zr, in_=ZT[FW + f0:FW + f0 + 128, t0:t0 + 512]), writes=["Zr"], dma="zr")
                    lerp = [("r", "dve", PC[0], omu[:, pr, 0:1], 128), ("k", "dve", PC[1], omu[:, pr, 1:2], 128),
                            ("v", "dve", PC[2], omu[:, pr, 2:3], 128), ("wd", "dve", mtab[:, 0:1], omm[:, 0:1], 96),
                            ("ad", "dve", mtab[:, 1:2], omm[:, 1:2], 96)]
                    for n, eng, mu_c, omu_c, nr in lerp:
                        P.op(eng, I("tensor_scalar", out=tmp[n][0:nr, :], in0=raw[n][0:nr, 0:512], scalar1=mu_c[0:nr, :], scalar2=None,
                                    op0=ALU.mult), reads=["raw" + n, "ptab", "mtab"], writes=["tmp" + n])
                        P.op(eng, I("scalar_tensor_tensor", out=X[n][0:nr, :], in0=raw[n][0:nr, 1:513], scalar=omu_c[0:nr, :],
                                    in1=tmp[n][0:nr, :], op0=ALU.mult, op1=ALU.add),
                             reads=["raw" + n, "omu", "omm", "tmp" + n], writes=["X" + n])
                    P.op("act", I("activation", out=X["wd"][0:96, :], in_=X["wd"][0:96, :], func=AF.Tanh), reads=["Xwd"], writes=["Xwd"])
                    P.op("pe", I("matmul", banks[0][:], lhsT=wup[0:96, :], rhs=X["wd"][0:96, :], start=True, stop=True),
                         reads=["wup", "Xwd"], writes=["ps0"])
                    P.op("act", I("activation", out=sg, in_=banks[0][:], func=AF.Sigmoid, bias=PC[3], scale=1.0),
                         reads=["ps0", "ptab"], writes=["sg"])
                    P.op("pe", I("matmul", banks[1][:], lhsT=aup[0:96, :], rhs=X["ad"][0:96, :], start=True, stop=True),
                         reads=["aup", "Xad"], writes=["ps1"])
                    P.op("act", I("activation", out=av, in_=banks[1][:], func=AF.Sigmoid, bias=PC[4], scale=1.0),
                         reads=["ps1", "ptab"], writes=["av"])
                    P.op("pool", I("tensor_scalar", out=kk, in0=X["k"], scalar1=PC[5], scalar2=None, op0=ALU.mult),
                         reads=["Xk", "ptab"], writes=["kk"])
                    P.op("pool", I("tensor_tensor", out=kk2, in0=kk, in1=kk, op=ALU.mult), reads=["kk"], writes=["kk2"])
                    P.op("pe", I("matmul", banks[2][:], lhsT=ones_bd, rhs=kk2, start=True, stop=True),
                         reads=["ones_bd", "kk2"], writes=["ps2"])
                    P.op("dve", I("tensor_scalar", out=rn, in0=banks[2][:], scalar1=1e-24, scalar2=None, op0=ALU.max),
                         reads=["ps2"], writes=["rn"])
                    P.op("act", I("activation", out=rn, in_=rn, func=AF.Sqrt), reads=["rn"], writes=["rn"])
                    P.op("dve", I("reciprocal", out=rn, in_=rn), reads=["rn"], writes=["rn"])
                    P.op("dve", I("tensor_tensor", out=kk, in0=kk, in1=rn, op=ALU.mult), reads=["kk", "rn"], writes=["kk"])
                    P.op("dve", I("tensor_scalar", out=t1, in0=av, scalar1=PC[6], scalar2=omka[:, pr:pr + 1], op0=ALU.mult, op1=ALU.add),
                         reads=["av", "ptab", "omka"], writes=["t1"])
                    P.op("dve", I("tensor_tensor", out=kmod, in0=X["k"], in1=t1, op=ALU.mult), reads=["Xk", "t1"], writes=["kmod"])
                    P.op("pool", I("tensor_tensor", out=bvec, in0=kk, in1=av, op=ALU.mult), reads=["kk", "av"], writes=["bvec"])
                    P.op("dve", I("scalar_tensor_tensor", out=t2, in0=X["r"], scalar=PC[7], in1=kmod, op0=ALU.mult, op1=ALU.mult),
                         reads=["Xr", "ptab", "kmod"], writes=["t2"])
                    P.op("pe", I("matmul", banks[3][:], lhsT=ones_bd, rhs=t2, start=True, stop=True),
                         reads=["ones_bd", "t2"], writes=["ps3"])
                    P.op("dve", I("tensor_tensor", out=bon, in0=banks[3][:], in1=X["v"], op=ALU.mult), reads=["ps3", "Xv"], writes=["bon"])
                    P.op("dve", I("tensor_tensor_scan", out=Ls, data0=smask, data1=sg, initial=0.0, op0=ALU.mult, op1=ALU.add),
                         reads=["smask", "sg"], writes=["Ls"])
                    P.op("act", I("activation", out=Ep, in_=Ls, func=AF.Exp, scale=-DECAY_C), reads=["Ls"], writes=["Ep"])
                    P.op("act", I("activation", out=Em, in_=Ls, func=AF.Exp, scale=DECAY_C), reads=["Ls"], writes=["Em"])
                    P.op("pool", I("tensor_tensor", out=t3, in0=Ls, in1=sg, op=ALU.subtract), reads=["Ls", "sg"], writes=["t3"])
                    P.op("act", I("activation", out=Eex, in_=t3, func=AF.Exp, scale=-DECAY_C), reads=["t3"], writes=["Eex"])
                    Lsv = v3(Ls)
                    P.op("pool", I("tensor_tensor", out=v3(t3), in0=Lsv[:, :, 63:64].to_broadcast([128, 8, 64]), in1=Lsv, op=ALU.subtract),
                         reads=["Ls", "Eex"], writes=["t3"])
                    P.op("act", I("activation", out=Eend, in_=t3, func=AF.Exp, scale=-DECAY_C), reads=["t3"], writes=["Eend"])
                    P.op("act", I("activation", out=WC, in_=Lsv[:, :, 63], func=AF.Exp, scale=-DECAY_C), reads=["Ls"], writes=["WC"])
                    def bdw(eng, name, in0, in1, rk, neg=False):
                        for hh in range(2):
                            p0, p1 = hh * 64, hh * 64 + 64
                            o_ = B[name][p0:p1, :, p0:p1]
                            if in1 is None:
                                fn = I("tensor_copy", out=o_, in_=v3(in0, p0, p1))
                            elif neg:
                                fn = I("scalar_tensor_tensor", out=o_, in0=v3(in0, p0, p1), scalar=-1.0, in1=v3(in1, p0, p1),
                                       op0=ALU.mult, op1=ALU.mult)
                            else:
                                fn = I("tensor_tensor", out=o_, in0=v3(in0, p0, p1), in1=v3(in1, p0, p1), op=ALU.mult)
                            P.op(eng, fn, reads=rk, writes=[bk_ + name])
                    bdw("dve", "a", kk, Eex, ["kk", "Eex"], neg=True)
                    bdw("pool", "r", X["r"], Ep, ["Xr", "Ep"])
                    bdw("dve", "b", bvec, Em, ["bvec", "Em"])
                    bdw("pool", "k", kmod, Em, ["kmod", "Em"])
                    bdw("dve", "Kh", kmod, Eend, ["kmod", "Eend"])
                    bdw("pool", "Bh", bvec, Eend, ["bvec", "Eend"])
                    bdw("dve", "v", X["v"], None, ["Xv"])
                    ka, kr_, kb, kk_, kKh, kBh, kv = [bk_ + n for n in bdn]
                    for g in range(2):
                        p5 = banks[5][:].bitcast(BF16)
                        p4 = banks[4][:].bitcast(BF16)
                        for i in range(4):
                            c = 4 * g + i
                            P.op("pe", I("transpose", out=p5[:, i * 128:(i + 1) * 128], in_=B["Kh"][:, c, :], identity=ident_bf),
                                 reads=[kKh, "ident_bf"], writes=["ps5"])
                            P.op("pe", I("transpose", out=p5[:, 512 + i * 128:512 + (i + 1) * 128], in_=B["Bh"][:, c, :], identity=ident_bf),
                                 reads=[kBh, "ident_bf"], writes=["ps5"])
                            P.op("pe", I("transpose", out=p4[:, i * 128:(i + 1) * 128], in_=B["v"][:, c, :], identity=ident_bf),
                                 reads=[kv, "ident_bf"], writes=["ps4"])
                        P.op("act", I("copy", out=Kh_t4, in_=p5[:, 0:512]), reads=["ps5"], writes=["Kh_t"])
                        P.op("act", I("copy", out=Bh_t4, in_=p5[:, 512:1024]), reads=["ps5"], writes=["Bh_t"])
                        for hh in range(2):
                            p0, p1 = hh * 64, hh * 64 + 64
                            P.op("act", I("copy", out=V_t4[p0:p1, :].rearrange("p (c t) -> p c t", t=64),
                                          in_=p4[p0:p1, 0:512].rearrange("p (c t) -> p c t", t=128)[:, :, p0:p1]),
                                 reads=["ps4"], writes=["V_t"])
                        for i in range(4):
                            c = 4 * g + i
                            aT, rT, bT, kT = B["a"][:, c, :], B["r"][:, c, :], B["b"][:, c, :], B["k"][:, c, :]
                            cs = slice(i * 128, (i + 1) * 128)
                            P.op("pe", I("matmul", banks[0][:, cs], lhsT=aT, rhs=bT, start=True, stop=True), reads=[ka, kb], writes=["ps0"])
                            P.op("pe", I("matmul", banks[1][:, cs], lhsT=bT, rhs=aT, start=True, stop=True), reads=[ka, kb], writes=["ps1"])
                            P.op("pe", I("matmul", banks[2][:, cs], lhsT=kT, rhs=aT, start=True, stop=True), reads=[ka, kk_], writes=["ps2"])
                            P.op("pe", I("matmul", banks[3][:, cs], lhsT=bT, rhs=rT, start=True, stop=True), reads=[kr_, kb], writes=["ps3"])
                            P.op("pe", I("matmul", banks[6][:, cs], lhsT=kT, rhs=rT, start=True, stop=True), reads=[kr_, kk_], writes=["ps6"])
                        P.op("dve", I("tensor_tensor", out=Pk[0], in0=banks[0][:], in1=m_sl4, op=ALU.mult), reads=["ps0", "m_sl4"], writes=["Pk0"])
                        P.op("dve", I("tensor_tensor", out=PkT[0], in0=banks[1][:], in1=m_su4, op=ALU.mult), reads=["ps1", "m_su4"], writes=["PkT0"])
                        P.op("dve", I("tensor_tensor", out=LakT, in0=banks[2][:], in1=m_su4, op=ALU.mult), reads=["ps2", "m_su4"], writes=["LakT"])
                        P.op("dve", I("tensor_tensor", out=MrbT, in0=banks[3][:], in1=m_ui4, op=ALU.mult), reads=["ps3", "m_ui4"], writes=["MrbT"])
                        P.op("dve", I("tensor_tensor", out=MrkT, in0=banks[6][:], in1=m_ui4, op=ALU.mult), reads=["ps6", "m_ui4"], writes=["MrkT"])
                        P.op("pool", I("tensor_tensor", out=AccT[0], in0=PkT[0], in1=ident4, op=ALU.add), reads=["PkT0", "ident4"], writes=["AccT0"])
                        P.op("pool", I("tensor_copy", out=AccTb[0], in_=AccT[0]), reads=["AccT0"], writes=["AccTb0"])
                        for l in range(5):
                            s_, d_ = l % 2, (l + 1) % 2
                            for i in range(4):
                                cs = slice(i * 128, (i + 1) * 128)
                                P.op("pe", I("matmul", banks[0][:, cs], lhsT=PkT[s_][:, cs], rhs=Pk[s_][:, cs], start=True, stop=True),
                                     reads=["Pk%d" % s_, "PkT%d" % s_], writes=["ps0"])
                                if l < 4:
                                    P.op("pe", I("matmul", banks[1][:, cs], lhsT=Pk[s_][:, cs], rhs=PkT[s_][:, cs], start=True, stop=True),
                                         reads=["Pk%d" % s_, "PkT%d" % s_], writes=["ps1"])
                            P.op("act", I("copy", out=Pk[d_], in_=banks[0][:]), reads=["ps0"], writes=["Pk%d" % d_])
                            if l < 4:
                                P.op("act", I("copy", out=PkT[d_], in_=banks[1][:]), reads=["ps1"], writes=["PkT%d" % d_])
                            for i in range(4):
                                cs = slice(i * 128, (i + 1) * 128)
                                P.op("pe", I("matmul", banks[7][:, cs], lhsT=Pk[d_][:, cs], rhs=AccTb[s_][:, cs], start=True, stop=True),
                                     reads=["Pk%d" % d_, "AccTb%d" % s_], writes=["ps7"])
                            P.op("dve", I("tensor_tensor", out=AccT[d_], in0=banks[7][:], in1=AccT[s_], op=ALU.add),
                                 reads=["ps7", "AccT%d" % s_], writes=["AccT%d" % d_])
                            P.op("pool", I("tensor_copy", out=AccTb[d_], in_=AccT[d_]), reads=["AccT%d" % d_], writes=["AccTb%d" % d_])
                        for i in range(4):
                            c = 4 * g + i
                            aT, rT = B["a"][:, c, :], B["r"][:, c, :]
                            cs = slice(i * 128, (i + 1) * 128)
                            vs = slice(i * 64, (i + 1) * 64)
                            xs = slice(i * 128, i * 128 + 64)
                            us = slice(i * 128 + 64, i * 128 + 128)
                            P.op("pe", I("matmul", banks[2][:, xs], lhsT=aT, rhs=S0b, start=True, stop=False), reads=[ka, "S0b"], writes=["ps2"])
                            P.op("pe", I("matmul", banks[2][:, xs], lhsT=LakT[:, cs], rhs=V_t4[:, vs], start=False, stop=True),
                                 reads=["LakT", "V_t"], writes=["ps2"])
                            P.op("act", I("copy", out=Xc, in_=banks[2][:, xs]), reads=["ps2"], writes=["Xc"])
                            P.op("pe", I("matmul", banks[2][:, us], lhsT=AccTb[1][:, cs], rhs=Xc, start=True, stop=True),
                                 reads=["AccTb1", "Xc"], writes=["ps2"])
                            P.op("act", I("copy", out=Uc, in_=banks[2][:, us]), reads=["ps2"], writes=["Uc"])
                            P.op("pe", I("matmul", banks[3][:, vs], lhsT=rT, rhs=S0b, start=True, stop=False), reads=[kr_, "S0b"], writes=["ps3"])
                            P.op("pe", I("matmul", banks[3][:, vs], lhsT=MrbT[:, cs], rhs=Uc, start=False, stop=False), reads=["MrbT", "Uc"], writes=["ps3"])
                            P.op("pe", I("matmul", banks[3][:, vs], lhsT=MrkT[:, cs], rhs=V_t4[:, vs], start=False, stop=True),
                                 reads=["MrkT", "V_t"], writes=["ps3"])
                            P.op("pe", I("matmul", banks[6][:, 0:64], lhsT=Bh_t4[:, cs], rhs=Uc, start=True, stop=False), reads=["Bh_t", "Uc"], writes=["ps6"])
                            P.op("pe", I("matmul", banks[6][:, 0:64], lhsT=Kh_t4[:, cs], rhs=V_t4[:, vs], start=False, stop=True),
                                 reads=["Kh_t", "V_t"], writes=["ps6"])
                            P.op("dve", I("scalar_tensor_tensor", out=S0, in0=S0, scalar=WC[:, c:c + 1], in1=banks[6][:, 0:64],
                                          op0=ALU.mult, op1=ALU.add), reads=["S0", "WC", "ps6"], writes=["S0"])
                            P.op("pool", I("tensor_copy", out=S0b, in_=S0), reads=["S0"], writes=["S0b"])
                        Y3 = banks[3][:, 0:256].rearrange("p (c t) -> p c t", t=64)
                        Yc3 = Yc4.rearrange("p (c t) -> p c t", t=64)
                        P.op("dve", I("tensor_copy", out=Yc4, in_=banks[3][:, 0:256]), reads=["ps3"], writes=["Yc"])
                        P.op("dve", I("tensor_reduce", out=stat[:, 0:4], in_=Yc3, axis=AX.X, op=ALU.add), reads=["Yc"], writes=["stat0"])
                        P.op("dve", I("tensor_scalar", out=stat[:, 4:8], in0=stat[:, 0:4], scalar1=1.0 / 64.0, scalar2=None, op0=ALU.mult),
                             reads=["stat0"], writes=["stat1"])
                        P.op("dve", I("tensor_tensor", out=Yc3, in0=Yc3, in1=stat[:, 4:8].rearrange("p (c o) -> p c o", o=1).to_broadcast([128, 4, 64]),
                                      op=ALU.subtract), reads=["Yc", "stat1"], writes=["Yc"])
                        P.op("pool", I("tensor_tensor", out=ysq4, in0=Yc4, in1=Yc4, op=ALU.mult), reads=["Yc"], writes=["ysq"])
                        P.op("dve", I("tensor_reduce", out=stat[:, 8:12], in_=ysq4.rearrange("p (c t) -> p c t", t=64), axis=AX.X, op=ALU.add),
                             reads=["ysq"], writes=["stat2"])
                        P.op("act", I("activation", out=stat[:, 12:16], in_=stat[:, 8:12], func=AF.Sqrt, bias=eps_gn[:, 0:1], scale=1.0 / 64.0),
                             reads=["stat2", "eps_gn"], writes=["stat3"])
                        P.op("dve", I("reciprocal", out=stat[:, 16:20], in_=stat[:, 12:16]), reads=["stat3"], writes=["stat4"])
                        for hh in range(2):
                            p0, p1 = hh * 64, hh * 64 + 64
                            P.op("dve", I("tensor_tensor", out=Ybd[p0:p1, 4 * g:4 * g + 4, p0:p1], in0=Yc3[p0:p1],
                                          in1=stat[p0:p1, 16:20].rearrange("p (c o) -> p c o", o=1).to_broadcast([64, 4, 64]), op=ALU.mult),
                                 reads=["Yc", "stat4"], writes=["Ybd%d" % g])
                    for c in range(8):
                        bkk = 5 + (c % 2)
                        P.op("pe", I("transpose", out=banks[bkk][:, 0:128], in_=Ybd[:, c, :], identity=ident_f),
                             reads=["Ybd%d" % (c // 4), "ident_f"], writes=["ps%d" % bkk])
                        for hh in range(2):
                            p0, p1 = hh * 64, hh * 64 + 64
                            P.op("act", I("copy", out=yT[p0:p1, c * 64:(c + 1) * 64], in_=banks[bkk][p0:p1, p0:p1]),
                                 reads=["ps%d" % bkk], writes=["yT"])
                    P.op("dve", I("tensor_scalar", out=yT, in0=yT, scalar1=PC[8], scalar2=PC[9], op0=ALU.mult, op1=ALU.add),
                         reads=["yT", "ptab"], writes=["yT"])
                    P.op("dve", I("tensor_tensor", out=yT, in0=yT, in1=bon, op=ALU.add), reads=["yT", "bon"], writes=["yT"])
                    P.op("pool", I("tensor_tensor", out=hTb, in0=yT, in1=Zr, op=ALU.mult), reads=["yT", "Zr"], writes=["hTb"])
                    P.op("sp", I("dma_start", out=HT[FW + f0:FW + f0 + 128, t0:t0 + 512], in_=hTb), reads=["hTb"], dma="hTb")
            P.barrier()

        if "E" in cfg.phases:
            reset()
            TB = 256
            WKC = W // 128
            hT = bf16t(WKC * TB).rearrange("p (k t) -> p k t", t=TB)
            wo = [bf16t(WKC * 512).rearrange("p (k c) -> p k c", c=512) for _ in range(2)]
            acc = f32t(2 * D).rearrange("p (a d) -> p a d", a=2)
            xt = f32t(D)
            gain = f32t(D)
            beta = f32t(D)
            sq = f32t(D)
            st2 = f32t(8)
            P.op("sp", I("dma_start", out=gain, in_=ln_g_d.partition_broadcast(128)), writes=["gain"], dma="c2")
            P.op("sp", I("dma_start", out=beta, in_=ln_b_d.partition_broadcast(128)), writes=["beta"], dma="c2")
            HT_v = HT.rearrange("(k p) t -> p k t", p=128)
            wo_v = w_out_bf.rearrange("(k p) c -> p k c", p=128)
            NG = (D + 511) // 512
            wit = 0
            pit = 0
            for tb in range(S // TB):
                t0 = tb * TB
                for (k0, k1) in ksplit(WKC):
                    P.op("sp", I("dma_start", out=hT[:, k0:k1, :], in_=HT_v[:, k0:k1, t0:t0 + TB]), writes=["hT"], dma="hTld")
                for ng in range(NG):
                    n0 = ng * 512
                    nw = min(512, D - n0)
                    s = wit % 2
                    wit += 1
                    for (k0, k1) in ksplit(WKC):
                        P.op("sp", I("dma_start", out=wo[s][:, k0:k1, 0:nw], in_=wo_v[:, k0:k1, n0:n0 + nw]),
                             writes=["wo%d" % s], dma="wold%d" % s)
                    for tt in range(TB // 128):
                        bk = pit % 4
                        pit += 1
                        for kc in range(WKC):
                            P.op("pe", I("matmul", banks[bk][:, 0:nw], lhsT=hT[:, kc, tt * 128:(tt + 1) * 128], rhs=wo[s][:, kc, 0:nw],
                                         start=(kc == 0), stop=(kc == WKC - 1)),
                                 reads=["hT", "wo%d" % s], writes=["ps%d" % bk])
                        P.op("dve", I("tensor_copy", out=acc[:, tt, n0:n0 + nw], in_=banks[bk][:, 0:nw]),
                             reads=["ps%d" % bk], writes=["acc%d" % tt])
                for tt in range(TB // 128):
                    r0 = t0 + tt * 128
                    P.op("sp", I("dma_start", out=xt, in_=x[r0:r0 + 128, :]), writes=["xt"], dma="xtld")
                    P.op("dve", I("scalar_tensor_tensor", out=acc[:, tt, :], in0=xt, scalar=cfg.alpha, in1=acc[:, tt, :],
                                  op0=ALU.mult, op1=ALU.add), reads=["xt", "acc%d" % tt], writes=["acc%d" % tt])
                    P.op("act", I("activation", out=sq, in_=acc[:, tt, :], func=AF.Copy, accum_out=st2[:, 0:1]),
                         reads=["acc%d" % tt], writes=["sq", "st20"])
                    P.op("dve", I("tensor_scalar", out=st2[:, 1:2], in0=st2[:, 0:1], scalar1=-1.0 / D, scalar2=None, op0=ALU.mult),
                         reads=["st20"], writes=["st21"])
                    P.op("act", I("activation", out=sq, in_=acc[:, tt, :], func=AF.Square, bias=st2[:, 1:2], scale=1.0,
                                  accum_out=st2[:, 2:3]), reads=["acc%d" % tt, "st21"], writes=["sq", "st22"])
                    P.op("act", I("activation", out=st2[:, 3:4], in_=st2[:, 2:3], func=AF.Sqrt, bias=eps_ln[:, 0:1], scale=1.0 / D),
                         reads=["st22"], writes=["st23"])
                    P.op("dve", I("reciprocal", out=st2[:, 4:5], in_=st2[:, 3:4]), reads=["st23"], writes=["st24"])
                    P.op("dve", I("tensor_scalar", out=sq, in0=acc[:, tt, :], scalar1=st2[:, 1:2], scalar2=st2[:, 4:5],
                                  op0=ALU.add, op1=ALU.mult), reads=["acc%d" % tt, "st21", "st24"], writes=["sq"])
                    P.op("pool", I("tensor_tensor", out=sq, in0=sq, in1=gain, op=ALU.mult), reads=["sq", "gain"], writes=["sq"])
                    P.op("pool", I("tensor_tensor", out=sq, in0=sq, in1=beta, op=ALU.add), reads=["sq", "beta"], writes=["sq"])
                    P.op("sp", I("dma_start", out=y_out[r0:r0 + 128, :], in_=sq), reads=["sq"], dma="yst")
        P.emit()
    return nc


_CACHE = {}


def kernel(**inputs):
    cfg = Cfg()
    lay = host_layout(cfg, inputs)
    xfull = np.asarray(inputs["x"], np.float32)
    if "nc" not in _CACHE:
        _CACHE["nc"] = build(cfg)
    nc = _CACHE["nc"]
    in_maps = []
    for c in range(8):
        m = dict(lay)
        m["x"] = np.ascontiguousarray(xfull[c // 4])
        in_maps.append(m)
    res = run_bass_kernel_spmd(nc, in_maps, core_ids=list(range(8)))
    out = np.empty((2, cfg.S, cfg.D), np.float32)
    q = cfg.S // 4
    for c in range(8):
        b, g = c // 4, c % 4
        out[b, g * q:(g + 1) * q] = res.results[c]["y"][g * q:(g + 1) * q]
    return out
```

```python
import contextlib
import numpy as np
import concourse.bass as bass
import concourse.mybir as mybir
from concourse.bass_utils import run_bass_kernel_spmd

F32 = mybir.dt.float32
BF16 = mybir.dt.bfloat16
AF = mybir.ActivationFunctionType
ALU = mybir.AluOpType
AX = mybir.AxisListType

ENGS = ["sp", "pe", "act", "dve", "pool"]
DECAY_C = float(np.exp(-0.5))


class Prog:
    def __init__(self, nc):
        self.nc = nc
        self.ops = {e: [] for e in ENGS}
        self.cnt = {e: 0 for e in ENGS}
        self.dma_cnt = {}
        self.last_w = {}
        self.readers = {}
        self.waited = {e: {} for e in ENGS}
        self.pending_barrier = {e: [] for e in ENGS}

    def _need(self, engine, tok, kind):
        semkey, val, src = tok
        if semkey.startswith("D:"):
            val = self.dma_cnt[semkey]
        elif src == engine:
            if engine == "pe":
                return None
            if kind != "RAW":
                return None
        if self.waited[engine].get(semkey, -1) >= val:
            return None
        self.waited[engine][semkey] = val
        return (semkey, val)

    def op(self, engine, fn, reads=(), writes=(), dma=None):
        cand = {}

        def add(tok, kind):
            semkey, val, src = tok
            if semkey.startswith("D:"):
                val = self.dma_cnt[semkey]
            elif src == engine:
                if engine == "pe" or kind != "RAW":
                    return
            if val > cand.get(semkey, -1):
                cand[semkey] = val

        for tok in self.pending_barrier[engine]:
            add(tok, "RAW")
        self.pending_barrier[engine] = []
        for k in reads:
            t = self.last_w.get(k)
            if t is not None:
                add(t, "RAW")
        for k in writes:
            t = self.last_w.get(k)
            if t is not None:
                add(t, "WAW")
            for t in self.readers.get(k, ()):
                add(t, "WAR")
        waits = []
        for semkey, val in cand.items():
            if self.waited[engine].get(semkey, -1) >= val:
                continue
            self.waited[engine][semkey] = val
            waits.append((semkey, val))
        if dma is not None:
            semkey = "D:" + dma
            self.dma_cnt[semkey] = self.dma_cnt.get(semkey, 0) + 16
            tok = (semkey, self.dma_cnt[semkey], None)
            inc = (semkey, 16)
        else:
            self.cnt[engine] += 1
            semkey = "E:" + engine
            tok = (semkey, self.cnt[engine], engine)
            inc = (semkey, 1)
        for k in writes:
            self.last_w[k] = tok
            self.readers[k] = []
        for k in reads:
            if k not in writes:
                self.readers.setdefault(k, []).append(tok)
        self.ops[engine].append((fn, waits, inc))
        return tok

    def barrier(self):
        toks = []
        for e in ENGS:
            if self.cnt[e] > 0:
                toks.append(("E:" + e, self.cnt[e], "__none__"))
        for semkey, v in self.dma_cnt.items():
            toks.append((semkey, v, None))
        for e in ENGS:
            self.pending_barrier[e] = list(toks)
        self.last_w = {}
        self.readers = {}

    def emit(self):
        nc = self.nc
        self.barrier()
        fin = []
        for tok in self.pending_barrier["sp"]:
            w = self._need("sp", tok, "RAW")
            if w:
                fin.append(w)
        semkeys = ["E:" + e for e in ENGS if self.cnt[e] > 0] + list(self.dma_cnt.keys())
        with contextlib.ExitStack() as st:
            sems = {}
            for i, k in enumerate(semkeys):
                sems[k] = st.enter_context(nc.semaphore("s%d" % i))
            block = st.enter_context(nc.Block())

            def run(engname, eng):
                for fn, waits, inc in self.ops[engname]:
                    fuse = bool(waits) and engname in ("act", "dve", "pool") and getattr(fn, "fusable", False)
                    for (sk, v) in (waits[1:] if fuse else waits):
                        eng.wait_ge(sems[sk], v)
                    ins = fn(eng)
                    if fuse:
                        ins._wait_ge(sems[waits[0][0]], waits[0][1])
                    ins.then_inc(sems[inc[0]], inc[1])
                if engname == "sp":
                    for (sk, v) in fin:
                        eng.wait_ge(sems[sk], v)

            @block.sync
            def _(e):
                run("sp", e)

            @block.tensor
            def _(e):
                run("pe", e)

            @block.scalar
            def _(e):
                run("act", e)

            @block.vector
            def _(e):
                run("dve", e)

            @block.gpsimd
            def _(e):
                run("pool", e)


class Cfg:
    def __init__(self, D=4096, S=8192, NF=16, NP=16, dbg=False, phases="ABCDE"):
        self.D, self.S, self.NF, self.NP = D, S, NF, NP
        self.FW = NF * 128
        self.RW = NP * 128
        self.W = self.FW + self.RW
        self.KC = D // 128
        self.NB = S // 512
        self.NKT = S // 128
        self.NCH = S // 64
        self.MISC = 256
        self.secs = [("q", self.FW), ("k", self.FW), ("v", self.FW), ("r", self.RW), ("kr", self.RW),
                     ("vr", self.RW), ("z", self.W), ("misc", self.MISC)]
        self.NC = sum(n for _, n in self.secs)
        self.dbg = dbg
        self.phases = phases
        self.alpha = 2.0 ** 0.25
        self.ln_eps = 1e-5
        self.gn_eps = 64e-5


def host_layout(cfg, inp):
    FW, RW, NF = cfg.FW, cfg.RW, cfg.NF
    w_in = np.asarray(inp["w_in"], np.float32)
    o = 0
    q = w_in[:, o:o + FW]; o += FW
    k = w_in[:, o:o + FW]; o += FW
    v = w_in[:, o:o + FW]; o += FW
    f = w_in[:, o:o + NF]; o += NF
    r = w_in[:, o:o + RW]; o += RW
    kr = w_in[:, o:o + RW]; o += RW
    vr = w_in[:, o:o + RW]; o += RW
    wd = w_in[:, o:o + 96]; o += 96
    ad = w_in[:, o:o + 96]; o += 96
    z = w_in[:, o:o + cfg.W]; o += cfg.W
    assert o == w_in.shape[1]
    misc = np.zeros((cfg.D, cfg.MISC), np.float32)
    misc[:, 0:96] = wd
    misc[:, 96:192] = ad
    misc[:, 192:192 + NF] = f
    w_perm = np.ascontiguousarray(np.concatenate([q, k, v, r, kr, vr, z, misc], axis=1))
    mu = np.asarray(inp["mu_shift"], np.float32)
    def pc(vec):
        return np.ascontiguousarray(np.asarray(vec, np.float32).reshape(cfg.NP, 128).T)
    ptab = np.stack([pc(mu[0:RW]), pc(mu[RW:2 * RW]), pc(mu[2 * RW:3 * RW]), pc(inp["w0"]), pc(inp["a0"]),
                     pc(inp["k_k"]), pc(inp["k_a"]), pc(np.asarray(inp["r_k"]).reshape(-1)),
                     pc(inp["gn_gain"]), pc(inp["gn_bias"])], axis=2)
    mtab = np.zeros((128, 2), np.float32)
    mtab[0:96, 0] = mu[3 * RW:3 * RW + 96]
    mtab[0:96, 1] = mu[3 * RW + 96:3 * RW + 192]
    fb = np.zeros((128, 1), np.float32)
    fb[0:NF, 0] = np.asarray(inp["f_bias"], np.float32)
    return {
        "w_in_p": w_perm,
        "w_out": np.ascontiguousarray(np.asarray(inp["w_out"], np.float32)),
        "ptab": np.ascontiguousarray(ptab),
        "mtab": mtab,
        "fb": fb,
        "w_up": np.ascontiguousarray(np.asarray(inp["w_up"], np.float32)),
        "a_up": np.ascontiguousarray(np.asarray(inp["a_up"], np.float32)),
        "ln_g": np.ascontiguousarray(np.asarray(inp["ln_gain"], np.float32).reshape(1, -1)),
        "ln_b": np.ascontiguousarray(np.asarray(inp["ln_bias"], np.float32).reshape(1, -1)),
    }


_FUSABLE = ("tensor_tensor", "tensor_copy", "tensor_scalar", "scalar_tensor_tensor", "copy", "reciprocal",
            "tensor_tensor_scan", "activation")


def I(method, *args, **kw):
    f = lambda e: getattr(e, method)(*args, **kw)
    f.fusable = method in _FUSABLE and "accum_out" not in kw
    return f


def build(cfg):
    D, S, NF, NP, KC, NB, NKT = cfg.D, cfg.S, cfg.NF, cfg.NP, cfg.KC, cfg.NB, cfg.NKT
    FW, RW, W, NC = cfg.FW, cfg.RW, cfg.W, cfg.NC
    nc = bass.Bass("TRN2", target_bir_lowering=False)

    def din(name, shape, dt=F32):
        return nc.dram_tensor(name, shape, dt, kind="ExternalInput").ap()

    x = din("x", [S, D])
    w_in_p = din("w_in_p", [D, NC])
    w_out = din("w_out", [W, D])
    ptab_d = din("ptab", [128, NP, 10])
    mtab_d = din("mtab", [128, 2])
    fb_d = din("fb", [128, 1])
    w_up_d = din("w_up", [96, RW])
    a_up_d = din("a_up", [96, RW])
    ln_g_d = din("ln_g", [1, D])
    ln_b_d = din("ln_b", [1, D])
    y_out = nc.dram_tensor("y", [S, D], F32, kind="ExternalOutput").ap()

    def scr(name, shape, dt):
        kind = "ExternalOutput" if (cfg.dbg and name in cfg.dbg) else "Internal"
        return nc.dram_tensor(name, shape, dt, kind=kind).ap()

    w_in_bf = scr("w_in_bf", [D, NC], BF16)
    w_out_bf = scr("w_out_bf", [W, D], BF16)
    QT = scr("QT", [FW, S], BF16)
    KT = scr("KT", [FW, S], BF16)
    Vs = scr("Vs", [S, FW], BF16)
    RWs = scr("RWs", [3 * RW, S], F32)
    MISCs = scr("MISCs", [256, S], F32)
    ZT = scr("ZT", [W, S], F32)
    HT = scr("HT", [W, S], BF16)

    P = Prog(nc)
    with contextlib.ExitStack() as st:
        ARENA_N = 52600
        arena = st.enter_context(nc.sbuf_tensor("arena", [128, ARENA_N], F32))
        banks = [st.enter_context(nc.psum_tensor("bank%d" % i, [128, 512], F32)) for i in range(8)]
        apos = [0]
        atop = [ARENA_N]

        def reset():
            apos[0] = 0

        def f32t(n):
            a = apos[0]
            apos[0] += n
            assert apos[0] <= atop[0], (apos[0], atop[0])
            return arena[:, a:a + n]

        def bf16t(n):
            n2 = (n + 1) // 2
            return f32t(n2).bitcast(BF16)[:, 0:n]

        def const_f32(n):
            atop[0] -= n
            return arena[:, atop[0]:atop[0] + n]

        ident_bf = const_f32(64).bitcast(BF16)
        ident_f = const_f32(128)
        ones_bd = const_f32(128)
        ones_bf = const_f32(64).bitcast(BF16)
        m_su = const_f32(128)
        m_sl = const_f32(128)
        m_ui = const_f32(128)
        tri_bf = const_f32(64).bitcast(BF16)
        tmpf = const_f32(128)
        m_su4 = const_f32(512)
        m_sl4 = const_f32(512)
        m_ui4 = const_f32(512)
        ident4 = const_f32(512)
        ptab = const_f32(NP * 10).rearrange("p (a b) -> p a b", b=10)
        omu = const_f32(NP * 3).rearrange("p (a b) -> p a b", b=3)
        omka = const_f32(NP)
        mtab = const_f32(2)
        omm = const_f32(2)
        fbc = const_f32(1)
        nfb = const_f32(1)
        eps_gn = const_f32(1)
        eps_ln = const_f32(1)

        P.op("pool", I("memset", ident_f, 0.0), writes=["ident_f"])
        P.op("pool", I("affine_select", out=ident_f, in_=ident_f, pattern=[[-1, 128]], compare_op=ALU.not_equal,
                       fill=1.0, base=0, channel_multiplier=1), reads=["ident_f"], writes=["ident_f"])
        P.op("pool", I("tensor_copy", out=ident_bf, in_=ident_f), reads=["ident_f"], writes=["ident_bf"])
        P.op("pool", I("memset", ones_bf, 1.0), writes=["ones_bf"])

        def tri_mask(t, key, chan_mult, step, cmp, bd=True):
            P.op("pool", I("memset", t, 1.0), writes=[key])
            P.op("pool", I("affine_select", out=t, in_=t, pattern=[[step, 128]], compare_op=cmp, fill=0.0,
                           base=0, channel_multiplier=chan_mult), reads=[key], writes=[key])
            if bd:
                P.op("pool", I("memset", t[0:64, 64:128], 0.0), reads=[key], writes=[key])
                P.op("pool", I("memset", t[64:128, 0:64], 0.0), reads=[key], writes=[key])
        tri_mask(m_su, "m_su", -1, 1, ALU.is_gt)
        tri_mask(m_sl, "m_sl", 1, -1, ALU.is_gt)
        tri_mask(m_ui, "m_ui", -1, 1, ALU.is_ge)
        tri_mask(tmpf, "tmpf", -1, 1, ALU.is_ge, bd=False)
        P.op("pool", I("tensor_copy", out=tri_bf, in_=tmpf), reads=["tmpf"], writes=["tri_bf"])
        for (m1, m4, k1, k4) in ((m_su, m_su4, "m_su", "m_su4"), (m_sl, m_sl4, "m_sl", "m_sl4"), (m_ui, m_ui4, "m_ui", "m_ui4"),
                                 (ident_f, ident4, "ident_f", "ident4")):
            for i in range(4):
                P.op("pool", I("tensor_copy", out=m4[:, i * 128:(i + 1) * 128], in_=m1), reads=[k1], writes=[k4])
        P.op("pool", I("memset", ones_bd, 1.0), writes=["ones_bd"])
        P.op("pool", I("memset", ones_bd[0:64, 64:128], 0.0), reads=["ones_bd"], writes=["ones_bd"])
        P.op("pool", I("memset", ones_bd[64:128, 0:64], 0.0), reads=["ones_bd"], writes=["ones_bd"])
        P.op("sp", I("dma_start", out=ptab, in_=ptab_d), writes=["ptab"], dma="c0")
        P.op("sp", I("dma_start", out=mtab, in_=mtab_d), writes=["mtab"], dma="c0")
        P.op("sp", I("dma_start", out=fbc, in_=fb_d), writes=["fbc"], dma="c0")
        P.op("dve", I("tensor_scalar", out=omu, in0=ptab[:, :, 0:3], scalar1=-1.0, scalar2=1.0, op0=ALU.mult, op1=ALU.add),
             reads=["ptab"], writes=["omu"])
        P.op("dve", I("tensor_scalar", out=omka, in0=ptab[:, :, 6], scalar1=-1.0, scalar2=1.0, op0=ALU.mult, op1=ALU.add),
             reads=["ptab"], writes=["omka"])
        P.op("dve", I("tensor_scalar", out=omm, in0=mtab, scalar1=-1.0, scalar2=1.0, op0=ALU.mult, op1=ALU.add),
             reads=["mtab"], writes=["omm"])
        P.op("dve", I("tensor_scalar", out=nfb, in0=fbc, scalar1=-1.0, scalar2=None, op0=ALU.mult),
             reads=["fbc"], writes=["nfb"])
        P.op("dve", I("memset", eps_gn, cfg.gn_eps), writes=["eps_gn"])
        P.op("dve", I("memset", eps_ln, cfg.ln_eps), writes=["eps_ln"])
        P.barrier()

        def castload(dst_bf, src_f32, key, dma):
            n = src_f32.shape[-1]
            for c0 in range(0, n, 2048):
                c1 = min(n, c0 + 2048)
                P.op("pool", I("dma_start", out=dst_bf[:, c0:c1], in_=src_f32[:, c0:c1]), writes=[key], dma=dma)

        def ksplit(n, step=8):
            return [(k0, min(n, k0 + step)) for k0 in range(0, n, step)]

        if "A" in cfg.phases:
            reset()
            wb = [bf16t(max(NC, D)) for _ in range(2)]
            it = 0
            for (src, dst, nrows, ncols) in ((w_in_p, w_in_bf, D, NC), (w_out, w_out_bf, W, D)):
                for rc in range(nrows // 128):
                    s = it % 2
                    it += 1
                    castload(wb[s][:, 0:ncols], src[rc * 128:(rc + 1) * 128, :], "wb%d" % s, "wbld%d" % s)
                    P.op("sp", I("dma_start", out=dst[rc * 128:(rc + 1) * 128, :], in_=wb[s][:, 0:ncols]),
                         reads=["wb%d" % s], dma="wbst%d" % s)
            P.barrier()

        if "B" in cfg.phases:
            reset()
            xb = bf16t(4 * D).rearrange("p (a d) -> p a d", a=4)
            xT = bf16t(KC * 512).rearrange("p (k t) -> p k t", t=512)
            wg = [bf16t(KC * 512).rearrange("p (k c) -> p k c", c=512) for _ in range(2)]
            stf = [f32t(512) for _ in range(4)]
            stb = [bf16t(512) for _ in range(4)]
            groups = []
            o = 0
            for name, n in cfg.secs:
                lo = 0
                while lo < n:
                    gw = min(512, n - lo)
                    groups.append((name, o + lo, gw, lo))
                    lo += gw
                o += n
            w_in_v = w_in_bf.rearrange("(k p) c -> p k c", p=128)
            git = 0
            pit = 0
            sit = 0
            sec_row = {"r": 0, "kr": RW, "vr": 2 * RW}
            for tb in range(NB):
                t0 = tb * 512
                for tt in range(4):
                    castload(xb[:, tt, :], x[t0 + tt * 128:t0 + (tt + 1) * 128, :], "xb", "xbld")
                for kc in range(KC):
                    bk = 4 + (kc % 2)
                    psb = banks[bk][:].bitcast(BF16)
                    for tt in range(4):
                        P.op("pe", I("transpose", out=psb[:, tt * 128:(tt + 1) * 128], in_=xb[:, tt, kc * 128:(kc + 1) * 128],
                                     identity=ident_bf), reads=["xb", "ident_bf"], writes=["ps%d" % bk])
                    if kc % 2 == 0:
                        P.op("act", I("copy", out=xT[:, kc, :], in_=psb[:, 0:512]), reads=["ps%d" % bk], writes=["xT%d" % kc])
                    else:
                        P.op("dve", I("tensor_copy", out=xT[:, kc, :], in_=psb[:, 0:512]), reads=["ps%d" % bk], writes=["xT%d" % kc])
                xTk = ["xT%d" % kc for kc in range(KC)]
                for (name, c0, gw, lo) in groups:
                    s = git % 2
                    git += 1
                    for (k0, k1) in ksplit(KC):
                        P.op("sp", I("dma_start", out=wg[s][:, k0:k1, 0:gw], in_=w_in_v[:, k0:k1, c0:c0 + gw]),
                             writes=["wg%d" % s], dma="wgld%d" % s)
                    if name == "v":
                        for tt in range(4):
                            bk = pit % 4
                            pit += 1
                            for kc in range(KC):
                                P.op("pe", I("matmul", banks[bk][:, 0:gw], lhsT=xT[:, kc, tt * 128:(tt + 1) * 128],
                                             rhs=wg[s][:, kc, 0:gw], start=(kc == 0), stop=(kc == KC - 1)),
                                     reads=[xTk[kc], "wg%d" % s], writes=["ps%d" % bk])
                            ss = sit % 4
                            sit += 1
                            P.op("act", I("copy", out=stb[ss][:, 0:gw], in_=banks[bk][:, 0:gw]),
                                 reads=["ps%d" % bk], writes=["stb%d" % ss])
                            P.op("sp", I("dma_start", out=Vs[t0 + tt * 128:t0 + (tt + 1) * 128, lo:lo + gw], in_=stb[ss][:, 0:gw]),
                                 reads=["stb%d" % ss], dma="stbst%d" % ss)
                        continue
                    for cc in range((gw + 127) // 128):
                        cw = min(128, gw - cc * 128)
                        bk = pit % 4
                        pit += 1
                        for kc in range(KC):
                            P.op("pe", I("matmul", banks[bk][0:cw, :], lhsT=wg[s][:, kc, cc * 128:cc * 128 + cw], rhs=xT[:, kc, :],
                                         start=(kc == 0), stop=(kc == KC - 1)),
                                 reads=[xTk[kc], "wg%d" % s], writes=["ps%d" % bk])
                        ss = sit % 4
                        sit += 1
                        row = lo + cc * 128
                        if name in ("q", "k"):
                            sc = (128.0 ** -0.5) if name == "q" else 1.0
                            dst = QT if name == "q" else KT
                            P.op("act", I("activation", out=stb[ss], in_=banks[bk][:], func=AF.Copy, scale=sc),
                                 reads=["ps%d" % bk], writes=["stb%d" % ss])
                            P.op("sp", I("dma_start", out=dst[row:row + 128, t0:t0 + 512], in_=stb[ss]),
                                 reads=["stb%d" % ss], dma="stbst%d" % ss)
                        elif name == "z":
                            P.op("act", I("activation", out=stf[ss], in_=banks[bk][:], func=AF.Silu),
                                 reads=["ps%d" % bk], writes=["stf%d" % ss])
                            P.op("sp", I("dma_start", out=ZT[row:row + 128, t0:t0 + 512], in_=stf[ss]),
                                 reads=["stf%d" % ss], dma="stfst%d" % ss)
                        else:
                            if name == "misc":
                                dst, r0 = MISCs, row
                            else:
                                dst, r0 = RWs, sec_row[name] + row
                            P.op("dve", I("tensor_copy", out=stf[ss][0:cw, :], in_=banks[bk][0:cw, :]),
                                 reads=["ps%d" % bk], writes=["stf%d" % ss])
                            P.op("sp", I("dma_start", out=dst[r0:r0 + cw, t0:t0 + 512], in_=stf[ss][0:cw, :]),
                                 reads=["stf%d" % ss], dma="stfst%d" % ss)
            P.barrier()

        if "C" in cfg.phases:
            reset()
            NQB = NB
            FFt = f32t(S)
            CL = f32t(S)
            onesF = f32t(S)
            CLR = f32t(NQB)
            SEL = f32t(NF * 128).rearrange("p (h m) -> p h m", m=128)
            CT = f32t(NKT * NF).rearrange("p (k h) -> p k h", h=NF)
            CREF = f32t(NF * NQB).rearrange("p (h q) -> p h q", q=NQB)
            KTh = [bf16t(S) for _ in range(2)]
            Vh = [bf16t(S).rearrange("p (k d) -> p k d", d=128) for _ in range(2)]
            Qb = [bf16t(512) for _ in range(2)]
            Zb = [f32t(512) for _ in range(2)]
            Pt = [bf16t(512) for _ in range(3)]
            Bt = [f32t(NKT) for _ in range(2)]
            Rr = f32t(512)
            Ot = f32t(512)
            Hb = [bf16t(512) for _ in range(2)]
            P.op("sp", I("dma_start", out=FFt[0:NF, :], in_=MISCs[192:192 + NF, :]), writes=["FFt"], dma="c1")
            P.op("pool", I("memset", onesF[0:NF, :], 1.0), writes=["onesF"])
            P.op("act", I("activation", out=FFt[0:NF, :], in_=FFt[0:NF, :], func=AF.Exp, bias=nfb[0:NF, :], scale=-1.0),
                 reads=["FFt", "nfb"], writes=["FFt"])
            P.op("act", I("activation", out=FFt[0:NF, :], in_=FFt[0:NF, :], func=AF.Ln, bias=1.0, scale=1.0),
                 reads=["FFt"], writes=["FFt"])
            P.op("dve", I("tensor_tensor_scan", out=CL[0:NF, :], data0=onesF[0:NF, :], data1=FFt[0:NF, :], initial=0.0,
                          op0=ALU.mult, op1=ALU.add), reads=["FFt", "onesF"], writes=["CL"])
            CLv = CL.rearrange("p (q t) -> p q t", t=512)
            P.op("dve", I("tensor_copy", out=CLR[0:NF, :], in_=CLv[0:NF, :, 256]), reads=["CL"], writes=["CLR"])
            P.op("pool", I("memset", SEL[0:NF], 0.0), writes=["SEL"])
            P.op("pool", I("affine_select", out=SEL[0:NF], in_=SEL[0:NF], pattern=[[-1, NF], [0, 128]],
                           compare_op=ALU.not_equal, fill=1.0, base=0, channel_multiplier=1), reads=["SEL"], writes=["SEL"])
            per = 512 // NF
            for k0 in range(0, NKT, per):
                k1 = min(NKT, k0 + per)
                bk = (k0 // per) % 2
                for kt in range(k0, k1):
                    P.op("pe", I("matmul", banks[bk][:, (kt - k0) * NF:(kt - k0 + 1) * NF], lhsT=CL[0:NF, kt * 128:(kt + 1) * 128],
                                 rhs=ident_f[0:NF, 0:NF], start=True, stop=True),
                         reads=["CL", "ident_f"], writes=["ps%d" % bk])
                P.op("dve", I("tensor_copy", out=CT[:, k0:k1, :],
                              in_=banks[bk][:, 0:(k1 - k0) * NF].rearrange("p (k h) -> p k h", h=NF)),
                     reads=["ps%d" % bk], writes=["CT"])
            for h in range(NF):
                bk = 2 + h % 2
                P.op("pe", I("matmul", banks[bk][:, 0:NQB], lhsT=SEL[0:NF, h, :], rhs=CLR[0:NF, :], start=True, stop=True),
                     reads=["SEL", "CLR"], writes=["ps%d" % bk])
                P.op("dve", I("tensor_copy", out=CREF[:, h, :], in_=banks[bk][:, 0:NQB]), reads=["ps%d" % bk], writes=["CREF"])
            Vs_v = Vs.rearrange("(k p) c -> p k c", p=128)
            sidx = 0
            for h in range(NF):
                hs = h % 2
                P.op("sp", I("dma_start", out=KTh[hs], in_=KT[h * 128:(h + 1) * 128, :]), writes=["KTh%d" % hs], dma="kth%d" % hs)
                for (k0, k1) in ksplit(NKT):
                    P.op("sp", I("dma_start", out=Vh[hs][:, k0:k1, :], in_=Vs_v[:, k0:k1, h * 128:(h + 1) * 128]),
                         writes=["Vh%d" % hs], dma="vh%d" % hs)
                for qb in range(NQB):
                    qs = qb % 2
                    q0 = qb * 512
                    P.op("sp", I("dma_start", out=Qb[qs], in_=QT[h * 128:(h + 1) * 128, q0:q0 + 512]), writes=["Qb%d" % qs], dma="qb%d" % qs)
                    P.op("sp", I("dma_start", out=Zb[qs], in_=ZT[h * 128:(h + 1) * 128, q0:q0 + 512]), writes=["Zb%d" % qs], dma="zb%d" % qs)
                    nkt = 4 * qb + 4
                    P.op("dve", I("tensor_scalar", out=Bt[qs][:, 0:nkt], in0=CT[:, 0:nkt, h], scalar1=CREF[:, h, qb:qb + 1],
                                  scalar2=None, op0=ALU.subtract), reads=["CT", "CREF"], writes=["Bt%d" % qs])
                    pso = 2 + qs
                    psr = 4 + qs
                    for kt in range(nkt):
                        j = kt - 4 * qb
                        cs = 0 if j < 0 else j * 128
                        bs = sidx % 2
                        ps_ = sidx % 3
                        sidx += 1
                        P.op("pe", I("matmul", banks[bs][:, cs:512], lhsT=KTh[hs][:, kt * 128:(kt + 1) * 128], rhs=Qb[qs][:, cs:512],
                                     start=True, stop=True), reads=["KTh%d" % hs, "Qb%d" % qs], writes=["ps%d" % bs])
                        P.op("act", I("activation", out=Pt[ps_][:, cs:512], in_=banks[bs][:, cs:512], func=AF.Exp,
                                      bias=Bt[qs][:, kt:kt + 1], scale=1.0),
                             reads=["ps%d" % bs, "Bt%d" % qs], writes=["Pt%d" % ps_])
                        if j >= 0:
                            P.op("pool", I("tensor_tensor", out=Pt[ps_][:, cs:cs + 128], in0=Pt[ps_][:, cs:cs + 128], in1=tri_bf,
                                           op=ALU.mult), reads=["Pt%d" % ps_, "tri_bf"], writes=["Pt%d" % ps_])
                        P.op("pe", I("matmul", banks[pso][:, cs:512], lhsT=Vh[hs][:, kt, :], rhs=Pt[ps_][:, cs:512],
                                     start=(kt == 0), stop=(kt == nkt - 1)),
                             reads=["Vh%d" % hs, "Pt%d" % ps_], writes=["ps%d" % pso])
                        P.op("pe", I("matmul", banks[psr][:, cs:512], lhsT=ones_bf, rhs=Pt[ps_][:, cs:512],
                                     start=(kt == 0), stop=(kt == nkt - 1)),
                             reads=["ones_bf", "Pt%d" % ps_], writes=["ps%d" % psr])
                    P.op("dve", I("reciprocal", out=Rr, in_=banks[psr][:]), reads=["ps%d" % psr], writes=["Rr"])
                    P.op("dve", I("tensor_tensor", out=Ot, in0=banks[pso][:], in1=Rr, op=ALU.mult),
                         reads=["ps%d" % pso, "Rr"], writes=["Ot"])
                    P.op("pool", I("tensor_tensor", out=Hb[qs], in0=Ot, in1=Zb[qs], op=ALU.mult),
                         reads=["Ot", "Zb%d" % qs], writes=["Hb%d" % qs])
                    P.op("sp", I("dma_start", out=HT[h * 128:(h + 1) * 128, q0:q0 + 512], in_=Hb[qs]),
                         reads=["Hb%d" % qs], dma="hb%d" % qs)
            P.barrier()

        if "D" in cfg.phases:
            reset()
            smask = f32t(512)
            P.op("pool", I("memset", smask, 1.0), writes=["smask"])
            P.op("pool", I("memset", smask.rearrange("p (c t) -> p c t", t=64)[:, :, 0:1], 0.0), reads=["smask"], writes=["smask"])
            bdn = ("a", "r", "b", "k", "Kh", "Bh", "v")

            def v3(t, p0=0, p1=128):
                return t[p0:p1, :].rearrange("p (c t) -> p c t", t=64)

            class TS:
                pass

            def mk_tiles(s):
                T = TS()
                T.s = s
                T.raw = {n: f32t(513) for n in ("r", "k", "v", "wd", "ad")}
                T.X = {n: f32t(512) for n in ("r", "k", "v", "wd", "ad")}
                T.tmp = f32t(512)
                T.wup = f32t(128)
                T.aup = f32t(128)
                (T.sg, T.av, T.kk, T.kmod, T.t1, T.bvec, T.Ls, T.Ep, T.Em, T.Eex, T.Eend, T.bon, T.Zr, T.yT, T.t2,
                 T.t3) = [f32t(512) for _ in range(16)]
                T.WC = f32t(8)
                T.hTb = bf16t(512)
                T.B = {n: bf16t(8 * 128).rearrange("p (c t) -> p c t", t=128) for n in bdn}
                T.S0 = f32t(64)
                T.S0b = bf16t(64)
                T.Pk = [bf16t(512) for _ in range(2)]
                T.PkT = [bf16t(512) for _ in range(2)]
                T.AccT = [f32t(512) for _ in range(2)]
                T.AccTb = [bf16t(512) for _ in range(2)]
                T.LakT, T.MrbT, T.MrkT, T.Kh_t4, T.Bh_t4 = [bf16t(512) for _ in range(5)]
                T.V_t4 = bf16t(256)
                T.Xc, T.Uc = [bf16t(64) for _ in range(2)]
                T.Yc4 = f32t(256)
                T.ysq4 = f32t(256)
                T.stat = f32t(20)
                T.Ybd = f32t(8 * 128).rearrange("p (c t) -> p c t", t=128)
                for n in bdn:
                    P.op("pool", I("memset", T.B[n], 0.0), writes=["s%d:bd%s" % (s, n)])
                P.op("pool", I("memset", T.Ybd, 0.0), writes=["s%d:Ybd0" % s, "s%d:Ybd1" % s])
                return T

            shared_keys = {"smask", "ptab", "mtab", "omu", "omm", "omka", "eps_gn", "ident_bf", "ident_f", "ident4", "ones_bd",
                           "m_sl4", "m_su4", "m_ui4"}

            def pair_prog(pr, T, rec):
                s = T.s

                def K_(k):
                    if k in shared_keys:
                        return k
                    if k.startswith("ps"):
                        return "ps%d" % (4 * s + int(k[2:]))
                    return "s%d:%s" % (s, k)

                def OP(eng, fn, reads=(), writes=(), dma=None):
                    rec.append((eng, fn, [K_(k) for k in reads], [K_(k) for k in writes], (dma + str(s)) if dma else None))

                def bank(j):
                    return banks[4 * s + j]

                raw, X, tmp, B = T.raw, T.X, T.tmp, T.B
                sg, av, kk, kmod, t1, bvec, Ls, Ep, Em, Eex, Eend = T.sg, T.av, T.kk, T.kmod, T.t1, T.bvec, T.Ls, T.Ep, T.Em, T.Eex, T.Eend
                bon, Zr, yT, t2, t3, WC, hTb, S0, S0b = T.bon, T.Zr, T.yT, T.t2, T.t3, T.WC, T.hTb, T.S0, T.S0b
                Pk, PkT, AccT, AccTb = T.Pk, T.PkT, T.AccT, T.AccTb
                LakT, MrbT, MrkT, Kh_t4, Bh_t4, V_t4, Xc, Uc, Yc4, ysq4, stat, Ybd = (T.LakT, T.MrbT, T.MrkT, T.Kh_t4, T.Bh_t4, T.V_t4,
                                                                                      T.Xc, T.Uc, T.Yc4, T.ysq4, T.stat, T.Ybd)
                rn = t3
                f0 = pr * 128
                PC = [ptab[:, pr, j:j + 1] for j in range(10)]
                OP("sp", I("dma_start", out=T.wup[0:96, :], in_=w_up_d[:, f0:f0 + 128]), writes=["wup"], dma="wup")
                OP("sp", I("dma_start", out=T.aup[0:96, :], in_=a_up_d[:, f0:f0 + 128]), writes=["aup"], dma="wup")
                OP("pool", I("memset", S0, 0.0), writes=["S0"])
                OP("pool", I("memset", S0b, 0.0), writes=["S0b"])
                for tb in range(NB):
                    t0 = tb * 512
                    srcs = [("r", RWs, f0, 128), ("k", RWs, RW + f0, 128), ("v", RWs, 2 * RW + f0, 128),
                            ("wd", MISCs, 0, 96), ("ad", MISCs, 96, 96)]
                    for n, src, row, nr in srcs:
                        if tb == 0:
                            OP("pool", I("memset", raw[n][0:nr, 0:1], 0.0), writes=["raw" + n])
                            OP("sp", I("dma_start", out=raw[n][0:nr, 1:513], in_=src[row:row + nr, 0:512]), writes=["raw" + n], dma="raw" + n)
                        else:
                            OP("sp", I("dma_start", out=raw[n][0:nr, :], in_=src[row:row + nr, t0 - 1:t0 + 512]), writes=["raw" + n], dma="raw" + n)
                    OP("sp", I("dma_start", out=Zr, in_=ZT[FW + f0:FW + f0 + 128, t0:t0 + 512]), writes=["Zr"], dma="zr")
                    lerp = [("r", PC[0], omu[:, pr, 0:1], 128), ("k", PC[1], omu[:, pr, 1:2], 128), ("v", PC[2], omu[:, pr, 2:3], 128),
                            ("wd", mtab[:, 0:1], omm[:, 0:1], 96), ("ad", mtab[:, 1:2], omm[:, 1:2], 96)]
                    for n, mu_c, omu_c, nr in lerp:
                        OP("dve", I("tensor_scalar", out=tmp[0:nr, :], in0=raw[n][0:nr, 0:512], scalar1=mu_c[0:nr, :], scalar2=None,
                                    op0=ALU.mult), reads=["raw" + n, "ptab", "mtab"], writes=["tmp"])
                        OP("dve", I("scalar_tensor_tensor", out=X[n][0:nr, :], in0=raw[n][0:nr, 1:513], scalar=omu_c[0:nr, :],
                                    in1=tmp[0:nr, :], op0=ALU.mult, op1=ALU.add), reads=["raw" + n, "omu", "omm", "tmp"], writes=["X" + n])
                    OP("act", I("activation", out=X["wd"][0:96, :], in_=X["wd"][0:96, :], func=AF.Tanh), reads=["Xwd"], writes=["Xwd"])
                    OP("pe", I("matmul", bank(0)[:], lhsT=T.wup[0:96, :], rhs=X["wd"][0:96, :], start=True, stop=True),
                       reads=["wup", "Xwd"], writes=["ps0"])
                    OP("act", I("activation", out=sg, in_=bank(0)[:], func=AF.Sigmoid, bias=PC[3], scale=1.0), reads=["ps0", "ptab"], writes=["sg"])
                    OP("pe", I("matmul", bank(1)[:], lhsT=T.aup[0:96, :], rhs=X["ad"][0:96, :], start=True, stop=True),
                       reads=["aup", "Xad"], writes=["ps1"])
                    OP("act", I("activation", out=av, in_=bank(1)[:], func=AF.Sigmoid, bias=PC[4], scale=1.0), reads=["ps1", "ptab"], writes=["av"])
                    OP("pool", I("tensor_scalar", out=kk, in0=X["k"], scalar1=PC[5], scalar2=None, op0=ALU.mult), reads=["Xk", "ptab"], writes=["kk"])
                    OP("pool", I("tensor_tensor", out=t2, in0=kk, in1=kk, op=ALU.mult), reads=["kk"], writes=["t2"])
                    OP("pe", I("matmul", bank(2)[:], lhsT=ones_bd, rhs=t2, start=True, stop=True), reads=["ones_bd", "t2"], writes=["ps2"])
                    OP("dve", I("tensor_scalar", out=rn, in0=bank(2)[:], scalar1=1e-24, scalar2=None, op0=ALU.max), reads=["ps2"], writes=["t3"])
                    OP("act", I("activation", out=rn, in_=rn, func=AF.Sqrt), reads=["t3"], writes=["t3"])
                    OP("dve", I("reciprocal", out=rn, in_=rn), reads=["t3"], writes=["t3"])
                    OP("dve", I("tensor_tensor", out=kk, in0=kk, in1=rn, op=ALU.mult), reads=["kk", "t3"], writes=["kk"])
                    OP("dve", I("tensor_scalar", out=t1, in0=av, scalar1=PC[6], scalar2=omka[:, pr:pr + 1], op0=ALU.mult, op1=ALU.add),
                       reads=["av", "ptab", "omka"], writes=["t1"])
                    OP("dve", I("tensor_tensor", out=kmod, in0=X["k"], in1=t1, op=ALU.mult), reads=["Xk", "t1"], writes=["kmod"])
                    OP("pool", I("tensor_tensor", out=bvec, in0=kk, in1=av, op=ALU.mult), reads=["kk", "av"], writes=["bvec"])
                    OP("dve", I("scalar_tensor_tensor", out=t2, in0=X["r"], scalar=PC[7], in1=kmod, op0=ALU.mult, op1=ALU.mult),
                       reads=["Xr", "ptab", "kmod"], writes=["t2"])
                    OP("pe", I("matmul", bank(3)[:], lhsT=ones_bd, rhs=t2, start=True, stop=True), reads=["ones_bd", "t2"], writes=["ps3"])
                    OP("dve", I("tensor_tensor", out=bon, in0=bank(3)[:], in1=X["v"], op=ALU.mult), reads=["ps3", "Xv"], writes=["bon"])
                    OP("dve", I("tensor_tensor_scan", out=Ls, data0=smask, data1=sg, initial=0.0, op0=ALU.mult, op1=ALU.add),
                       reads=["smask", "sg"], writes=["Ls"])
                    OP("act", I("activation", out=Ep, in_=Ls, func=AF.Exp, scale=-DECAY_C), reads=["Ls"], writes=["Ep"])
                    OP("act", I("activation", out=Em, in_=Ls, func=AF.Exp, scale=DECAY_C), reads=["Ls"], writes=["Em"])
                    OP("pool", I("tensor_tensor", out=t3, in0=Ls, in1=sg, op=ALU.subtract), reads=["Ls", "sg"], writes=["t3"])
                    OP("act", I("activation", out=Eex, in_=t3, func=AF.Exp, scale=-DECAY_C), reads=["t3"], writes=["Eex"])
                    Lsv = v3(Ls)
                    OP("pool", I("tensor_tensor", out=v3(t3), in0=Lsv[:, :, 63:64].to_broadcast([128, 8, 64]), in1=Lsv, op=ALU.subtract),
                       reads=["Ls", "Eex"], writes=["t3"])
                    OP("act", I("activation", out=Eend, in_=t3, func=AF.Exp, scale=-DECAY_C), reads=["t3"], writes=["Eend"])
                    OP("act", I("activation", out=WC, in_=Lsv[:, :, 63], func=AF.Exp, scale=-DECAY_C), reads=["Ls"], writes=["WC"])

                    def bdw(eng, name, in0, in1, rk, neg=False):
                        for hh in range(2):
                            p0, p1 = hh * 64, hh * 64 + 64
                            o_ = B[name][p0:p1, :, p0:p1]
                            if in1 is None:
                                fn = I("tensor_copy", out=o_, in_=v3(in0, p0, p1))
                            elif neg:
                                fn = I("scalar_tensor_tensor", out=o_, in0=v3(in0, p0, p1), scalar=-1.0, in1=v3(in1, p0, p1),
                                       op0=ALU.mult, op1=ALU.mult)
                            else:
                                fn = I("tensor_tensor", out=o_, in0=v3(in0, p0, p1), in1=v3(in1, p0, p1), op=ALU.mult)
                            OP(eng, fn, reads=rk, writes=["bd" + name])
                    bdw("dve", "a", kk, Eex, ["kk", "Eex"], neg=True)
                    bdw("pool", "r", X["r"], Ep, ["Xr", "Ep"])
                    bdw("dve", "b", bvec, Em, ["bvec", "Em"])
                    bdw("pool", "k", kmod, Em, ["kmod", "Em"])
                    bdw("dve", "Kh", kmod, Eend, ["kmod", "Eend"])
                    bdw("pool", "Bh", bvec, Eend, ["bvec", "Eend"])
                    bdw("dve", "v", X["v"], None, ["Xv"])
                    ka, kr_, kb, kk_, kKh, kBh, kv = ["bd" + n for n in bdn]
                    for g in range(2):
                        p3b = bank(3)[:].bitcast(BF16)
                        p2b = bank(2)[:].bitcast(BF16)
                        for i in range(4):
                            c = 4 * g + i
                            OP("pe", I("transpose", out=p3b[:, i * 128:(i + 1) * 128], in_=B["Kh"][:, c, :], identity=ident_bf),
                               reads=[kKh, "ident_bf"], writes=["ps3"])
                            OP("pe", I("transpose", out=p3b[:, 512 + i * 128:512 + (i + 1) * 128], in_=B["Bh"][:, c, :], identity=ident_bf),
                               reads=[kBh, "ident_bf"], writes=["ps3"])
                            OP("pe", I("transpose", out=p2b[:, i * 128:(i + 1) * 128], in_=B["v"][:, c, :], identity=ident_bf),
                               reads=[kv, "ident_bf"], writes=["ps2"])
                        OP("act", I("copy", out=Kh_t4, in_=p3b[:, 0:512]), reads=["ps3"], writes=["Kh_t"])
                        OP("act", I("copy", out=Bh_t4, in_=p3b[:, 512:1024]), reads=["ps3"], writes=["Bh_t"])
                        for hh in range(2):
                            p0, p1 = hh * 64, hh * 64 + 64
                            OP("act", I("copy", out=V_t4[p0:p1, :].rearrange("p (c t) -> p c t", t=64),
                                        in_=p2b[p0:p1, 0:512].rearrange("p (c t) -> p c t", t=128)[:, :, p0:p1]),
                               reads=["ps2"], writes=["V_t"])
                        for i in range(4):
                            c = 4 * g + i
                            aT, bT = B["a"][:, c, :], B["b"][:, c, :]
                            cs = slice(i * 128, (i + 1) * 128)
                            OP("pe", I("matmul", bank(0)[:, cs], lhsT=aT, rhs=bT, start=True, stop=True), reads=[ka, kb], writes=["ps0"])
                            OP("pe", I("matmul", bank(1)[:, cs], lhsT=bT, rhs=aT, start=True, stop=True), reads=[ka, kb], writes=["ps1"])
                        OP("dve", I("tensor_tensor", out=Pk[0], in0=bank(0)[:], in1=m_sl4, op=ALU.mult), reads=["ps0", "m_sl4"], writes=["Pk0"])
                        OP("dve", I("tensor_tensor", out=PkT[0], in0=bank(1)[:], in1=m_su4, op=ALU.mult), reads=["ps1", "m_su4"], writes=["PkT0"])
                        for i in range(4):
                            c = 4 * g + i
                            aT, rT, bT, kT = B["a"][:, c, :], B["r"][:, c, :], B["b"][:, c, :], B["k"][:, c, :]
                            cs = slice(i * 128, (i + 1) * 128)
                            OP("pe", I("matmul", bank(2)[:, cs], lhsT=kT, rhs=aT, start=True, stop=True), reads=[ka, kk_], writes=["ps2"])
                            OP("pe", I("matmul", bank(3)[:, cs], lhsT=bT, rhs=rT, start=True, stop=True), reads=[kr_, kb], writes=["ps3"])
                            OP("pe", I("matmul", bank(0)[:, cs], lhsT=kT, rhs=rT, start=True, stop=True), reads=[kr_, kk_], writes=["ps0"])
                        OP("dve", I("tensor_tensor", out=LakT, in0=bank(2)[:], in1=m_su4, op=ALU.mult), reads=["ps2", "m_su4"], writes=["LakT"])
                        OP("dve", I("tensor_tensor", out=MrbT, in0=bank(3)[:], in1=m_ui4, op=ALU.mult), reads=["ps3", "m_ui4"], writes=["MrbT"])
                        OP("dve", I("tensor_tensor", out=MrkT, in0=bank(0)[:], in1=m_ui4, op=ALU.mult), reads=["ps0", "m_ui4"], writes=["MrkT"])
                        OP("pool", I("tensor_tensor", out=AccT[0], in0=PkT[0], in1=ident4, op=ALU.add), reads=["PkT0", "ident4"], writes=["AccT0"])
                        OP("pool", I("tensor_copy", out=AccTb[0], in_=AccT[0]), reads=["AccT0"], writes=["AccTb0"])
                        for l in range(5):
                            s_, d_ = l % 2, (l + 1) % 2
                            for i in range(4):
                                cs = slice(i * 128, (i + 1) * 128)
                                OP("pe", I("matmul", bank(0)[:, cs], lhsT=PkT[s_][:, cs], rhs=Pk[s_][:, cs], start=True, stop=True),
                                   reads=["Pk%d" % s_, "PkT%d" % s_], writes=["ps0"])
                                if l < 4:
                                    OP("pe", I("matmul", bank(1)[:, cs], lhsT=Pk[s_][:, cs], rhs=PkT[s_][:, cs], start=True, stop=True),
                                       reads=["Pk%d" % s_, "PkT%d" % s_], writes=["ps1"])
                            OP("act", I("copy", out=Pk[d_], in_=bank(0)[:]), reads=["ps0"], writes=["Pk%d" % d_])
                            if l < 4:
                                OP("act", I("copy", out=PkT[d_], in_=bank(1)[:]), reads=["ps1"], writes=["PkT%d" % d_])
                            for i in range(4):
                                cs = slice(i * 128, (i + 1) * 128)
                                OP("pe", I("matmul", bank(2)[:, cs], lhsT=Pk[d_][:, cs], rhs=AccTb[s_][:, cs], start=True, stop=True),
                                   reads=["Pk%d" % d_, "AccTb%d" % s_], writes=["ps2"])
                            OP("dve", I("tensor_tensor", out=AccT[d_], in0=bank(2)[:], in1=AccT[s_], op=ALU.add),
                               reads=["ps2", "AccT%d" % s_], writes=["AccT%d" % d_])
                            OP("pool", I("tensor_copy", out=AccTb[d_], in_=AccT[d_]), reads=["AccT%d" % d_], writes=["AccTb%d" % d_])
                        for i in range(4):
                            c = 4 * g + i
                            aT, rT = B["a"][:, c, :], B["r"][:, c, :]
                            cs = slice(i * 128, (i + 1) * 128)
                            vs = slice(i * 64, (i + 1) * 64)
                            xs = slice(i * 128, i * 128 + 64)
                            us = slice(i * 128 + 64, i * 128 + 128)
                            OP("pe", I("matmul", bank(0)[:, xs], lhsT=aT, rhs=S0b, start=True, stop=False), reads=[ka, "S0b"], writes=["ps0"])
                            OP("pe", I("matmul", bank(0)[:, xs], lhsT=LakT[:, cs], rhs=V_t4[:, vs], start=False, stop=True),
                               reads=["LakT", "V_t"], writes=["ps0"])
                            OP("act", I("copy", out=Xc, in_=bank(0)[:, xs]), reads=["ps0"], writes=["Xc"])
                            OP("pe", I("matmul", bank(0)[:, us], lhsT=AccTb[1][:, cs], rhs=Xc, start=True, stop=True),
                               reads=["AccTb1", "Xc"], writes=["ps0"])
                            OP("act", I("copy", out=Uc, in_=bank(0)[:, us]), reads=["ps0"], writes=["Uc"])
                            OP("pe", I("matmul", bank(1)[:, vs], lhsT=rT, rhs=S0b, start=True, stop=False), reads=[kr_, "S0b"], writes=["ps1"])
                            OP("pe", I("matmul", bank(1)[:, vs], lhsT=MrbT[:, cs], rhs=Uc, start=False, stop=False), reads=["MrbT", "Uc"], writes=["ps1"])
                            OP("pe", I("matmul", bank(1)[:, vs], lhsT=MrkT[:, cs], rhs=V_t4[:, vs], start=False, stop=True),
                               reads=["MrkT", "V_t"], writes=["ps1"])
                            OP("pe", I("matmul", bank(2)[:, 0:64], lhsT=Bh_t4[:, cs], rhs=Uc, start=True, stop=False), reads=["Bh_t", "Uc"], writes=["ps2"])
                            OP("pe", I("matmul", bank(2)[:, 0:64], lhsT=Kh_t4[:, cs], rhs=V_t4[:, vs], start=False, stop=True),
                               reads=["Kh_t", "V_t"], writes=["ps2"])
                            OP("dve", I("scalar_tensor_tensor", out=S0, in0=S0, scalar=WC[:, c:c + 1], in1=bank(2)[:, 0:64],
                                        op0=ALU.mult, op1=ALU.add), reads=["S0", "WC", "ps2"], writes=["S0"])
                            OP("pool", I("tensor_copy", out=S0b, in_=S0), reads=["S0"], writes=["S0b"])
                        Yc3 = Yc4.rearrange("p (c t) -> p c t", t=64)
                        OP("dve", I("tensor_copy", out=Yc4, in_=bank(1)[:, 0:256]), reads=["ps1"], writes=["Yc"])
                        OP("dve", I("tensor_reduce", out=stat[:, 0:4], in_=Yc3, axis=AX.X, op=ALU.add), reads=["Yc"], writes=["stat0"])
                        OP("dve", I("tensor_scalar", out=stat[:, 4:8], in0=stat[:, 0:4], scalar1=1.0 / 64.0, scalar2=None, op0=ALU.mult),
                           reads=["stat0"], writes=["stat1"])
                        OP("dve", I("tensor_tensor", out=Yc3, in0=Yc3, in1=stat[:, 4:8].rearrange("p (c o) -> p c o", o=1).to_broadcast([128, 4, 64]),
                                    op=ALU.subtract), reads=["Yc", "stat1"], writes=["Yc"])
                        OP("pool", I("tensor_tensor", out=ysq4, in0=Yc4, in1=Yc4, op=ALU.mult), reads=["Yc"], writes=["ysq"])
                        OP("dve", I("tensor_reduce", out=stat[:, 8:12], in_=ysq4.rearrange("p (c t) -> p c t", t=64), axis=AX.X, op=ALU.add),
                           reads=["ysq"], writes=["stat2"])
                        OP("act", I("activation", out=stat[:, 12:16], in_=stat[:, 8:12], func=AF.Sqrt, bias=eps_gn[:, 0:1], scale=1.0 / 64.0),
                           reads=["stat2", "eps_gn"], writes=["stat3"])
                        OP("dve", I("reciprocal", out=stat[:, 16:20], in_=stat[:, 12:16]), reads=["stat3"], writes=["stat4"])
                        for hh in range(2):
                            p0, p1 = hh * 64, hh * 64 + 64
                            OP("dve", I("tensor_tensor", out=Ybd[p0:p1, 4 * g:4 * g + 4, p0:p1], in0=Yc3[p0:p1],
                                        in1=stat[p0:p1, 16:20].rearrange("p (c o) -> p c o", o=1).to_broadcast([64, 4, 64]), op=ALU.mult),
                               reads=["Yc", "stat4"], writes=["Ybd%d" % g])
                    for c in range(8):
                        bj = 2 + (c % 2)
                        OP("pe", I("transpose", out=bank(bj)[:, 0:128], in_=Ybd[:, c, :], identity=ident_f),
                           reads=["Ybd%d" % (c // 4), "ident_f"], writes=["ps%d" % bj])
                        for hh in range(2):
                            p0, p1 = hh * 64, hh * 64 + 64
                            OP("act", I("copy", out=yT[p0:p1, c * 64:(c + 1) * 64], in_=bank(bj)[p0:p1, p0:p1]), reads=["ps%d" % bj], writes=["yT"])
                    OP("dve", I("tensor_scalar", out=yT, in0=yT, scalar1=PC[8], scalar2=PC[9], op0=ALU.mult, op1=ALU.add),
                       reads=["yT", "ptab"], writes=["yT"])
                    OP("dve", I("tensor_tensor", out=yT, in0=yT, in1=bon, op=ALU.add), reads=["yT", "bon"], writes=["yT"])
                    OP("pool", I("tensor_tensor", out=hTb, in0=yT, in1=Zr, op=ALU.mult), reads=["yT", "Zr"], writes=["hTb"])
                    OP("sp", I("dma_start", out=HT[FW + f0:FW + f0 + 128, t0:t0 + 512], in_=hTb), reads=["hTb"], dma="hTb")

            NSTR = 2 if NP >= 2 else 1
            tiles = [mk_tiles(s) for s in range(NSTR)]
            for p0_ in range(0, NP, NSTR):
                recs = []
                for s in range(NSTR):
                    if p0_ + s < NP:
                        r_ = []
                        pair_prog(p0_ + s, tiles[s], r_)
                        recs.append(r_)
                n = max(len(r_) for r_ in recs)
                for i in range(n):
                    for r_ in recs:
                        if i < len(r_):
                            eng, fn, rd, wr, dm = r_[i]
                            P.op(eng, fn, reads=rd, writes=wr, dma=dm)
            P.barrier()

        if "E" in cfg.phases:
            reset()
            TB = 256
            WKC = W // 128
            hT = bf16t(WKC * TB).rearrange("p (k t) -> p k t", t=TB)
            wo = [bf16t(WKC * 512).rearrange("p (k c) -> p k c", c=512) for _ in range(2)]
            acc = f32t(2 * D).rearrange("p (a d) -> p a d", a=2)
            xt = f32t(D)
            gain = f32t(D)
            beta = f32t(D)
            sq = f32t(D)
            st2 = f32t(8)
            P.op("sp", I("dma_start", out=gain, in_=ln_g_d.partition_broadcast(128)), writes=["gain"], dma="c2")
            P.op("sp", I("dma_start", out=beta, in_=ln_b_d.partition_broadcast(128)), writes=["beta"], dma="c2")
            HT_v = HT.rearrange("(k p) t -> p k t", p=128)
            wo_v = w_out_bf.rearrange("(k p) c -> p k c", p=128)
            NG = (D + 511) // 512
            wit = 0
            pit = 0
            for tb in range(S // TB):
                t0 = tb * TB
                for (k0, k1) in ksplit(WKC):
                    P.op("sp", I("dma_start", out=hT[:, k0:k1, :], in_=HT_v[:, k0:k1, t0:t0 + TB]), writes=["hT"], dma="hTld")
                for ng in range(NG):
                    n0 = ng * 512
                    nw = min(512, D - n0)
                    s = wit % 2
                    wit += 1
                    for (k0, k1) in ksplit(WKC):
                        P.op("sp", I("dma_start", out=wo[s][:, k0:k1, 0:nw], in_=wo_v[:, k0:k1, n0:n0 + nw]),
                             writes=["wo%d" % s], dma="wold%d" % s)
                    for tt in range(TB // 128):
                        bk = pit % 4
                        pit += 1
                        for kc in range(WKC):
                            P.op("pe", I("matmul", banks[bk][:, 0:nw], lhsT=hT[:, kc, tt * 128:(tt + 1) * 128], rhs=wo[s][:, kc, 0:nw],
                                         start=(kc == 0), stop=(kc == WKC - 1)),
                                 reads=["hT", "wo%d" % s], writes=["ps%d" % bk])
                        P.op("dve", I("tensor_copy", out=acc[:, tt, n0:n0 + nw], in_=banks[bk][:, 0:nw]),
                             reads=["ps%d" % bk], writes=["acc%d" % tt])
                for tt in range(TB // 128):
                    r0 = t0 + tt * 128
                    P.op("sp", I("dma_start", out=xt, in_=x[r0:r0 + 128, :]), writes=["xt"], dma="xtld")
                    P.op("dve", I("scalar_tensor_tensor", out=acc[:, tt, :], in0=xt, scalar=cfg.alpha, in1=acc[:, tt, :],
                                  op0=ALU.mult, op1=ALU.add), reads=["xt", "acc%d" % tt], writes=["acc%d" % tt])
                    P.op("act", I("activation", out=sq, in_=acc[:, tt, :], func=AF.Copy, accum_out=st2[:, 0:1]),
                         reads=["acc%d" % tt], writes=["sq", "st20"])
                    P.op("dve", I("tensor_scalar", out=st2[:, 1:2], in0=st2[:, 0:1], scalar1=-1.0 / D, scalar2=None, op0=ALU.mult),
                         reads=["st20"], writes=["st21"])
                    P.op("act", I("activation", out=sq, in_=acc[:, tt, :], func=AF.Square, bias=st2[:, 1:2], scale=1.0,
                                  accum_out=st2[:, 2:3]), reads=["acc%d" % tt, "st21"], writes=["sq", "st22"])
                    P.op("act", I("activation", out=st2[:, 3:4], in_=st2[:, 2:3], func=AF.Sqrt, bias=eps_ln[:, 0:1], scale=1.0 / D),
                         reads=["st22"], writes=["st23"])
                    P.op("dve", I("reciprocal", out=st2[:, 4:5], in_=st2[:, 3:4]), reads=["st23"], writes=["st24"])
                    P.op("dve", I("tensor_scalar", out=sq, in0=acc[:, tt, :], scalar1=st2[:, 1:2], scalar2=st2[:, 4:5],
                                  op0=ALU.add, op1=ALU.mult), reads=["acc%d" % tt, "st21", "st24"], writes=["sq"])
                    P.op("pool", I("tensor_tensor", out=sq, in0=sq, in1=gain, op=ALU.mult), reads=["sq", "gain"], writes=["sq"])
                    P.op("pool", I("tensor_tensor", out=sq, in0=sq, in1=beta, op=ALU.add), reads=["sq", "beta"], writes=["sq"])
                    P.op("sp", I("dma_start", out=y_out[r0:r0 + 128, :], in_=sq), reads=["sq"], dma="yst")
        P.emit()
    return nc


_CACHE = {}


def kernel(**inputs):
    cfg = Cfg()
    lay = host_layout(cfg, inputs)
    xfull = np.asarray(inputs["x"], np.float32)
    if "nc" not in _CACHE:
        _CACHE["nc"] = build(cfg)
    nc = _CACHE["nc"]
    in_maps = []
    for c in range(8):
        m = dict(lay)
        m["x"] = np.ascontiguousarray(xfull[c // 4])
        in_maps.append(m)
    res = run_bass_kernel_spmd(nc, in_maps, core_ids=list(range(8)))
    out = np.empty((2, cfg.S, cfg.D), np.float32)
    q = cfg.S // 4
    for c in range(8):
        b, g = c // 4, c % 4
        out[b, g * q:(g + 1) * q] = res.results[c]["y"][g * q:(g + 1) * q]
    return out
```

```python
import contextlib
import numpy as np
import concourse.bass as bass
import concourse.mybir as mybir
from concourse.bass_utils import run_bass_kernel_spmd

F32 = mybir.dt.float32
BF16 = mybir.dt.bfloat16
AF = mybir.ActivationFunctionType
ALU = mybir.AluOpType
AX = mybir.AxisListType

ENGS = ["sp", "pe", "act", "dve", "pool"]
DECAY_C = float(np.exp(-0.5))


class Prog:
    def __init__(self, nc):
        self.nc = nc
        self.ops = {e: [] for e in ENGS}
        self.cnt = {e: 0 for e in ENGS}
        self.dma_cnt = {}
        self.last_w = {}
        self.readers = {}
        self.waited = {e: {} for e in ENGS}
        self.pending_barrier = {e: [] for e in ENGS}

    def _need(self, engine, tok, kind):
        semkey, val, src = tok
        if semkey.startswith("D:"):
            val = self.dma_cnt[semkey]
        elif src == engine:
            if engine == "pe":
                return None
            if kind != "RAW":
                return None
        if self.waited[engine].get(semkey, -1) >= val:
            return None
        self.waited[engine][semkey] = val
        return (semkey, val)

    def op(self, engine, fn, reads=(), writes=(), dma=None):
        cand = {}

        def add(tok, kind):
            semkey, val, src = tok
            if semkey.startswith("D:"):
                val = self.dma_cnt[semkey]
            elif src == engine:
                if engine == "pe" or kind != "RAW":
                    return
            if val > cand.get(semkey, -1):
                cand[semkey] = val

        for tok in self.pending_barrier[engine]:
            add(tok, "RAW")
        self.pending_barrier[engine] = []
        for k in reads:
            t = self.last_w.get(k)
            if t is not None:
                add(t, "RAW")
        for k in writes:
            t = self.last_w.get(k)
            if t is not None:
                add(t, "WAW")
            for t in self.readers.get(k, ()):
                add(t, "WAR")
        waits = []
        for semkey, val in cand.items():
            if self.waited[engine].get(semkey, -1) >= val:
                continue
            self.waited[engine][semkey] = val
            waits.append((semkey, val))
        if dma is not None:
            semkey = "D:" + dma
            self.dma_cnt[semkey] = self.dma_cnt.get(semkey, 0) + 16
            tok = (semkey, self.dma_cnt[semkey], None)
            inc = (semkey, 16)
        else:
            self.cnt[engine] += 1
            semkey = "E:" + engine
            tok = (semkey, self.cnt[engine], engine)
            inc = (semkey, 1)
        for k in writes:
            self.last_w[k] = tok
            self.readers[k] = []
        for k in reads:
            if k not in writes:
                self.readers.setdefault(k, []).append(tok)
        self.ops[engine].append((fn, waits, inc))
        return tok

    def barrier(self):
        toks = []
        for e in ENGS:
            if self.cnt[e] > 0:
                toks.append(("E:" + e, self.cnt[e], "__none__"))
        for semkey, v in self.dma_cnt.items():
            toks.append((semkey, v, None))
        for e in ENGS:
            self.pending_barrier[e] = list(toks)
        self.last_w = {}
        self.readers = {}

    def emit(self):
        nc = self.nc
        self.barrier()
        fin = []
        for tok in self.pending_barrier["sp"]:
            w = self._need("sp", tok, "RAW")
            if w:
                fin.append(w)
        semkeys = ["E:" + e for e in ENGS if self.cnt[e] > 0] + list(self.dma_cnt.keys())
        with contextlib.ExitStack() as st:
            sems = {}
            for i, k in enumerate(semkeys):
                sems[k] = st.enter_context(nc.semaphore("s%d" % i))
            block = st.enter_context(nc.Block())

            def run(engname, eng):
                for fn, waits, inc in self.ops[engname]:
                    fuse = bool(waits) and engname in ("act", "dve", "pool") and getattr(fn, "fusable", False)
                    for (sk, v) in (waits[1:] if fuse else waits):
                        eng.wait_ge(sems[sk], v)
                    ins = fn(eng)
                    if fuse:
                        ins._wait_ge(sems[waits[0][0]], waits[0][1])
                    ins.then_inc(sems[inc[0]], inc[1])
                if engname == "sp":
                    for (sk, v) in fin:
                        eng.wait_ge(sems[sk], v)

            @block.sync
            def _(e):
                run("sp", e)

            @block.tensor
            def _(e):
                run("pe", e)

            @block.scalar
            def _(e):
                run("act", e)

            @block.vector
            def _(e):
                run("dve", e)

            @block.gpsimd
            def _(e):
                run("pool", e)


class Cfg:
    def __init__(self, D=4096, S=8192, NF=16, NP=16, dbg=False, phases="ABCDE"):
        self.D, self.S, self.NF, self.NP = D, S, NF, NP
        self.FW = NF * 128
        self.RW = NP * 128
        self.W = self.FW + self.RW
        self.KC = D // 128
        self.NB = S // 512
        self.NKT = S // 128
        self.NCH = S // 64
        self.MISC = 256
        self.secs = [("q", self.FW), ("k", self.FW), ("v", self.FW), ("r", self.RW), ("kr", self.RW),
                     ("vr", self.RW), ("z", self.W), ("misc", self.MISC)]
        self.NC = sum(n for _, n in self.secs)
        self.dbg = dbg
        self.phases = phases
        self.alpha = 2.0 ** 0.25
        self.ln_eps = 1e-5
        self.gn_eps = 64e-5


def host_layout(cfg, inp):
    FW, RW, NF = cfg.FW, cfg.RW, cfg.NF
    w_in = np.asarray(inp["w_in"], np.float32)
    o = 0
    q = w_in[:, o:o + FW]; o += FW
    k = w_in[:, o:o + FW]; o += FW
    v = w_in[:, o:o + FW]; o += FW
    f = w_in[:, o:o + NF]; o += NF
    r = w_in[:, o:o + RW]; o += RW
    kr = w_in[:, o:o + RW]; o += RW
    vr = w_in[:, o:o + RW]; o += RW
    wd = w_in[:, o:o + 96]; o += 96
    ad = w_in[:, o:o + 96]; o += 96
    z = w_in[:, o:o + cfg.W]; o += cfg.W
    assert o == w_in.shape[1]
    misc = np.zeros((cfg.D, cfg.MISC), np.float32)
    misc[:, 0:96] = wd
    misc[:, 96:192] = ad
    misc[:, 192:192 + NF] = f
    w_perm = np.ascontiguousarray(np.concatenate([q, k, v, r, kr, vr, z, misc], axis=1))
    mu = np.asarray(inp["mu_shift"], np.float32)
    def pc(vec):
        return np.ascontiguousarray(np.asarray(vec, np.float32).reshape(cfg.NP, 128).T)
    ptab = np.stack([pc(mu[0:RW]), pc(mu[RW:2 * RW]), pc(mu[2 * RW:3 * RW]), pc(inp["w0"]), pc(inp["a0"]),
                     pc(inp["k_k"]), pc(inp["k_a"]), pc(np.asarray(inp["r_k"]).reshape(-1)),
                     pc(inp["gn_gain"]), pc(inp["gn_bias"])], axis=2)
    mtab = np.zeros((128, 2), np.float32)
    mtab[0:96, 0] = mu[3 * RW:3 * RW + 96]
    mtab[0:96, 1] = mu[3 * RW + 96:3 * RW + 192]
    fb = np.zeros((128, 1), np.float32)
    fb[0:NF, 0] = np.asarray(inp["f_bias"], np.float32)
    return {
        "w_in_p": w_perm,
        "w_out": np.ascontiguousarray(np.asarray(inp["w_out"], np.float32)),
        "ptab": np.ascontiguousarray(ptab),
        "mtab": mtab,
        "fb": fb,
        "w_up": np.ascontiguousarray(np.asarray(inp["w_up"], np.float32)),
        "a_up": np.ascontiguousarray(np.asarray(inp["a_up"], np.float32)),
        "ln_g": np.ascontiguousarray(np.asarray(inp["ln_gain"], np.float32).reshape(1, -1)),
        "ln_b": np.ascontiguousarray(np.asarray(inp["ln_bias"], np.float32).reshape(1, -1)),
    }


_FUSABLE = ("tensor_tensor", "tensor_copy", "tensor_scalar", "scalar_tensor_tensor", "copy", "reciprocal",
            "tensor_tensor_scan", "activation")


def I(method, *args, **kw):
    f = lambda e: getattr(e, method)(*args, **kw)
    f.fusable = method in _FUSABLE and "accum_out" not in kw
    return f


def build(cfg):
    D, S, NF, NP, KC, NB, NKT = cfg.D, cfg.S, cfg.NF, cfg.NP, cfg.KC, cfg.NB, cfg.NKT
    FW, RW, W, NC = cfg.FW, cfg.RW, cfg.W, cfg.NC
    nc = bass.Bass("TRN2", target_bir_lowering=False)

    def din(name, shape, dt=F32):
        return nc.dram_tensor(name, shape, dt, kind="ExternalInput").ap()

    x = din("x", [S, D])
    w_in_p = din("w_in_p", [D, NC])
    w_out = din("w_out", [W, D])
    ptab_d = din("ptab", [128, NP, 10])
    mtab_d = din("mtab", [128, 2])
    fb_d = din("fb", [128, 1])
    w_up_d = din("w_up", [96, RW])
    a_up_d = din("a_up", [96, RW])
    ln_g_d = din("ln_g", [1, D])
    ln_b_d = din("ln_b", [1, D])
    y_out = nc.dram_tensor("y", [S, D], F32, kind="ExternalOutput").ap()

    def scr(name, shape, dt):
        kind = "ExternalOutput" if (cfg.dbg and name in cfg.dbg) else "Internal"
        return nc.dram_tensor(name, shape, dt, kind=kind).ap()

    w_in_bf = scr("w_in_bf", [D, NC], BF16)
    w_out_bf = scr("w_out_bf", [W, D], BF16)
    QT = scr("QT", [FW, S], BF16)
    KT = scr("KT", [FW, S], BF16)
    Vs = scr("Vs", [S, FW], BF16)
    RWs = scr("RWs", [3 * RW, S], F32)
    MISCs = scr("MISCs", [256, S], F32)
    ZT = scr("ZT", [W, S], F32)
    HT = scr("HT", [W, S], BF16)

    P = Prog(nc)
    with contextlib.ExitStack() as st:
        ARENA_N = 52600
        arena = st.enter_context(nc.sbuf_tensor("arena", [128, ARENA_N], F32))
        banks = [st.enter_context(nc.psum_tensor("bank%d" % i, [128, 512], F32)) for i in range(8)]
        apos = [0]
        atop = [ARENA_N]

        def reset():
            apos[0] = 0

        def f32t(n):
            a = apos[0]
            apos[0] += n
            assert apos[0] <= atop[0], (apos[0], atop[0])
            return arena[:, a:a + n]

        def bf16t(n):
            n2 = (n + 1) // 2
            return f32t(n2).bitcast(BF16)[:, 0:n]

        def const_f32(n):
            atop[0] -= n
            return arena[:, atop[0]:atop[0] + n]

        ident_bf = const_f32(64).bitcast(BF16)
        ident_f = const_f32(128)
        ones_bd = const_f32(128)
        ones_bf = const_f32(64).bitcast(BF16)
        m_su = const_f32(128)
        m_sl = const_f32(128)
        m_ui = const_f32(128)
        tri_bf = const_f32(64).bitcast(BF16)
        tmpf = const_f32(128)
        m_su4 = const_f32(512)
        m_sl4 = const_f32(512)
        m_ui4 = const_f32(512)
        ident4 = const_f32(512)
        ptab = const_f32(NP * 10).rearrange("p (a b) -> p a b", b=10)
        omu = const_f32(NP * 3).rearrange("p (a b) -> p a b", b=3)
        omka = const_f32(NP)
        mtab = const_f32(2)
        omm = const_f32(2)
        fbc = const_f32(1)
        nfb = const_f32(1)
        eps_gn = const_f32(1)
        eps_ln = const_f32(1)

        P.op("pool", I("memset", ident_f, 0.0), writes=["ident_f"])
        P.op("pool", I("affine_select", out=ident_f, in_=ident_f, pattern=[[-1, 128]], compare_op=ALU.not_equal,
                       fill=1.0, base=0, channel_multiplier=1), reads=["ident_f"], writes=["ident_f"])
        P.op("pool", I("tensor_copy", out=ident_bf, in_=ident_f), reads=["ident_f"], writes=["ident_bf"])
        P.op("pool", I("memset", ones_bf, 1.0), writes=["ones_bf"])

        def tri_mask(t, key, chan_mult, step, cmp, bd=True):
            P.op("pool", I("memset", t, 1.0), writes=[key])
            P.op("pool", I("affine_select", out=t, in_=t, pattern=[[step, 128]], compare_op=cmp, fill=0.0,
                           base=0, channel_multiplier=chan_mult), reads=[key], writes=[key])
            if bd:
                P.op("pool", I("memset", t[0:64, 64:128], 0.0), reads=[key], writes=[key])
                P.op("pool", I("memset", t[64:128, 0:64], 0.0), reads=[key], writes=[key])
        tri_mask(m_su, "m_su", -1, 1, ALU.is_gt)
        tri_mask(m_sl, "m_sl", 1, -1, ALU.is_gt)
        tri_mask(m_ui, "m_ui", -1, 1, ALU.is_ge)
        tri_mask(tmpf, "tmpf", -1, 1, ALU.is_ge, bd=False)
        P.op("pool", I("tensor_copy", out=tri_bf, in_=tmpf), reads=["tmpf"], writes=["tri_bf"])
        for (m1, m4, k1, k4) in ((m_su, m_su4, "m_su", "m_su4"), (m_sl, m_sl4, "m_sl", "m_sl4"), (m_ui, m_ui4, "m_ui", "m_ui4"),
                                 (ident_f, ident4, "ident_f", "ident4")):
            for i in range(4):
                P.op("pool", I("tensor_copy", out=m4[:, i * 128:(i + 1) * 128], in_=m1), reads=[k1], writes=[k4])
        P.op("pool", I("memset", ones_bd, 1.0), writes=["ones_bd"])
        P.op("pool", I("memset", ones_bd[0:64, 64:128], 0.0), reads=["ones_bd"], writes=["ones_bd"])
        P.op("pool", I("memset", ones_bd[64:128, 0:64], 0.0), reads=["ones_bd"], writes=["ones_bd"])
        P.op("sp", I("dma_start", out=ptab, in_=ptab_d), writes=["ptab"], dma="c0")
        P.op("sp", I("dma_start", out=mtab, in_=mtab_d), writes=["mtab"], dma="c0")
        P.op("sp", I("dma_start", out=fbc, in_=fb_d), writes=["fbc"], dma="c0")
        P.op("dve", I("tensor_scalar", out=omu, in0=ptab[:, :, 0:3], scalar1=-1.0, scalar2=1.0, op0=ALU.mult, op1=ALU.add),
             reads=["ptab"], writes=["omu"])
        P.op("dve", I("tensor_scalar", out=omka, in0=ptab[:, :, 6], scalar1=-1.0, scalar2=1.0, op0=ALU.mult, op1=ALU.add),
             reads=["ptab"], writes=["omka"])
        P.op("dve", I("tensor_scalar", out=omm, in0=mtab, scalar1=-1.0, scalar2=1.0, op0=ALU.mult, op1=ALU.add),
             reads=["mtab"], writes=["omm"])
        P.op("dve", I("tensor_scalar", out=nfb, in0=fbc, scalar1=-1.0, scalar2=None, op0=ALU.mult),
             reads=["fbc"], writes=["nfb"])
        P.op("dve", I("memset", eps_gn, cfg.gn_eps), writes=["eps_gn"])
        P.op("dve", I("memset", eps_ln, cfg.ln_eps), writes=["eps_ln"])
        P.barrier()

        def castload(dst_bf, src_f32, key, dma):
            n = src_f32.shape[-1]
            for c0 in range(0, n, 2048):
                c1 = min(n, c0 + 2048)
                P.op("pool", I("dma_start", out=dst_bf[:, c0:c1], in_=src_f32[:, c0:c1]), writes=[key], dma=dma)

        def ksplit(n, step=8):
            return [(k0, min(n, k0 + step)) for k0 in range(0, n, step)]

        if "A" in cfg.phases:
            reset()
            wb = [bf16t(max(NC, D)) for _ in range(2)]
            it = 0
            for (src, dst, nrows, ncols) in ((w_in_p, w_in_bf, D, NC), (w_out, w_out_bf, W, D)):
                for rc in range(nrows // 128):
                    s = it % 2
                    it += 1
                    castload(wb[s][:, 0:ncols], src[rc * 128:(rc + 1) * 128, :], "wb%d" % s, "wbld%d" % s)
                    P.op("sp", I("dma_start", out=dst[rc * 128:(rc + 1) * 128, :], in_=wb[s][:, 0:ncols]),
                         reads=["wb%d" % s], dma="wbst%d" % s)
            P.barrier()

        if "B" in cfg.phases:
            reset()
            xb = bf16t(4 * D).rearrange("p (a d) -> p a d", a=4)
            xT = bf16t(KC * 512).rearrange("p (k t) -> p k t", t=512)
            wg = [bf16t(KC * 512).rearrange("p (k c) -> p k c", c=512) for _ in range(2)]
            stf = [f32t(512) for _ in range(4)]
            stb = [bf16t(512) for _ in range(4)]
            groups = []
            o = 0
            for name, n in cfg.secs:
                lo = 0
                while lo < n:
                    gw = min(512, n - lo)
                    groups.append((name, o + lo, gw, lo))
                    lo += gw
                o += n
            w_in_v = w_in_bf.rearrange("(k p) c -> p k c", p=128)
            git = 0
            pit = 0
            sit = 0
            sec_row = {"r": 0, "kr": RW, "vr": 2 * RW}
            for tb in range(NB):
                t0 = tb * 512
                for tt in range(4):
                    castload(xb[:, tt, :], x[t0 + tt * 128:t0 + (tt + 1) * 128, :], "xb", "xbld")
                for kc in range(KC):
                    bk = 4 + (kc % 2)
                    psb = banks[bk][:].bitcast(BF16)
                    for tt in range(4):
                        P.op("pe", I("transpose", out=psb[:, tt * 128:(tt + 1) * 128], in_=xb[:, tt, kc * 128:(kc + 1) * 128],
                                     identity=ident_bf), reads=["xb", "ident_bf"], writes=["ps%d" % bk])
                    if kc % 2 == 0:
                        P.op("act", I("copy", out=xT[:, kc, :], in_=psb[:, 0:512]), reads=["ps%d" % bk], writes=["xT%d" % kc])
                    else:
                        P.op("dve", I("tensor_copy", out=xT[:, kc, :], in_=psb[:, 0:512]), reads=["ps%d" % bk], writes=["xT%d" % kc])
                xTk = ["xT%d" % kc for kc in range(KC)]
                for (name, c0, gw, lo) in groups:
                    s = git % 2
                    git += 1
                    for (k0, k1) in ksplit(KC):
                        P.op("sp", I("dma_start", out=wg[s][:, k0:k1, 0:gw], in_=w_in_v[:, k0:k1, c0:c0 + gw]),
                             writes=["wg%d" % s], dma="wgld%d" % s)
                    if name == "v":
                        for tt in range(4):
                            bk = pit % 4
                            pit += 1
                            for kc in range(KC):
                                P.op("pe", I("matmul", banks[bk][:, 0:gw], lhsT=xT[:, kc, tt * 128:(tt + 1) * 128],
                                             rhs=wg[s][:, kc, 0:gw], start=(kc == 0), stop=(kc == KC - 1)),
                                     reads=[xTk[kc], "wg%d" % s], writes=["ps%d" % bk])
                            ss = sit % 4
                            sit += 1
                            P.op("act", I("copy", out=stb[ss][:, 0:gw], in_=banks[bk][:, 0:gw]),
                                 reads=["ps%d" % bk], writes=["stb%d" % ss])
                            P.op("sp", I("dma_start", out=Vs[t0 + tt * 128:t0 + (tt + 1) * 128, lo:lo + gw], in_=stb[ss][:, 0:gw]),
                                 reads=["stb%d" % ss], dma="stbst%d" % ss)
                        continue
                    for cc in range((gw + 127) // 128):
                        cw = min(128, gw - cc * 128)
                        bk = pit % 4
                        pit += 1
                        for kc in range(KC):
                            P.op("pe", I("matmul", banks[bk][0:cw, :], lhsT=wg[s][:, kc, cc * 128:cc * 128 + cw], rhs=xT[:, kc, :],
                                         start=(kc == 0), stop=(kc == KC - 1)),
                                 reads=[xTk[kc], "wg%d" % s], writes=["ps%d" % bk])
                        ss = sit % 4
                        sit += 1
                        row = lo + cc * 128
                        if name in ("q", "k"):
                            sc = (128.0 ** -0.5) if name == "q" else 1.0
                            dst = QT if name == "q" else KT
                            P.op("act", I("activation", out=stb[ss], in_=banks[bk][:], func=AF.Copy, scale=sc),
                                 reads=["ps%d" % bk], writes=["stb%d" % ss])
                            P.op("sp", I("dma_start", out=dst[row:row + 128, t0:t0 + 512], in_=stb[ss]),
                                 reads=["stb%d" % ss], dma="stbst%d" % ss)
                        elif name == "z":
                            P.op("act", I("activation", out=stf[ss], in_=banks[bk][:], func=AF.Silu),
                                 reads=["ps%d" % bk], writes=["stf%d" % ss])
                            P.op("sp", I("dma_start", out=ZT[row:row + 128, t0:t0 + 512], in_=stf[ss]),
                                 reads=["stf%d" % ss], dma="stfst%d" % ss)
                        else:
                            if name == "misc":
                                dst, r0 = MISCs, row
                            else:
                                dst, r0 = RWs, sec_row[name] + row
                            P.op("dve", I("tensor_copy", out=stf[ss][0:cw, :], in_=banks[bk][0:cw, :]),
                                 reads=["ps%d" % bk], writes=["stf%d" % ss])
                            P.op("sp", I("dma_start", out=dst[r0:r0 + cw, t0:t0 + 512], in_=stf[ss][0:cw, :]),
                                 reads=["stf%d" % ss], dma="stfst%d" % ss)
            P.barrier()

        if "C" in cfg.phases:
            reset()
            NQB = NB
            FFt = f32t(S)
            CL = f32t(S)
            onesF = f32t(S)
            CLR = f32t(NQB)
            SEL = f32t(NF * 128).rearrange("p (h m) -> p h m", m=128)
            CT = f32t(NKT * NF).rearrange("p (k h) -> p k h", h=NF)
            CREF = f32t(NF * NQB).rearrange("p (h q) -> p h q", q=NQB)
            KTh = [bf16t(S) for _ in range(2)]
            Vh = [bf16t(S).rearrange("p (k d) -> p k d", d=128) for _ in range(2)]
            Qb = [bf16t(512) for _ in range(2)]
            Zb = [f32t(512) for _ in range(2)]
            Pt = [bf16t(512) for _ in range(3)]
            Bt = [f32t(NKT) for _ in range(2)]
            Rr = f32t(512)
            Ot = f32t(512)
            Hb = [bf16t(512) for _ in range(2)]
            P.op("sp", I("dma_start", out=FFt[0:NF, :], in_=MISCs[192:192 + NF, :]), writes=["FFt"], dma="c1")
            P.op("pool", I("memset", onesF[0:NF, :], 1.0), writes=["onesF"])
            P.op("act", I("activation", out=FFt[0:NF, :], in_=FFt[0:NF, :], func=AF.Exp, bias=nfb[0:NF, :], scale=-1.0),
                 reads=["FFt", "nfb"], writes=["FFt"])
            P.op("act", I("activation", out=FFt[0:NF, :], in_=FFt[0:NF, :], func=AF.Ln, bias=1.0, scale=1.0),
                 reads=["FFt"], writes=["FFt"])
            P.op("dve", I("tensor_tensor_scan", out=CL[0:NF, :], data0=onesF[0:NF, :], data1=FFt[0:NF, :], initial=0.0,
                          op0=ALU.mult, op1=ALU.add), reads=["FFt", "onesF"], writes=["CL"])
            CLv = CL.rearrange("p (q t) -> p q t", t=512)
            P.op("dve", I("tensor_copy", out=CLR[0:NF, :], in_=CLv[0:NF, :, 256]), reads=["CL"], writes=["CLR"])
            P.op("pool", I("memset", SEL[0:NF], 0.0), writes=["SEL"])
            P.op("pool", I("affine_select", out=SEL[0:NF], in_=SEL[0:NF], pattern=[[-1, NF], [0, 128]],
                           compare_op=ALU.not_equal, fill=1.0, base=0, channel_multiplier=1), reads=["SEL"], writes=["SEL"])
            per = 512 // NF
            for k0 in range(0, NKT, per):
                k1 = min(NKT, k0 + per)
                bk = (k0 // per) % 2
                for kt in range(k0, k1):
                    P.op("pe", I("matmul", banks[bk][:, (kt - k0) * NF:(kt - k0 + 1) * NF], lhsT=CL[0:NF, kt * 128:(kt + 1) * 128],
                                 rhs=ident_f[0:NF, 0:NF], start=True, stop=True),
                         reads=["CL", "ident_f"], writes=["ps%d" % bk])
                P.op("dve", I("tensor_copy", out=CT[:, k0:k1, :],
                              in_=banks[bk][:, 0:(k1 - k0) * NF].rearrange("p (k h) -> p k h", h=NF)),
                     reads=["ps%d" % bk], writes=["CT"])
            for h in range(NF):
                bk = 2 + h % 2
                P.op("pe", I("matmul", banks[bk][:, 0:NQB], lhsT=SEL[0:NF, h, :], rhs=CLR[0:NF, :], start=True, stop=True),
                     reads=["SEL", "CLR"], writes=["ps%d" % bk])
                P.op("dve", I("tensor_copy", out=CREF[:, h, :], in_=banks[bk][:, 0:NQB]), reads=["ps%d" % bk], writes=["CREF"])
            Vs_v = Vs.rearrange("(k p) c -> p k c", p=128)
            sidx = 0
            for h in range(NF):
                hs = h % 2
                P.op("sp", I("dma_start", out=KTh[hs], in_=KT[h * 128:(h + 1) * 128, :]), writes=["KTh%d" % hs], dma="kth%d" % hs)
                for (k0, k1) in ksplit(NKT):
                    P.op("sp", I("dma_start", out=Vh[hs][:, k0:k1, :], in_=Vs_v[:, k0:k1, h * 128:(h + 1) * 128]),
                         writes=["Vh%d" % hs], dma="vh%d" % hs)
                for qb in range(NQB):
                    qs = qb % 2
                    q0 = qb * 512
                    P.op("sp", I("dma_start", out=Qb[qs], in_=QT[h * 128:(h + 1) * 128, q0:q0 + 512]), writes=["Qb%d" % qs], dma="qb%d" % qs)
                    P.op("sp", I("dma_start", out=Zb[qs], in_=ZT[h * 128:(h + 1) * 128, q0:q0 + 512]), writes=["Zb%d" % qs], dma="zb%d" % qs)
                    nkt = 4 * qb + 4
                    P.op("dve", I("tensor_scalar", out=Bt[qs][:, 0:nkt], in0=CT[:, 0:nkt, h], scalar1=CREF[:, h, qb:qb + 1],
                                  scalar2=None, op0=ALU.subtract), reads=["CT", "CREF"], writes=["Bt%d" % qs])
                    pso = 2 + qs
                    psr = 4 + qs
                    base = sidx
                    sidx += nkt

                    def score(kt):
                        j = kt - 4 * qb
                        cs = 0 if j < 0 else j * 128
                        bs = (base + kt) % 2
                        ps_ = (base + kt) % 3
                        P.op("pe", I("matmul", banks[bs][:, cs:512], lhsT=KTh[hs][:, kt * 128:(kt + 1) * 128], rhs=Qb[qs][:, cs:512],
                                     start=True, stop=True), reads=["KTh%d" % hs, "Qb%d" % qs], writes=["ps%d" % bs])
                        P.op("act", I("activation", out=Pt[ps_][:, cs:512], in_=banks[bs][:, cs:512], func=AF.Exp,
                                      bias=Bt[qs][:, kt:kt + 1], scale=1.0),
                             reads=["ps%d" % bs, "Bt%d" % qs], writes=["Pt%d" % ps_])
                        if j >= 0:
                            P.op("pool", I("tensor_tensor", out=Pt[ps_][:, cs:cs + 128], in0=Pt[ps_][:, cs:cs + 128], in1=tri_bf,
                                           op=ALU.mult), reads=["Pt%d" % ps_, "tri_bf"], writes=["Pt%d" % ps_])

                    score(0)
                    for kt in range(nkt):
                        if kt + 1 < nkt:
                            score(kt + 1)
                        j = kt - 4 * qb
                        cs = 0 if j < 0 else j * 128
                        ps_ = (base + kt) % 3
                        P.op("pe", I("matmul", banks[pso][:, cs:512], lhsT=Vh[hs][:, kt, :], rhs=Pt[ps_][:, cs:512],
                                     start=(kt == 0), stop=(kt == nkt - 1)),
                             reads=["Vh%d" % hs, "Pt%d" % ps_], writes=["ps%d" % pso])
                        P.op("pe", I("matmul", banks[psr][:, cs:512], lhsT=ones_bf, rhs=Pt[ps_][:, cs:512],
                                     start=(kt == 0), stop=(kt == nkt - 1)),
                             reads=["ones_bf", "Pt%d" % ps_], writes=["ps%d" % psr])
                    P.op("dve", I("reciprocal", out=Rr, in_=banks[psr][:]), reads=["ps%d" % psr], writes=["Rr"])
                    P.op("dve", I("tensor_tensor", out=Ot, in0=banks[pso][:], in1=Rr, op=ALU.mult),
                         reads=["ps%d" % pso, "Rr"], writes=["Ot"])
                    P.op("pool", I("tensor_tensor", out=Hb[qs], in0=Ot, in1=Zb[qs], op=ALU.mult),
                         reads=["Ot", "Zb%d" % qs], writes=["Hb%d" % qs])
                    P.op("sp", I("dma_start", out=HT[h * 128:(h + 1) * 128, q0:q0 + 512], in_=Hb[qs]),
                         reads=["Hb%d" % qs], dma="hb%d" % qs)
            P.barrier()

        if "D" in cfg.phases:
            reset()
            smask = f32t(512)
            P.op("pool", I("memset", smask, 1.0), writes=["smask"])
            P.op("pool", I("memset", smask.rearrange("p (c t) -> p c t", t=64)[:, :, 0:1], 0.0), reads=["smask"], writes=["smask"])
            bdn = ("a", "r", "b", "k", "Kh", "Bh", "v")

            def v3(t, p0=0, p1=128):
                return t[p0:p1, :].rearrange("p (c t) -> p c t", t=64)

            class TS:
                pass

            def mk_tiles(s):
                T = TS()
                T.s = s
                T.raw = {n: f32t(513) for n in ("r", "k", "v", "wd", "ad")}
                T.X = {n: f32t(512) for n in ("r", "k", "v", "wd", "ad")}
                T.tmp = f32t(512)
                T.wup = f32t(128)
                T.aup = f32t(128)
                (T.sg, T.av, T.kk, T.kmod, T.t1, T.bvec, T.Ls, T.Ep, T.Em, T.Eex, T.Eend, T.bon, T.Zr, T.yT, T.t2,
                 T.t3) = [f32t(512) for _ in range(16)]
                T.WC = f32t(8)
                T.hTb = bf16t(512)
                T.B = {n: bf16t(8 * 128).rearrange("p (c t) -> p c t", t=128) for n in bdn}
                T.S0 = f32t(64)
                T.S0b = bf16t(64)
                T.Pk = [bf16t(512) for _ in range(2)]
                T.PkT = [bf16t(512) for _ in range(2)]
                T.AccT = [f32t(512) for _ in range(2)]
                T.AccTb = [bf16t(512) for _ in range(2)]
                T.LakT, T.MrbT, T.MrkT, T.Kh_t4, T.Bh_t4 = [bf16t(512) for _ in range(5)]
                T.V_t4 = bf16t(256)
                T.Xc, T.Uc = [bf16t(64) for _ in range(2)]
                T.Yc4 = f32t(256)
                T.ysq4 = f32t(256)
                T.stat = f32t(20)
                T.Ybd = f32t(8 * 128).rearrange("p (c t) -> p c t", t=128)
                for n in bdn:
                    P.op("pool", I("memset", T.B[n], 0.0), writes=["s%d:bd%s" % (s, n)])
                P.op("pool", I("memset", T.Ybd, 0.0), writes=["s%d:Ybd0" % s, "s%d:Ybd1" % s])
                return T

            shared_keys = {"smask", "ptab", "mtab", "omu", "omm", "omka", "eps_gn", "ident_bf", "ident_f", "ident4", "ones_bd",
                           "m_sl4", "m_su4", "m_ui4"}

            def pair_prog(pr, T, rec):
                s = T.s

                def K_(k):
                    if k in shared_keys:
                        return k
                    if k.startswith("ps"):
                        return "ps%d" % (4 * s + int(k[2:]))
                    return "s%d:%s" % (s, k)

                def OP(eng, fn, reads=(), writes=(), dma=None):
                    rec.append((eng, fn, [K_(k) for k in reads], [K_(k) for k in writes], (dma + str(s)) if dma else None))

                def bank(j):
                    return banks[4 * s + j]

                raw, X, tmp, B = T.raw, T.X, T.tmp, T.B
                sg, av, kk, kmod, t1, bvec, Ls, Ep, Em, Eex, Eend = T.sg, T.av, T.kk, T.kmod, T.t1, T.bvec, T.Ls, T.Ep, T.Em, T.Eex, T.Eend
                bon, Zr, yT, t2, t3, WC, hTb, S0, S0b = T.bon, T.Zr, T.yT, T.t2, T.t3, T.WC, T.hTb, T.S0, T.S0b
                Pk, PkT, AccT, AccTb = T.Pk, T.PkT, T.AccT, T.AccTb
                LakT, MrbT, MrkT, Kh_t4, Bh_t4, V_t4, Xc, Uc, Yc4, ysq4, stat, Ybd = (T.LakT, T.MrbT, T.MrkT, T.Kh_t4, T.Bh_t4, T.V_t4,
                                                                                      T.Xc, T.Uc, T.Yc4, T.ysq4, T.stat, T.Ybd)
                rn = t3
                f0 = pr * 128
                PC = [ptab[:, pr, j:j + 1] for j in range(10)]
                OP("sp", I("dma_start", out=T.wup[0:96, :], in_=w_up_d[:, f0:f0 + 128]), writes=["wup"], dma="wup")
                OP("sp", I("dma_start", out=T.aup[0:96, :], in_=a_up_d[:, f0:f0 + 128]), writes=["aup"], dma="wup")
                OP("pool", I("memset", S0, 0.0), writes=["S0"])
                OP("pool", I("memset", S0b, 0.0), writes=["S0b"])
                for tb in range(NB):
                    t0 = tb * 512
                    srcs = [("r", RWs, f0, 128), ("k", RWs, RW + f0, 128), ("v", RWs, 2 * RW + f0, 128),
                            ("wd", MISCs, 0, 96), ("ad", MISCs, 96, 96)]
                    for n, src, row, nr in srcs:
                        if tb == 0:
                            OP("pool", I("memset", raw[n][0:nr, 0:1], 0.0), writes=["raw" + n])
                            OP("sp", I("dma_start", out=raw[n][0:nr, 1:513], in_=src[row:row + nr, 0:512]), writes=["raw" + n], dma="raw" + n)
                        else:
                            OP("sp", I("dma_start", out=raw[n][0:nr, :], in_=src[row:row + nr, t0 - 1:t0 + 512]), writes=["raw" + n], dma="raw" + n)
                    OP("sp", I("dma_start", out=Zr, in_=ZT[FW + f0:FW + f0 + 128, t0:t0 + 512]), writes=["Zr"], dma="zr")
                    lerp = [("r", PC[0], omu[:, pr, 0:1], 128), ("k", PC[1], omu[:, pr, 1:2], 128), ("v", PC[2], omu[:, pr, 2:3], 128),
                            ("wd", mtab[:, 0:1], omm[:, 0:1], 96), ("ad", mtab[:, 1:2], omm[:, 1:2], 96)]
                    for n, mu_c, omu_c, nr in lerp:
                        OP("dve", I("tensor_scalar", out=tmp[0:nr, :], in0=raw[n][0:nr, 0:512], scalar1=mu_c[0:nr, :], scalar2=None,
                                    op0=ALU.mult), reads=["raw" + n, "ptab", "mtab"], writes=["tmp"])
                        OP("dve", I("scalar_tensor_tensor", out=X[n][0:nr, :], in0=raw[n][0:nr, 1:513], scalar=omu_c[0:nr, :],
                                    in1=tmp[0:nr, :], op0=ALU.mult, op1=ALU.add), reads=["raw" + n, "omu", "omm", "tmp"], writes=["X" + n])
                    OP("act", I("activation", out=X["wd"][0:96, :], in_=X["wd"][0:96, :], func=AF.Tanh), reads=["Xwd"], writes=["Xwd"])
                    OP("pe", I("matmul", bank(0)[:], lhsT=T.wup[0:96, :], rhs=X["wd"][0:96, :], start=True, stop=True),
                       reads=["wup", "Xwd"], writes=["ps0"])
                    OP("act", I("activation", out=sg, in_=bank(0)[:], func=AF.Sigmoid, bias=PC[3], scale=1.0), reads=["ps0", "ptab"], writes=["sg"])
                    OP("pe", I("matmul", bank(1)[:], lhsT=T.aup[0:96, :], rhs=X["ad"][0:96, :], start=True, stop=True),
                       reads=["aup", "Xad"], writes=["ps1"])
                    OP("act", I("activation", out=av, in_=bank(1)[:], func=AF.Sigmoid, bias=PC[4], scale=1.0), reads=["ps1", "ptab"], writes=["av"])
                    OP("pool", I("tensor_scalar", out=kk, in0=X["k"], scalar1=PC[5], scalar2=None, op0=ALU.mult), reads=["Xk", "ptab"], writes=["kk"])
                    OP("pool", I("tensor_tensor", out=t2, in0=kk, in1=kk, op=ALU.mult), reads=["kk"], writes=["t2"])
                    OP("pe", I("matmul", bank(2)[:], lhsT=ones_bd, rhs=t2, start=True, stop=True), reads=["ones_bd", "t2"], writes=["ps2"])
                    OP("dve", I("tensor_scalar", out=rn, in0=bank(2)[:], scalar1=1e-24, scalar2=None, op0=ALU.max), reads=["ps2"], writes=["t3"])
                    OP("act", I("activation", out=rn, in_=rn, func=AF.Sqrt), reads=["t3"], writes=["t3"])
                    OP("dve", I("reciprocal", out=rn, in_=rn), reads=["t3"], writes=["t3"])
                    OP("dve", I("tensor_tensor", out=kk, in0=kk, in1=rn, op=ALU.mult), reads=["kk", "t3"], writes=["kk"])
                    OP("dve", I("tensor_scalar", out=t1, in0=av, scalar1=PC[6], scalar2=omka[:, pr:pr + 1], op0=ALU.mult, op1=ALU.add),
                       reads=["av", "ptab", "omka"], writes=["t1"])
                    OP("dve", I("tensor_tensor", out=kmod, in0=X["k"], in1=t1, op=ALU.mult), reads=["Xk", "t1"], writes=["kmod"])
                    OP("pool", I("tensor_tensor", out=bvec, in0=kk, in1=av, op=ALU.mult), reads=["kk", "av"], writes=["bvec"])
                    OP("dve", I("scalar_tensor_tensor", out=t2, in0=X["r"], scalar=PC[7], in1=kmod, op0=ALU.mult, op1=ALU.mult),
                       reads=["Xr", "ptab", "kmod"], writes=["t2"])
                    OP("pe", I("matmul", bank(3)[:], lhsT=ones_bd, rhs=t2, start=True, stop=True), reads=["ones_bd", "t2"], writes=["ps3"])
                    OP("dve", I("tensor_tensor", out=bon, in0=bank(3)[:], in1=X["v"], op=ALU.mult), reads=["ps3", "Xv"], writes=["bon"])
                    OP("dve", I("tensor_tensor_scan", out=Ls, data0=smask, data1=sg, initial=0.0, op0=ALU.mult, op1=ALU.add),
                       reads=["smask", "sg"], writes=["Ls"])
                    OP("act", I("activation", out=Ep, in_=Ls, func=AF.Exp, scale=-DECAY_C), reads=["Ls"], writes=["Ep"])
                    OP("act", I("activation", out=Em, in_=Ls, func=AF.Exp, scale=DECAY_C), reads=["Ls"], writes=["Em"])
                    OP("pool", I("tensor_tensor", out=t3, in0=Ls, in1=sg, op=ALU.subtract), reads=["Ls", "sg"], writes=["t3"])
                    OP("act", I("activation", out=Eex, in_=t3, func=AF.Exp, scale=-DECAY_C), reads=["t3"], writes=["Eex"])
                    Lsv = v3(Ls)
                    OP("pool", I("tensor_tensor", out=v3(t3), in0=Lsv[:, :, 63:64].to_broadcast([128, 8, 64]), in1=Lsv, op=ALU.subtract),
                       reads=["Ls", "Eex"], writes=["t3"])
                    OP("act", I("activation", out=Eend, in_=t3, func=AF.Exp, scale=-DECAY_C), reads=["t3"], writes=["Eend"])
                    OP("act", I("activation", out=WC, in_=Lsv[:, :, 63], func=AF.Exp, scale=-DECAY_C), reads=["Ls"], writes=["WC"])

                    def bdw(eng, name, in0, in1, rk, neg=False):
                        for hh in range(2):
                            p0, p1 = hh * 64, hh * 64 + 64
                            o_ = B[name][p0:p1, :, p0:p1]
                            if in1 is None:
                                fn = I("tensor_copy", out=o_, in_=v3(in0, p0, p1))
                            elif neg:
                                fn = I("scalar_tensor_tensor", out=o_, in0=v3(in0, p0, p1), scalar=-1.0, in1=v3(in1, p0, p1),
                                       op0=ALU.mult, op1=ALU.mult)
                            else:
                                fn = I("tensor_tensor", out=o_, in0=v3(in0, p0, p1), in1=v3(in1, p0, p1), op=ALU.mult)
                            OP(eng, fn, reads=rk, writes=["bd" + name])
                    bdw("dve", "a", kk, Eex, ["kk", "Eex"], neg=True)
                    bdw("pool", "r", X["r"], Ep, ["Xr", "Ep"])
                    bdw("dve", "b", bvec, Em, ["bvec", "Em"])
                    bdw("pool", "k", kmod, Em, ["kmod", "Em"])
                    bdw("dve", "Kh", kmod, Eend, ["kmod", "Eend"])
                    bdw("pool", "Bh", bvec, Eend, ["bvec", "Eend"])
                    bdw("dve", "v", X["v"], None, ["Xv"])
                    ka, kr_, kb, kk_, kKh, kBh, kv = ["bd" + n for n in bdn]
                    for g in range(2):
                        p3b = bank(3)[:].bitcast(BF16)
                        p2b = bank(2)[:].bitcast(BF16)
                        for i in range(4):
                            c = 4 * g + i
                            OP("pe", I("transpose", out=p3b[:, i * 128:(i + 1) * 128], in_=B["Kh"][:, c, :], identity=ident_bf),
                               reads=[kKh, "ident_bf"], writes=["ps3"])
                            OP("pe", I("transpose", out=p3b[:, 512 + i * 128:512 + (i + 1) * 128], in_=B["Bh"][:, c, :], identity=ident_bf),
                               reads=[kBh, "ident_bf"], writes=["ps3"])
                            OP("pe", I("transpose", out=p2b[:, i * 128:(i + 1) * 128], in_=B["v"][:, c, :], identity=ident_bf),
                               reads=[kv, "ident_bf"], writes=["ps2"])
                        OP("act", I("copy", out=Kh_t4, in_=p3b[:, 0:512]), reads=["ps3"], writes=["Kh_t"])
                        OP("act", I("copy", out=Bh_t4, in_=p3b[:, 512:1024]), reads=["ps3"], writes=["Bh_t"])
                        for hh in range(2):
                            p0, p1 = hh * 64, hh * 64 + 64
                            OP("act", I("copy", out=V_t4[p0:p1, :].rearrange("p (c t) -> p c t", t=64),
                                        in_=p2b[p0:p1, 0:512].rearrange("p (c t) -> p c t", t=128)[:, :, p0:p1]),
                               reads=["ps2"], writes=["V_t"])
                        for i in range(4):
                            c = 4 * g + i
                            aT, bT = B["a"][:, c, :], B["b"][:, c, :]
                            cs = slice(i * 128, (i + 1) * 128)
                            OP("pe", I("matmul", bank(0)[:, cs], lhsT=aT, rhs=bT, start=True, stop=True), reads=[ka, kb], writes=["ps0"])
                            OP("pe", I("matmul", bank(1)[:, cs], lhsT=bT, rhs=aT, start=True, stop=True), reads=[ka, kb], writes=["ps1"])
                        OP("dve", I("tensor_tensor", out=Pk[0], in0=bank(0)[:], in1=m_sl4, op=ALU.mult), reads=["ps0", "m_sl4"], writes=["Pk0"])
                        OP("dve", I("tensor_tensor", out=PkT[0], in0=bank(1)[:], in1=m_su4, op=ALU.mult), reads=["ps1", "m_su4"], writes=["PkT0"])
                        for i in range(4):
                            c = 4 * g + i
                            aT, rT, bT, kT = B["a"][:, c, :], B["r"][:, c, :], B["b"][:, c, :], B["k"][:, c, :]
                            cs = slice(i * 128, (i + 1) * 128)
                            OP("pe", I("matmul", bank(2)[:, cs], lhsT=kT, rhs=aT, start=True, stop=True), reads=[ka, kk_], writes=["ps2"])
                            OP("pe", I("matmul", bank(3)[:, cs], lhsT=bT, rhs=rT, start=True, stop=True), reads=[kr_, kb], writes=["ps3"])
                            OP("pe", I("matmul", bank(0)[:, cs], lhsT=kT, rhs=rT, start=True, stop=True), reads=[kr_, kk_], writes=["ps0"])
                        OP("dve", I("tensor_tensor", out=LakT, in0=bank(2)[:], in1=m_su4, op=ALU.mult), reads=["ps2", "m_su4"], writes=["LakT"])
                        OP("dve", I("tensor_tensor", out=MrbT, in0=bank(3)[:], in1=m_ui4, op=ALU.mult), reads=["ps3", "m_ui4"], writes=["MrbT"])
                        OP("dve", I("tensor_tensor", out=MrkT, in0=bank(0)[:], in1=m_ui4, op=ALU.mult), reads=["ps0", "m_ui4"], writes=["MrkT"])
                        OP("pool", I("tensor_tensor", out=AccT[0], in0=PkT[0], in1=ident4, op=ALU.add), reads=["PkT0", "ident4"], writes=["AccT0"])
                        OP("pool", I("tensor_copy", out=AccTb[0], in_=AccT[0]), reads=["AccT0"], writes=["AccTb0"])
                        for l in range(5):
                            s_, d_ = l % 2, (l + 1) % 2
                            for i in range(4):
                                cs = slice(i * 128, (i + 1) * 128)
                                OP("pe", I("matmul", bank(0)[:, cs], lhsT=PkT[s_][:, cs], rhs=Pk[s_][:, cs], start=True, stop=True),
                                   reads=["Pk%d" % s_, "PkT%d" % s_], writes=["ps0"])
                                if l < 4:
                                    OP("pe", I("matmul", bank(1)[:, cs], lhsT=Pk[s_][:, cs], rhs=PkT[s_][:, cs], start=True, stop=True),
                                       reads=["Pk%d" % s_, "PkT%d" % s_], writes=["ps1"])
                            OP("act", I("copy", out=Pk[d_], in_=bank(0)[:]), reads=["ps0"], writes=["Pk%d" % d_])
                            if l < 4:
                                OP("act", I("copy", out=PkT[d_], in_=bank(1)[:]), reads=["ps1"], writes=["PkT%d" % d_])
                            for i in range(4):
                                cs = slice(i * 128, (i + 1) * 128)
                                OP("pe", I("matmul", bank(2)[:, cs], lhsT=Pk[d_][:, cs], rhs=AccTb[s_][:, cs], start=True, stop=True),
                                   reads=["Pk%d" % d_, "AccTb%d" % s_], writes=["ps2"])
                            OP("dve", I("tensor_tensor", out=AccT[d_], in0=bank(2)[:], in1=AccT[s_], op=ALU.add),
                               reads=["ps2", "AccT%d" % s_], writes=["AccT%d" % d_])
                            OP("pool", I("tensor_copy", out=AccTb[d_], in_=AccT[d_]), reads=["AccT%d" % d_], writes=["AccTb%d" % d_])
                        for i in range(4):
                            c = 4 * g + i
                            aT, rT = B["a"][:, c, :], B["r"][:, c, :]
                            cs = slice(i * 128, (i + 1) * 128)
                            vs = slice(i * 64, (i + 1) * 64)
                            xs = slice(i * 128, i * 128 + 64)
                            us = slice(i * 128 + 64, i * 128 + 128)
                            OP("pe", I("matmul", bank(0)[:, xs], lhsT=aT, rhs=S0b, start=True, stop=False), reads=[ka, "S0b"], writes=["ps0"])
                            OP("pe", I("matmul", bank(0)[:, xs], lhsT=LakT[:, cs], rhs=V_t4[:, vs], start=False, stop=True),
                               reads=["LakT", "V_t"], writes=["ps0"])
                            OP("act", I("copy", out=Xc, in_=bank(0)[:, xs]), reads=["ps0"], writes=["Xc"])
                            OP("pe", I("matmul", bank(0)[:, us], lhsT=AccTb[1][:, cs], rhs=Xc, start=True, stop=True),
                               reads=["AccTb1", "Xc"], writes=["ps0"])
                            OP("act", I("copy", out=Uc, in_=bank(0)[:, us]), reads=["ps0"], writes=["Uc"])
                            OP("pe", I("matmul", bank(1)[:, vs], lhsT=rT, rhs=S0b, start=True, stop=False), reads=[kr_, "S0b"], writes=["ps1"])
                            OP("pe", I("matmul", bank(1)[:, vs], lhsT=MrbT[:, cs], rhs=Uc, start=False, stop=False), reads=["MrbT", "Uc"], writes=["ps1"])
                            OP("pe", I("matmul", bank(1)[:, vs], lhsT=MrkT[:, cs], rhs=V_t4[:, vs], start=False, stop=True),
                               reads=["MrkT", "V_t"], writes=["ps1"])
                            OP("pe", I("matmul", bank(2)[:, 0:64], lhsT=Bh_t4[:, cs], rhs=Uc, start=True, stop=False), reads=["Bh_t", "Uc"], writes=["ps2"])
                            OP("pe", I("matmul", bank(2)[:, 0:64], lhsT=Kh_t4[:, cs], rhs=V_t4[:, vs], start=False, stop=True),
                               reads=["Kh_t", "V_t"], writes=["ps2"])
                            OP("dve", I("scalar_tensor_tensor", out=S0, in0=S0, scalar=WC[:, c:c + 1], in1=bank(2)[:, 0:64],
                                        op0=ALU.mult, op1=ALU.add), reads=["S0", "WC", "ps2"], writes=["S0"])
                            OP("pool", I("tensor_copy", out=S0b, in_=S0), reads=["S0"], writes=["S0b"])
                        Yc3 = Yc4.rearrange("p (c t) -> p c t", t=64)
                        OP("dve", I("tensor_copy", out=Yc4, in_=bank(1)[:, 0:256]), reads=["ps1"], writes=["Yc"])
                        OP("dve", I("tensor_reduce", out=stat[:, 0:4], in_=Yc3, axis=AX.X, op=ALU.add), reads=["Yc"], writes=["stat0"])
                        OP("dve", I("tensor_scalar", out=stat[:, 4:8], in0=stat[:, 0:4], scalar1=1.0 / 64.0, scalar2=None, op0=ALU.mult),
                           reads=["stat0"], writes=["stat1"])
                        OP("dve", I("tensor_tensor", out=Yc3, in0=Yc3, in1=stat[:, 4:8].rearrange("p (c o) -> p c o", o=1).to_broadcast([128, 4, 64]),
                                    op=ALU.subtract), reads=["Yc", "stat1"], writes=["Yc"])
                        OP("pool", I("tensor_tensor", out=ysq4, in0=Yc4, in1=Yc4, op=ALU.mult), reads=["Yc"], writes=["ysq"])
                        OP("dve", I("tensor_reduce", out=stat[:, 8:12], in_=ysq4.rearrange("p (c t) -> p c t", t=64), axis=AX.X, op=ALU.add),
                           reads=["ysq"], writes=["stat2"])
                        OP("act", I("activation", out=stat[:, 12:16], in_=stat[:, 8:12], func=AF.Sqrt, bias=eps_gn[:, 0:1], scale=1.0 / 64.0),
                           reads=["stat2", "eps_gn"], writes=["stat3"])
                        OP("dve", I("reciprocal", out=stat[:, 16:20], in_=stat[:, 12:16]), reads=["stat3"], writes=["stat4"])
                        for hh in range(2):
                            p0, p1 = hh * 64, hh * 64 + 64
                            OP("dve", I("tensor_tensor", out=Ybd[p0:p1, 4 * g:4 * g + 4, p0:p1], in0=Yc3[p0:p1],
                                        in1=stat[p0:p1, 16:20].rearrange("p (c o) -> p c o", o=1).to_broadcast([64, 4, 64]), op=ALU.mult),
                               reads=["Yc", "stat4"], writes=["Ybd%d" % g])
                    for c in range(8):
                        bj = 2 + (c % 2)
                        OP("pe", I("transpose", out=bank(bj)[:, 0:128], in_=Ybd[:, c, :], identity=ident_f),
                           reads=["Ybd%d" % (c // 4), "ident_f"], writes=["ps%d" % bj])
                        for hh in range(2):
                            p0, p1 = hh * 64, hh * 64 + 64
                            OP("act", I("copy", out=yT[p0:p1, c * 64:(c + 1) * 64], in_=bank(bj)[p0:p1, p0:p1]), reads=["ps%d" % bj], writes=["yT"])
                    OP("dve", I("tensor_scalar", out=yT, in0=yT, scalar1=PC[8], scalar2=PC[9], op0=ALU.mult, op1=ALU.add),
                       reads=["yT", "ptab"], writes=["yT"])
                    OP("dve", I("tensor_tensor", out=yT, in0=yT, in1=bon, op=ALU.add), reads=["yT", "bon"], writes=["yT"])
                    OP("pool", I("tensor_tensor", out=hTb, in0=yT, in1=Zr, op=ALU.mult), reads=["yT", "Zr"], writes=["hTb"])
                    OP("sp", I("dma_start", out=HT[FW + f0:FW + f0 + 128, t0:t0 + 512], in_=hTb), reads=["hTb"], dma="hTb")

            NSTR = 2 if NP >= 2 else 1
            tiles = [mk_tiles(s) for s in range(NSTR)]
            for p0_ in range(0, NP, NSTR):
                recs = []
                for s in range(NSTR):
                    if p0_ + s < NP:
                        r_ = []
                        pair_prog(p0_ + s, tiles[s], r_)
                        recs.append(r_)
                n = max(len(r_) for r_ in recs)
                for i in range(n):
                    for r_ in recs:
                        if i < len(r_):
                            eng, fn, rd, wr, dm = r_[i]
                            P.op(eng, fn, reads=rd, writes=wr, dma=dm)
            P.barrier()

        if "E" in cfg.phases:
            reset()
            TB = 256
            WKC = W // 128
            hT = bf16t(WKC * TB).rearrange("p (k t) -> p k t", t=TB)
            wo = [bf16t(WKC * 512).rearrange("p (k c) -> p k c", c=512) for _ in range(2)]
            acc = f32t(2 * D).rearrange("p (a d) -> p a d", a=2)
            xt = f32t(D)
            gain = f32t(D)
            beta = f32t(D)
            sq = f32t(D)
            st2 = f32t(8)
            P.op("sp", I("dma_start", out=gain, in_=ln_g_d.partition_broadcast(128)), writes=["gain"], dma="c2")
            P.op("sp", I("dma_start", out=beta, in_=ln_b_d.partition_broadcast(128)), writes=["beta"], dma="c2")
            HT_v = HT.rearrange("(k p) t -> p k t", p=128)
            wo_v = w_out_bf.rearrange("(k p) c -> p k c", p=128)
            NG = (D + 511) // 512
            wit = 0
            pit = 0
            for tb in range(S // TB):
                t0 = tb * TB
                for (k0, k1) in ksplit(WKC):
                    P.op("sp", I("dma_start", out=hT[:, k0:k1, :], in_=HT_v[:, k0:k1, t0:t0 + TB]), writes=["hT"], dma="hTld")
                for ng in range(NG):
                    n0 = ng * 512
                    nw = min(512, D - n0)
                    s = wit % 2
                    wit += 1
                    for (k0, k1) in ksplit(WKC):
                        P.op("sp", I("dma_start", out=wo[s][:, k0:k1, 0:nw], in_=wo_v[:, k0:k1, n0:n0 + nw]),
                             writes=["wo%d" % s], dma="wold%d" % s)
                    for tt in range(TB // 128):
                        bk = pit % 4
                        pit += 1
                        for kc in range(WKC):
                            P.op("pe", I("matmul", banks[bk][:, 0:nw], lhsT=hT[:, kc, tt * 128:(tt + 1) * 128], rhs=wo[s][:, kc, 0:nw],
                                         start=(kc == 0), stop=(kc == WKC - 1)),
                                 reads=["hT", "wo%d" % s], writes=["ps%d" % bk])
                        P.op("dve", I("tensor_copy", out=acc[:, tt, n0:n0 + nw], in_=banks[bk][:, 0:nw]),
                             reads=["ps%d" % bk], writes=["acc%d" % tt])
                for tt in range(TB // 128):
                    r0 = t0 + tt * 128
                    P.op("sp", I("dma_start", out=xt, in_=x[r0:r0 + 128, :]), writes=["xt"], dma="xtld")
                    P.op("dve", I("scalar_tensor_tensor", out=acc[:, tt, :], in0=xt, scalar=cfg.alpha, in1=acc[:, tt, :],
                                  op0=ALU.mult, op1=ALU.add), reads=["xt", "acc%d" % tt], writes=["acc%d" % tt])
                    P.op("act", I("activation", out=sq, in_=acc[:, tt, :], func=AF.Copy, accum_out=st2[:, 0:1]),
                         reads=["acc%d" % tt], writes=["sq", "st20"])
                    P.op("dve", I("tensor_scalar", out=st2[:, 1:2], in0=st2[:, 0:1], scalar1=-1.0 / D, scalar2=None, op0=ALU.mult),
                         reads=["st20"], writes=["st21"])
                    P.op("act", I("activation", out=sq, in_=acc[:, tt, :], func=AF.Square, bias=st2[:, 1:2], scale=1.0,
                                  accum_out=st2[:, 2:3]), reads=["acc%d" % tt, "st21"], writes=["sq", "st22"])
                    P.op("act", I("activation", out=st2[:, 3:4], in_=st2[:, 2:3], func=AF.Sqrt, bias=eps_ln[:, 0:1], scale=1.0 / D),
                         reads=["st22"], writes=["st23"])
                    P.op("dve", I("reciprocal", out=st2[:, 4:5], in_=st2[:, 3:4]), reads=["st23"], writes=["st24"])
                    P.op("dve", I("tensor_scalar", out=sq, in0=acc[:, tt, :], scalar1=st2[:, 1:2], scalar2=st2[:, 4:5],
                                  op0=ALU.add, op1=ALU.mult), reads=["acc%d" % tt, "st21", "st24"], writes=["sq"])
                    P.op("pool", I("tensor_tensor", out=sq, in0=sq, in1=gain, op=ALU.mult), reads=["sq", "gain"], writes=["sq"])
                    P.op("pool", I("tensor_tensor", out=sq, in0=sq, in1=beta, op=ALU.add), reads=["sq", "beta"], writes=["sq"])
                    P.op("sp", I("dma_start", out=y_out[r0:r0 + 128, :], in_=sq), reads=["sq"], dma="yst")
        P.emit()
    return nc


_CACHE = {}


def kernel(**inputs):
    cfg = Cfg()
    lay = host_layout(cfg, inputs)
    xfull = np.asarray(inputs["x"], np.float32)
    if "nc" not in _CACHE:
        _CACHE["nc"] = build(cfg)
    nc = _CACHE["nc"]
    in_maps = []
    for c in range(8):
        m = dict(lay)
        m["x"] = np.ascontiguousarray(xfull[c // 4])
        in_maps.append(m)
    res = run_bass_kernel_spmd(nc, in_maps, core_ids=list(range(8)))
    out = np.empty((2, cfg.S, cfg.D), np.float32)
    q = cfg.S // 4
    for c in range(8):
        b, g = c // 4, c % 4
        out[b, g * q:(g + 1) * q] = res.results[c]["y"][g * q:(g + 1) * q]
    return out
```
